# Optimizing a Trainium2 kernel written in Bass

```python
import jax, jax.numpy as jnp
from jax import lax
import numpy as np

D_MODEL = 1024
BATCH = 8
SEQ = 4096
DEPTH = 1

MEM_TOKENS = 256
EPS = 1e-6
GLA_HEADS = 4
GLA_DK = 64
GLA_DV = 128
GLA_GATE_RANK = 16
GLA_GATE_NORM = 16.0
GLA_CHUNK = 64
MLA_HEADS = 8
MLA_Q_RANK = 256
MLA_KV_RANK = 128
MLA_NOPE = 64
MLA_ROPE = 32
MLA_QK = MLA_NOPE + MLA_ROPE
MLA_V = 64
ROPE_THETA = 10000.0
Q_BLOCK = 128
XA_HEADS = 4
XA_HEAD_DIM = 128
D_FF = 2816
CONV_W = 3

GLA_QK_W = GLA_HEADS * GLA_DK
GLA_V_W = GLA_HEADS * GLA_DV
MLA_OUT_W = MLA_HEADS * MLA_V
MIX_WIDTH = GLA_V_W + MLA_OUT_W
IN_SPLITS = (GLA_QK_W, GLA_QK_W, GLA_V_W, GLA_GATE_RANK, GLA_V_W, MLA_Q_RANK, MLA_KV_RANK, MLA_ROPE)
IN_WIDTH = int(sum(IN_SPLITS))
IN_SPLIT_POINTS = tuple(int(p) for p in np.cumsum(IN_SPLITS)[:-1])

kernel_name = "hymba_gla_mla_memxattn_convffn"


def rms_norm(x, g):
    xf = x.astype(jnp.float32)
    y = xf * lax.rsqrt(jnp.mean(xf * xf, axis=-1, keepdims=True) + EPS)
    return (y * g.astype(jnp.float32)).astype(x.dtype)


def to_heads(t, n_heads):
    b, s, _ = t.shape
    return t.reshape(b, s, n_heads, -1).transpose(0, 2, 1, 3)


def from_heads(t):
    b, h, s, d = t.shape
    return t.transpose(0, 2, 1, 3).reshape(b, s, h * d)


def rope(x, pos):
    half = x.shape[-1] // 2
    inv = ROPE_THETA ** (-jnp.arange(half, dtype=jnp.float32) / half)
    ang = pos.astype(jnp.float32)[:, None, :, None] * inv
    cos, sin = jnp.cos(ang), jnp.sin(ang)
    xf = x.astype(jnp.float32)
    x1, x2 = xf[..., :half], xf[..., half:]
    return jnp.concatenate([x1 * cos - x2 * sin, x2 * cos + x1 * sin], axis=-1).astype(x.dtype)


def gla_chunked(q, k, v, log_a):
    b, h, s, dk = q.shape
    dv = v.shape[-1]
    c = GLA_CHUNK
    n = s // c
    rs = lambda t: t.reshape(b, h, n, c, t.shape[-1]).astype(jnp.float32)
    qf, kf, vf, la = rs(q) * (GLA_DK ** -0.5), rs(k), rs(v), rs(log_a)
    cum = jnp.cumsum(la, axis=3)
    cum_last = cum[:, :, :, -1:]
    q_dec = qf * jnp.exp(cum)
    k_inv = kf * jnp.exp(-cum)
    k_end = kf * jnp.exp(cum_last - cum)
    att = jnp.einsum('bhnik,bhnjk->bhnij', q_dec, k_inv)
    causal = jnp.tril(jnp.ones((c, c), dtype=bool))
    att = jnp.where(causal, att, 0.0)
    o_intra = jnp.einsum('bhnij,bhnjv->bhniv', att, vf)
    d_state = jnp.einsum('bhnjk,bhnjv->nbhkv', k_end, vf)
    decay = jnp.exp(cum_last[:, :, :, 0]).transpose(2, 0, 1, 3)

    def step(state, inp):
        d, ds = inp
        return d[..., None] * state + ds, state

    _, s_prev = lax.scan(step, jnp.zeros((b, h, dk, dv), jnp.float32), (decay, d_state))
    o_inter = jnp.einsum('bhnik,nbhkv->bhniv', q_dec, s_prev)
    return (o_intra + o_inter).reshape(b, h, s, dv)


def causal_block_attention(q, k, v):
    b, h, s, d = q.shape
    dv = v.shape[-1]
    nb = s // Q_BLOCK
    scale = d ** -0.5
    qb = q.reshape(b, h, nb, Q_BLOCK, d).transpose(2, 0, 1, 3, 4)
    starts = jnp.arange(nb, dtype=jnp.int32) * Q_BLOCK
    key_idx = jnp.arange(s, dtype=jnp.int32)
    kf = k.astype(jnp.float32)
    vf = v.astype(jnp.float32)

    def one_block(args):
        qi, s0 = args
        sc = jnp.einsum('bhqd,bhkd->bhqk', qi.astype(jnp.float32), kf) * scale
        q_idx = s0 + jnp.arange(Q_BLOCK, dtype=jnp.int32)
        sc = jnp.where(key_idx[None, :] <= q_idx[:, None], sc, -jnp.inf)
        p = jax.nn.softmax(sc, axis=-1)
        return jnp.einsum('bhqk,bhkd->bhqd', p, vf)

    out = lax.map(one_block, (qb, starts))
    return out.transpose(1, 2, 0, 3, 4).reshape(b, h, s, dv).astype(v.dtype)


def setup_inputs(seed: int = 0) -> dict:
    key = jax.random.key(seed)
    ks = jax.random.split(key, 32)
    L, D, F = DEPTH, D_MODEL, D_FF

    def w(k, shape, fan_in):
        return jax.random.normal(k, shape, jnp.float32) * (fan_in ** -0.5)

    def gain(k, shape):
        return 1.0 + 0.02 * jax.random.normal(k, shape, jnp.float32)

    x = jax.random.normal(ks[0], (BATCH, SEQ, D), jnp.float32)
    mem = jax.random.normal(ks[1], (BATCH, MEM_TOKENS, D), jnp.float32)
    positions = (jnp.arange(SEQ, dtype=jnp.int32)[None, :]
                 + jax.random.randint(ks[2], (BATCH, 1), 0, 1024, dtype=jnp.int32))
    return {
        "x": x,
        "mem": mem,
        "positions": positions,
        "norm_mix": gain(ks[3], (L, D)),
        "w_in": w(ks[4], (L, D, IN_WIDTH), D),
        "gla_gate_w2": w(ks[5], (L, GLA_GATE_RANK, GLA_QK_W), GLA_GATE_RANK),
        "gla_gate_b": 0.1 * jax.random.normal(ks[6], (L, GLA_QK_W), jnp.float32),
        "gla_out_norm": gain(ks[7], (L, GLA_DV)),
        "mla_q_a_norm": gain(ks[8], (L, MLA_Q_RANK)),
        "mla_w_uq": w(ks[9], (L, MLA_Q_RANK, MLA_HEADS * MLA_QK), MLA_Q_RANK),
        "mla_kv_a_norm": gain(ks[10], (L, MLA_KV_RANK)),
        "mla_w_ukv": w(ks[11], (L, MLA_KV_RANK, MLA_HEADS * (MLA_NOPE + MLA_V)), MLA_KV_RANK),
        "mla_q_norm": gain(ks[12], (L, MLA_QK)),
        "mla_k_norm": gain(ks[13], (L, MLA_QK)),
        "w_out": w(ks[14], (L, MIX_WIDTH, D), MIX_WIDTH),
        "norm_xa": gain(ks[15], (L, D)),
        "norm_mem": gain(ks[16], (L, D)),
        "xa_w_q": w(ks[17], (L, D, XA_HEADS * XA_HEAD_DIM), D),
        "xa_w_kv": w(ks[18], (L, D, 2 * XA_HEADS * XA_HEAD_DIM), D),
        "xa_q_norm": gain(ks[19], (L, XA_HEAD_DIM)),
        "xa_k_norm": gain(ks[20], (L, XA_HEAD_DIM)),
        "xa_w_o": w(ks[21], (L, XA_HEADS * XA_HEAD_DIM, D), XA_HEADS * XA_HEAD_DIM),
        "norm_ffn": gain(ks[22], (L, D)),
        "ffn_w_gate": w(ks[23], (L, D, F), D),
        "ffn_w_up": w(ks[24], (L, D, F), D),
        "ffn_conv_w": w(ks[25], (L, CONV_W, F), CONV_W),
        "ffn_conv_b": 0.02 * jax.random.normal(ks[26], (L, F), jnp.float32),
        "ffn_w_down": w(ks[27], (L, F, D), F),
    }


def reference(x, mem, positions, norm_mix, w_in, gla_gate_w2, gla_gate_b, gla_out_norm,
              mla_q_a_norm, mla_w_uq, mla_kv_a_norm, mla_w_ukv, mla_q_norm, mla_k_norm, w_out,
              norm_xa, norm_mem, xa_w_q, xa_w_kv, xa_q_norm, xa_k_norm, xa_w_o,
              norm_ffn, ffn_w_gate, ffn_w_up, ffn_conv_w, ffn_conv_b, ffn_w_down):
    b, s, _ = x.shape
    h = x
    for l in range(DEPTH):
        xn = rms_norm(h, norm_mix[l])
        proj = xn @ w_in[l]
        g_q, g_k, g_v, g_alr, g_og, c_q, c_kv, k_pe = jnp.split(proj, IN_SPLIT_POINTS, axis=-1)

        gate_logit = (g_alr @ gla_gate_w2[l] + gla_gate_b[l]).astype(jnp.float32)
        log_a = jax.nn.log_sigmoid(gate_logit) / GLA_GATE_NORM
        o_gla = gla_chunked(to_heads(g_q, GLA_HEADS), to_heads(g_k, GLA_HEADS),
                            to_heads(g_v, GLA_HEADS), to_heads(log_a, GLA_HEADS))
        o_gla = rms_norm(o_gla, gla_out_norm[l]).astype(h.dtype)
        o_gla = from_heads(o_gla) * jax.nn.silu(g_og)

        q = to_heads(rms_norm(c_q, mla_q_a_norm[l]) @ mla_w_uq[l], MLA_HEADS)
        kv = to_heads(rms_norm(c_kv, mla_kv_a_norm[l]) @ mla_w_ukv[l], MLA_HEADS)
        k_nope, v = kv[..., :MLA_NOPE], kv[..., MLA_NOPE:]
        k_rot = jnp.broadcast_to(k_pe[:, None], (b, MLA_HEADS, s, MLA_ROPE))
        k = jnp.concatenate([k_nope, k_rot], axis=-1)
        q = rms_norm(q, mla_q_norm[l])
        k = rms_norm(k, mla_k_norm[l])
        q = jnp.concatenate([q[..., :MLA_NOPE], rope(q[..., MLA_NOPE:], positions)], axis=-1)
        k = jnp.concatenate([k[..., :MLA_NOPE], rope(k[..., MLA_NOPE:], positions)], axis=-1)
        o_mla = from_heads(causal_block_attention(q, k, v))

        h = h + jnp.concatenate([o_gla, o_mla], axis=-1) @ w_out[l]

        hn = rms_norm(h, norm_xa[l])
        mn = rms_norm(mem, norm_mem[l])
        xq = rms_norm((hn @ xa_w_q[l]).reshape(b, s, XA_HEADS, XA_HEAD_DIM), xa_q_norm[l])
        xkv = (mn @ xa_w_kv[l]).reshape(b, mem.shape[1], 2, XA_HEADS, XA_HEAD_DIM)
        xk = rms_norm(xkv[:, :, 0], xa_k_norm[l])
        xv = xkv[:, :, 1]
        sc = jnp.einsum('bqhd,bkhd->bhqk', xq.astype(jnp.float32), xk.astype(jnp.float32)) * (XA_HEAD_DIM ** -0.5)
        p = jax.nn.softmax(sc, axis=-1)
        xo = jnp.einsum('bhqk,bkhd->bqhd', p, xv.astype(jnp.float32)).astype(h.dtype)
        h = h + xo.reshape(b, s, XA_HEADS * XA_HEAD_DIM) @ xa_w_o[l]

        fn = rms_norm(h, norm_ffn[l])
        g = fn @ ffn_w_gate[l]
        g_pad = jnp.pad(g, ((0, 0), (CONV_W - 1, 0), (0, 0)))
        cw = ffn_conv_w[l]
        g_conv = ffn_conv_b[l] + sum(cw[i] * g_pad[:, i:i + s] for i in range(CONV_W))
        h = h + (jax.nn.silu(g_conv) * (fn @ ffn_w_up[l])) @ ffn_w_down[l]
    return h
```

```python
import numpy as np
import concourse.bass as bass
import concourse.mybir as mybir

F32 = mybir.dt.float32
BF = mybir.dt.bfloat16
I32 = mybir.dt.int32
ALU = mybir.AluOpType
AF = mybir.ActivationFunctionType
AX = mybir.AxisListType


def _prod(xs):
    r = 1
    for v in xs:
        r *= int(v)
    return r


class Sched:
    def __init__(self, nc):
        self.nc = nc
        self.eng = dict(pe=nc.tensor, act=nc.scalar, dve=nc.vector, pool=nc.gpsimd, sp=nc.sync)
        self.sem = {}
        self.cnt = {}
        for e in self.eng:
            self.sem[e] = nc.alloc_semaphore("cs_" + e)
            self.cnt[e] = 0
        self.semh = {("cs_" + e): self.sem[e] for e in self.eng}
        self.seen = {e: {} for e in self.eng}
        self.hist = {}
        self.dma_cum = {}
        self.n_wait = 0
        self.n_ops = 0

    def new_sem(self, name):
        h = self.nc.alloc_semaphore(name)
        self.semh[name] = h
        self.dma_cum[name] = 0
        return name

    def _acc(self, ap):
        t = ap.tensor
        name = t.name
        space = str(ap.space)
        pat = ap.ap
        off = int(ap.offset)
        if "PSUM" in space:
            return (name, 0, 1 << 30, 0, 128, True)
        if "SB" in space:
            psz = _prod(t.shape[1:])
            p0 = off // psz
            f0 = off % psz
            npart = pat[0][1]
            span = sum((c - 1) * abs(s) for s, c in pat[1:]) + 1
            return (name, f0, f0 + span, p0, p0 + npart, False)
        span = sum((c - 1) * abs(s) for s, c in pat) + 1
        return (name, off, off + span, 0, 1, False)

    def _deps(self, e, is_dma, accs, skip_tok=None):
        need = {}
        for (key, lo, hi, plo, phi, excl), w in accs:
            lst = self.hist.get(key)
            if not lst:
                continue
            for h in lst:
                hlo, hhi, hplo, hphi, he, hdma, tok, hw = h
                if hhi <= lo or hi <= hlo or hphi <= plo or phi <= hplo:
                    continue
                if skip_tok is not None and tok is skip_tok:
                    continue
                if he == e and not is_dma and not hdma:
                    if e == "pe":
                        continue
                    if not (hw or w):
                        continue
                else:
                    if not (hw or w or excl):
                        continue
                sn, val = tok[0], tok[1]
                assert val is not None, "unsealed dma token used"
                if need.get(sn, 0) < val:
                    need[sn] = val
        return need

    def _record(self, e, is_dma, accs, tok):
        for (key, lo, hi, plo, phi, excl), w in accs:
            lst = self.hist.setdefault(key, [])
            keep = []
            for h in lst:
                hlo, hhi, hplo, hphi, he, hdma, htok, hw = h
                contained = lo <= hlo and hhi <= hi and plo <= hplo and hphi <= phi
                if contained:
                    if w:
                        continue
                    if excl and (he != e or hdma or is_dma):
                        continue
                    if (not hw) and he == e and not is_dma and not hdma:
                        continue
                keep.append(h)
            keep.append((lo, hi, plo, phi, e, is_dma, tok, w))
            self.hist[key] = keep

    def _emit_waits(self, e, need):
        eng = self.eng[e]
        seen = self.seen[e]
        for sn, val in need.items():
            if seen.get(sn, 0) < val:
                eng.wait_ge(self.semh[sn], val)
                seen[sn] = val
                self.n_wait += 1

    def op(self, e, fn, kwargs, reads, writes, sig=True):
        accs = [(self._acc(a), False) for a in reads] + [(self._acc(a), True) for a in writes]
        need = self._deps(e, False, accs)
        self._emit_waits(e, need)
        ins = fn(**kwargs)
        if sig:
            self.cnt[e] += 1
            ins.then_inc(self.sem[e], 1)
            tok = ("cs_" + e, self.cnt[e])
        else:
            tok = ("cs_" + e, self.cnt[e] + 1)
        self._record(e, False, accs, tok)
        self.n_ops += 1
        return ins

    def dma(self, e, out, in_, sem, tok=None, **kw):
        accs = [(self._acc(in_), False), (self._acc(out), True)]
        need = self._deps(e, True, accs, skip_tok=tok)
        self._emit_waits(e, need)
        ins = self.eng[e].dma_start(out=out, in_=in_, **kw)
        ins.then_inc(self.semh[sem], 16)
        self.dma_cum[sem] += 16
        if tok is None:
            tok = [sem, self.dma_cum[sem]]
        self._record(e, True, accs, tok)
        self.n_ops += 1
        return tok

    def group_tok(self, sem):
        return [sem, None]

    def seal(self, tok):
        tok[1] = self.dma_cum[tok[0]]

    def wait_all(self, e, toks):
        need = {}
        for sn, val in toks:
            if need.get(sn, 0) < val:
                need[sn] = val
        self._emit_waits(e, need)


def view(t, p0, npart, f0, dims):
    psz = _prod(t.shape[1:])
    return bass.AP(t, p0 * psz + f0, [[psz, npart]] + [[int(s), int(c)] for s, c in dims])


def dview(t, off, dims):
    return bass.AP(t, int(off), [[int(s), int(c)] for s, c in dims])


import math
from concourse.bass_utils import run_bass_kernel_spmd

D = 1024; SEQ = 4096; NBLK = 32; T = 256; NB = T // 128; NTILE = SEQ // T
FF = 2816; NFC = 22; EPS = 1e-6
GSZ = 4096
R_SLOTS = 3
VO = {}
def _vo():
    o = 0
    for n, w in [("nm", 8), ("nxa", 8), ("nffn", 8), ("nmem", 8), ("qan", 2), ("kvan", 1), ("gb", 256), ("gon", 128),
                 ("gq", 96), ("gk", 96), ("xqn", 128), ("xkn", 128), ("cw", 66), ("cb", 22)]:
        VO[n] = o; o += w
    return o
NV = _vo()


def build_program(dbg=False):
    nc = bass.Bass("TRN2", target_bir_lowering=False)
    S = Sched(nc)
    dt_in = lambda n, shp, dt=F32: nc.dram_tensor(n, shp, dt, kind="ExternalInput")
    x_d = dt_in("x", [SEQ, D]); mem_d = dt_in("mem", [256, D]); pos_d = dt_in("pos", [128, 32], I32)
    cst_d = dt_in("cst", [128, 272]); vec_d = dt_in("vec", [128, NV])
    w_in_d = dt_in("w_in", [D, 1968]); w2_d = dt_in("w2", [16, 256]); w_uq_d = dt_in("w_uq", [256, 768]); w_ukv_d = dt_in("w_ukv", [128, 1024])
    w_out_d = dt_in("w_out", [D, D]); xwq_d = dt_in("xa_w_q", [D, 512]); xwkv_d = dt_in("xa_w_kv", [D, 1024]); xwo_d = dt_in("xa_w_o", [512, D])
    wg_d = dt_in("w_gate", [D, FF]); wu_d = dt_in("w_up", [D, FF]); wd_d = dt_in("w_down", [FF, D])
    out_d = nc.dram_tensor("out", [SEQ, D], F32, kind="ExternalOutput")
    NG = 26
    wsc = nc.dram_tensor("wsc", [NG * 128, GSZ], BF, kind="Internal")

    def sb(name, n, dt):
        return nc.alloc_sbuf_tensor("s_" + name, [128, n], dt)
    KT = sb("KT", 8 * SEQ, BF); VA = sb("VA", NBLK * 8 * 66 + 64, BF)
    cst = sb("cst", 272, F32); identb = sb("identb", 128, BF); Ub = sb("Ub", 128, BF); onesb = sb("onesb", 128, BF); small = sb("small", 8, F32)
    vec = sb("vec", NV, F32); COS = sb("COS", 512, F32); SIN = sb("SIN", 512, F32); w2b = sb("w2b", 256, BF)
    KmT = sb("KmT", 4 * 256, BF); Vm = sb("Vm", 2 * 512, BF); Sst = sb("Sst", 512, F32); Sb = sb("Sb", 512, BF); halo = sb("halo", 44, F32)
    xh = sb("xh", 2 * NB * D, F32); xs = sb("xs", D, BF); nT = sb("nT", 8 * T, BF)
    qk = sb("qk", NB * 512, F32); vtok = sb("vtok", NB * 512, BF); Gt = sb("Gt", NB * 512, BF); r3 = sb("r3", NB * 432, F32)
    WF = sb("WF", 3520, F32)
    WB = sb("WB", 3072, BF)
    U = sb("U", NFC * T, BF)
    PT = sb("PT", 3 * 512, BF)
    stt = sb("stt", 64, F32)
    ring = sb("ring", R_SLOTS * GSZ, BF)
    posi = sb("posi", 32, I32)
    psA = [nc.alloc_psum_tensor("psA%d" % i, [128, 512], F32) for i in range(4)]
    psB = [nc.alloc_psum_tensor("psB%d" % i, [128, 512], F32) for i in range(2)]
    psT = [nc.alloc_psum_tensor("psT%d" % i, [128, 1024], BF) for i in range(2)]
    rot = {"A": 0, "B": 0, "T": 0}
    def bankA():
        rot["A"] += 1; return psA[rot["A"] % 3]
    def bankB():
        rot["B"] += 1; return psB[rot["B"] % 2]
    def bankT():
        rot["T"] += 1; return psT[rot["T"] % 2]

    V = lambda t, f0, dims, p0=0, np_=128: view(t, p0, np_, f0, dims)
    isap = lambda a: isinstance(a, bass.AP)
    E = S.eng

    def ACT(out, in_, func, **kw):
        reads = [in_] + [v for k, v in kw.items() if isap(v) and k != "accum_out"]
        writes = [out] + ([kw["accum_out"]] if "accum_out" in kw else [])
        S.op("act", nc.scalar.activation, dict(out=out, in_=in_, func=func, **kw), reads, writes)
    def TT(e, out, in0, in1, op):
        S.op(e, E[e].tensor_tensor, dict(out=out, in0=in0, in1=in1, op=op), [in0, in1], [out])
    def TS(e, out, in0, s1, s2, op0, op1=None):
        kw = dict(out=out, in0=in0, scalar1=s1, scalar2=s2, op0=op0)
        if op1 is not None: kw["op1"] = op1
        S.op(e, E[e].tensor_scalar, kw, [in0] + [a for a in (s1, s2) if isap(a)], [out])
    def STT(e, out, in0, sc, in1, op0, op1):
        S.op(e, E[e].scalar_tensor_tensor, dict(out=out, in0=in0, scalar=sc, in1=in1, op0=op0, op1=op1), [in0, in1] + ([sc] if isap(sc) else []), [out])
    def CP(e, out, in_):
        if e == "act":
            S.op(e, nc.scalar.copy, dict(out=out, in_=in_), [in_], [out])
        else:
            S.op(e, E[e].tensor_copy, dict(out=out, in_=in_), [in_], [out])
    def RED(out, in_):
        S.op("dve", nc.vector.tensor_reduce, dict(out=out, in_=in_, axis=AX.X, op=ALU.add), [in_], [out])
    def RECIP(out, in_):
        S.op("dve", nc.vector.reciprocal, dict(out=out, in_=in_), [in_], [out])
    def MEMSET(e, ap, c):
        S.op(e, E[e].memset, dict(ap=ap, constant=c), [], [ap])
    def MM(out, lhsT, rhs, start=True, stop=True, sig=None, **kw):
        S.op("pe", nc.tensor.matmul, dict(out=out, lhsT=lhsT, rhs=rhs, start=start, stop=stop, **kw), [lhsT, rhs], [out], sig=(stop if sig is None else sig))
    def TR(out, in_, sig=True):
        idn = V(identb, 0, [(1, 128)])
        S.op("pe", nc.tensor.transpose, dict(out=out, in_=in_, identity=idn), [in_, idn], [out], sig=sig)

    ident_f = V(cst, 0, [(1, 128)]); U_f = V(cst, 128, [(1, 128)])
    epsc = V(small, 0, [(1, 1)]); halfpi = V(small, 1, [(1, 1)]); onesf = V(small, 2, [(1, 1)])
    vcol = lambda n, i=0, w=1: V(vec, VO[n] + i, [(1, w)])

    def rstd_from(ss, out, n, k=1):
        ACT(out, ss, AF.Ln, scale=1.0 / n, bias=epsc)
        ACT(out, out, AF.Exp, scale=-0.5)

    semc = [S.new_sem("pro%d" % i) for i in range(6)]
    S.dma("sp", V(cst, 0, [(1, 272)]), dview(cst_d, 0, [(272, 128), (1, 272)]), semc[0])
    S.dma("sp", V(vec, 0, [(1, NV)]), dview(vec_d, 0, [(NV, 128), (1, NV)]), semc[1])
    S.dma("sp", V(posi, 0, [(1, 32)]), dview(pos_d, 0, [(32, 128), (1, 32)]), semc[2])
    S.dma("pool", V(w2b, 0, [(1, 256)], 0, 16), dview(w2_d, 0, [(256, 16), (1, 256)]), semc[3])
    gsem = [S.new_sem("gs%d" % g) for g in range(NG)]
    G = 128 * GSZ
    def cast(g, dst_off, dst_dims, src_t, src_off, src_dims, tok):
        S.dma("pool", dview(wsc, g * G + dst_off, [(GSZ, 128)] + dst_dims), dview(src_t, src_off, src_dims), gsem[g], tok=tok)
    def cast_group(g, items):
        tok = S.group_tok(gsem[g])
        for it in items:
            cast(g, *it, tok)
        S.seal(tok)
    WIN_ORDER = [3, 2, 0, 1]
    CAST = {}
    CAST[2] = [(0, [(512, 8), (1, 512)], w_in_d, 1040, [(1968, 128), (128 * 1968, 8), (1, 512)])]
    CAST[0] = [(0, [(512, 8), (1, 512)], w_in_d, 0, [(1968, 128), (128 * 1968, 8), (1, 512)])]
    CAST[1] = [(0, [(512, 8), (1, 512)], w_in_d, 512, [(1968, 128), (128 * 1968, 8), (1, 512)])]
    CAST[3] = [(0, [(512, 8), (1, 16)], w_in_d, 1024, [(1968, 128), (128 * 1968, 8), (1, 16)]),
               (16, [(512, 8), (1, 416)], w_in_d, 1552, [(1968, 128), (128 * 1968, 8), (1, 416)])]
    CAST[4] = [(0, [(768, 2), (1, 768)], w_uq_d, 0, [(768, 128), (128 * 768, 2), (1, 768)]),
               (1536, [(1, 1024)], w_ukv_d, 0, [(1024, 128), (1, 1024)])]
    for gi in range(2):
        CAST[5 + gi] = [(0, [(1024, 4), (1, 1024)], w_out_d, gi * 4 * 128 * 1024, [(1024, 128), (128 * 1024, 4), (1, 1024)])]
    CAST[7] = [(0, [(512, 8), (1, 512)], xwq_d, 0, [(512, 128), (128 * 512, 8), (1, 512)])]
    CAST[8] = [(0, [(1024, 4), (1, 1024)], xwo_d, 0, [(1024, 128), (128 * 1024, 4), (1, 1024)])]
    for k in range(11):
        items = []
        for sub in range(2):
            for m, wt in enumerate((wg_d, wu_d)):
                items.append((sub * 2048 + m * 1024, [(128, 8), (1, 128)], wt, (2 * k + sub) * 128, [(FF, 128), (128 * FF, 8), (1, 128)]))
        CAST[9 + k] = items
    for k in range(6):
        nj = 4 if k < 5 else 2
        CAST[20 + k] = [(0, [(1024, nj), (1, 1024)], wd_d, 4 * k * 128 * 1024, [(1024, 128), (128 * 1024, nj), (1, 1024)])]
    def emit_casts(gs):
        for g in gs:
            cast_group(g, CAST[g])
    emit_casts([3, 4, 2, 0, 1, 5, 6, 7, 8] + list(range(9, 26)))

    MEMSET("dve", epsc, EPS); MEMSET("dve", halfpi, math.pi / 2); MEMSET("dve", onesf, 1.0)
    MEMSET("dve", V(onesb, 0, [(1, 128)]), 1.0)
    CP("dve", V(identb, 0, [(1, 128)]), ident_f); CP("dve", V(Ub, 0, [(1, 128)]), U_f)
    MEMSET("pool", V(VA, 0, [(1, NBLK * 8 * 66 + 64)]), 1.0)
    TT("dve", vcol("gk", 0, 64), vcol("gk", 0, 64), vcol("gq", 0, 64), ALU.mult)
    MEMSET("pool", V(Sst, 0, [(1, 512)]), 0.0); MEMSET("pool", V(Sb, 0, [(1, 512)]), 0.0); MEMSET("pool", V(halo, 0, [(1, 44)]), 0.0)
    posf = V(WF, 0, [(1, 32)]); ang = V(WF, 32, [(1, 512)]); uu = V(WF, 544, [(1, 512)]); kf_ = V(WF, 1056, [(1, 512)]); s4 = V(WF, 1568, [(1, 512)]); c4 = V(WF, 2080, [(1, 480)])
    c4 = V(COS, 0, [(1, 512)])
    s4 = V(SIN, 0, [(1, 512)])
    CP("dve", posf, V(posi, 0, [(1, 32)]))
    TT("dve", V(WF, 32, [(16, 32), (1, 16)]), V(WF, 0, [(1, 32), (0, 16)]), V(cst, 256, [(0, 32), (1, 16)]), ALU.mult)
    TS("dve", uu, ang, 1.0 / (2 * math.pi), None, ALU.mult)
    CP("dve", V(WF, 2080, [(1, 512)]).bitcast(I32), uu)
    CP("dve", kf_, V(WF, 2080, [(1, 512)]).bitcast(I32))
    STT("dve", uu, kf_, -2 * math.pi, ang, ALU.mult, ALU.add)
    ACT(s4, uu, AF.Sin, scale=0.25)
    ACT(c4, uu, AF.Sin, scale=0.25, bias=halfpi)
    sh = V(WF, 1056, [(1, 512)]); ch = V(WF, 1568, [(1, 512)])
    TT("dve", sh, s4, c4, ALU.mult)
    TS("dve", sh, sh, 2.0, None, ALU.mult)
    TT("dve", ch, s4, s4, ALU.mult)
    TS("dve", ch, ch, -2.0, 1.0, ALU.mult, ALU.add)
    TT("dve", s4, sh, ch, ALU.mult)
    TS("dve", s4, s4, 2.0, None, ALU.mult)
    TT("dve", c4, sh, sh, ALU.mult)
    TS("dve", c4, c4, -2.0, 1.0, ALU.mult, ALU.add)

    ring_pos = [0]
    GUSE = {3: [(512, 8), (1, 432)], 4: [(1, 2560)], 25: [(1, 2048)]}
    rsem = [S.new_sem("rs%d" % i) for i in range(R_SLOTS)]
    def load_granule(g):
        s = ring_pos[0] % R_SLOTS; ring_pos[0] += 1
        use = GUSE.get(g, [(1, GSZ)])
        S.dma("sp", V(ring, s * GSZ, use), dview(wsc, g * G, [(GSZ, 128)] + use), rsem[s])
        return s * GSZ

    def norm_pre(xb, nblk=NB, sc0=0):
        for j in range(nblk):
            ACT(V(WF, 0, [(1, D)]), V(xh, xb + j * D, [(1, D)]), AF.Square, accum_out=V(stt, sc0 + j, [(1, 1)]))
        rstd_from(V(stt, sc0, [(1, nblk)]), V(stt, sc0, [(1, nblk)]), D)
        xsv = [V(xs, 0, [(1, D)]), V(vtok, 0, [(1, D)])]
        for j in range(nblk):
            if j == 0:
                ACT(xsv[j], V(xh, xb + j * D, [(1, D)]), AF.Copy, scale=V(stt, sc0 + j, [(1, 1)]))
            else:
                TS("dve", xsv[j], V(xh, xb + j * D, [(1, D)]), V(stt, sc0 + j, [(1, 1)]), None, ALU.mult)
    def norm_post(gname, nblk=NB):
        xso = [xs, vtok]
        for j in range(nblk):
            pt = psT[j]
            for c in range(8):
                TR(V(pt, c * 128, [(1, 128)]), V(xso[j], c * 128, [(1, 128)]), sig=(c == 7))
        for j in range(nblk):
            TT("dve", V(nT, j * 128, [(T, 8), (1, 128)]), V(psT[j], 0, [(128, 8), (1, 128)]), V(vec, VO[gname], [(1, 8), (0, 128)]), ALU.mult)
    def norm_transpose_all(gname, nblk=NB, xb=0):
        norm_pre(xb, nblk); norm_post(gname, nblk)

    msem = S.new_sem("msem")
    S.dma("sp", V(xh, 0, [(D, 2), (1, D)]), dview(mem_d, 0, [(D, 128), (128 * D, 2), (1, D)]), msem)
    S.dma("pool", V(ring, 0, [(512, 8), (1, 512)]), dview(xwkv_d, 0, [(1024, 128), (128 * 1024, 8), (1, 512)]), semc[4])
    S.dma("pool", V(ring, GSZ, [(512, 8), (1, 512)]), dview(xwkv_d, 512, [(1024, 128), (128 * 1024, 8), (1, 512)]), semc[5])
    norm_transpose_all("nmem", 2, 0)
    for j in range(2):
        pk = bankA()
        for c in range(8):
            MM(V(pk, 0, [(1, 512)]), V(nT, c * T + j * 128, [(1, 128)]), V(ring, c * 512, [(1, 512)]), start=(c == 0), stop=(c == 7))
        pv = bankA()
        for c in range(8):
            MM(V(pv, 0, [(1, 512)]), V(nT, c * T + j * 128, [(1, 128)]), V(ring, GSZ + c * 512, [(1, 512)]), start=(c == 0), stop=(c == 7))
        CP("act", V(Vm, j * 512, [(1, 512)]), V(pv, 0, [(1, 512)]))
        sq = V(WF, 0, [(1, 512)])
        ACT(sq, V(pk, 0, [(1, 512)]), AF.Square)
        ss4 = V(stt, 4, [(1, 4)])
        RED(ss4, V(WF, 0, [(128, 4), (1, 128)]))
        rstd_from(ss4, ss4, 128, 4)
        kn = V(WF, 512, [(1, 512)])
        TT("dve", V(WF, 512, [(128, 4), (1, 128)]), V(pk, 0, [(128, 4), (1, 128)]), V(stt, 4, [(1, 4), (0, 128)]), ALU.mult)
        TT("pool", V(WB, 0, [(128, 4), (1, 128)]), V(WF, 512, [(128, 4), (1, 128)]), V(vec, VO["xkn"], [(0, 4), (1, 128)]), ALU.mult)
        pt = bankT()
        for h in range(4):
            TR(V(pt, h * 128, [(1, 128)]), V(WB, h * 128, [(1, 128)]), sig=(h == 3))
        CP("act", V(KmT, j * 128, [(256, 4), (1, 128)]), V(pt, 0, [(128, 4), (1, 128)]))

    QT0 = 0; MIX0 = 8 * T; XQT0 = 0; XOT0 = 4 * T; PTX0 = 8 * T
    SC_MLA = 96 ** -0.5; SC_XA = 128 ** -0.5
    xsem = [[S.new_sem("xsem%d_%d" % (p_, j)) for j in range(NB)] for p_ in range(2)]; osem = [S.new_sem("osem%d" % j) for j in range(NB)]
    out_toks = []

    import itertools
    def run_chains(chains):
        chains = list(chains)
        while chains:
            for c_ in list(chains):
                try:
                    next(c_)
                except StopIteration:
                    chains.remove(c_)

    prefetched = {}
    def get_granule(g):
        if g in prefetched:
            return prefetched.pop(g)
        return load_granule(g)

    for j in range(NB):
        S.dma("sp", V(xh, j * D, [(1, D)]), dview(x_d, j * 128 * D, [(D, 128), (1, D)]), xsem[0][j])

    uq_box = [None]

    def step_chains(chs, k):
        for _ in range(k):
            for c_ in list(chs):
                try:
                    next(c_)
                except StopIteration:
                    chs.remove(c_)

    def norm_trs():
        xso = [xs, vtok]
        for j in range(NB):
            for c in range(8):
                TR(V(psT[j], c * 128, [(1, 128)]), V(xso[j], c * 128, [(1, 128)]), sig=(c == 7))
        for j in range(NB):
            TT("dve", V(nT, j * 128, [(T, 8), (1, 128)]), V(psT[j], 0, [(128, 8), (1, 128)]), V(vec, VO["nm"], [(1, 8), (0, 128)]), ALU.mult)

    def proj_group(g):
        go = get_granule(g)
        gw = 432 if g == 3 else 512
        for j in range(NB):
            pp = bankB()
            for c in range(8):
                MM(V(pp, 0, [(1, gw)]), V(nT, c * T + j * 128, [(1, 128)]), V(ring, go + c * 512, [(1, gw)]), start=(c == 0), stop=(c == 7))
            if g == 2:
                ACT(V(Gt, j * 512, [(1, 512)]), V(pp, 0, [(1, 512)]), AF.Silu)
                TT("pool", V(Gt, j * 512, [(128, 4), (1, 128)]), V(Gt, j * 512, [(128, 4), (1, 128)]), V(vec, VO["gon"], [(0, 4), (1, 128)]), ALU.mult)
            elif g == 0:
                CP("act", V(qk, j * 512, [(1, 512)]), V(pp, 0, [(1, 512)]))
            elif g == 1:
                CP("dve", V(vtok, j * 512, [(1, 512)]), V(pp, 0, [(1, 512)]))
            else:
                CP("dve", V(r3, j * 432, [(1, 432)]), V(pp, 0, [(1, 432)]))

    def mla(j, c, tn):
        b = tn * NB + j; R3 = j * 432; fb = c * 1760; bb = c * 1536; sc = 16 + c * 24
        px = psT[c]
        def PF(f0, dims, p0=0, np_=128):
            bd = [(2 * s_, n_) for (s_, n_) in dims[:-1]] + [(1, 2 * dims[-1][1])]
            return V(px, 2 * f0, bd, p0, np_).bitcast(F32)
        PBv = lambda f0, dims, p0=0, np_=128: V(px, f0, dims, p0, np_)
        cq = V(r3, R3 + 16, [(1, 256)]); ckv = V(r3, R3 + 272, [(1, 128)]); kpe = V(r3, R3 + 400, [(1, 32)])
        st = lambda i, w=1: V(stt, sc + i, [(1, w)])
        ACT(V(WF, fb + 768, [(1, 256)]), cq, AF.Square, accum_out=st(0))
        ACT(V(WF, fb + 1024, [(1, 128)]), ckv, AF.Square, accum_out=st(1))
        ACT(V(WF, fb + 1632, [(1, 32)]), kpe, AF.Square, accum_out=st(10))
        yield
        ACT(st(0), st(0), AF.Ln, scale=1.0 / 256, bias=epsc)
        ACT(st(1), st(1), AF.Ln, scale=1.0 / 128, bias=epsc)
        ACT(st(0, 2), st(0, 2), AF.Exp, scale=-0.5)
        yield
        ACT(V(WB, bb, [(1, 256)]), cq, AF.Copy, scale=st(0))
        ACT(V(WB, bb + 256, [(1, 128)]), ckv, AF.Copy, scale=st(1))
        yield
        for cc in range(3):
            TR(PBv(cc * 128, [(1, 128)]), V(WB, bb + cc * 128, [(1, 128)]), sig=(cc == 2))
        yield
        cT = bb + 384
        TT("dve", V(WB, cT, [(128, 3), (1, 128)]), PBv(0, [(128, 3), (1, 128)]), V(vec, VO["qan"], [(1, 3), (0, 128)]), ALU.mult)
        yield
        uq_off = uq_box[0]
        qf = bb + 768
        for hf in range(2):
            for cc in range(2):
                MM(PF(0, [(1, 384)]), V(WB, cT + cc * 128, [(1, 128)]), V(ring, uq_off + cc * 768 + hf * 384, [(1, 384)]), start=(cc == 0), stop=(cc == 1))
            yield
            ACT(V(WF, fb + 768, [(1, 384)]), PF(0, [(1, 384)]), AF.Square)
            yield
            RED(st(2 + hf * 4, 4), V(WF, fb + 768, [(96, 4), (1, 96)]))
            yield
            ACT(st(2 + hf * 4, 4), st(2 + hf * 4, 4), AF.Ln, scale=1.0 / 96, bias=epsc)
            yield
            ACT(st(2 + hf * 4, 4), st(2 + hf * 4, 4), AF.Exp, scale=-0.5)
            yield
            TT("dve", V(WB, qf + hf * 384, [(96, 4), (1, 64)]), PF(0, [(96, 4), (1, 64)]), V(stt, sc + 2 + hf * 4, [(1, 4), (0, 64)]), ALU.mult)
            TT("dve", V(WF, fb + hf * 128, [(32, 4), (1, 32)]), PF(64, [(96, 4), (1, 32)]), V(stt, sc + 2 + hf * 4, [(1, 4), (0, 32)]), ALU.mult)
            yield
        TT("pool", V(WF, fb, [(32, 8), (1, 32)]), V(WF, fb, [(32, 8), (1, 32)]), V(vec, VO["gq"] + 64, [(0, 8), (1, 32)]), ALU.mult)
        yield
        cosb = V(COS, b * 16, [(0, 8), (1, 16)]); sinb = V(SIN, b * 16, [(0, 8), (1, 16)])
        x1 = V(WF, fb, [(32, 8), (1, 16)]); x2 = V(WF, fb + 16, [(32, 8), (1, 16)])
        tA = V(WF, fb + 1280, [(16, 8), (1, 16)]); tB = V(WF, fb + 1408, [(16, 8), (1, 16)])
        TT("pool", tA, x1, cosb, ALU.mult); TT("pool", tB, x2, sinb, ALU.mult)
        yield
        TT("pool", V(WB, qf + 64, [(96, 8), (1, 16)]), tA, tB, ALU.subtract)
        yield
        TT("pool", tA, x2, cosb, ALU.mult); TT("pool", tB, x1, sinb, ALU.mult)
        yield
        TT("pool", V(WB, qf + 80, [(96, 8), (1, 16)]), tA, tB, ALU.add)
        yield
        for h in range(8):
            TR(PBv(h * 128, [(1, 128)], 0, 96), V(WB, qf + h * 96, [(1, 96)]), sig=(h == 7))
        yield
        CP("act", V(U, QT0 + j * 128, [(T, 8), (1, 128)], 0, 96), PBv(0, [(128, 8), (1, 128)], 0, 96))
        kp = V(WF, fb + 1536, [(1, 32)])
        TT("pool", kp, kpe, vcol("gk", 64, 32), ALU.mult)
        yield
        c1 = V(COS, b * 16, [(1, 16)]); s1 = V(SIN, b * 16, [(1, 16)])
        k1 = V(WF, fb + 1536, [(1, 16)]); k2 = V(WF, fb + 1552, [(1, 16)])
        rA = V(WF, fb + 1568, [(1, 16)]); rB = V(WF, fb + 1584, [(1, 16)])
        kr = fb + 1600
        TT("pool", rA, k1, c1, ALU.mult); TT("pool", rB, k2, s1, ALU.mult)
        yield
        TT("pool", V(WF, kr, [(1, 16)]), rA, rB, ALU.subtract)
        yield
        TT("pool", rA, k2, c1, ALU.mult); TT("pool", rB, k1, s1, ALU.mult)
        yield
        TT("pool", V(WF, kr + 16, [(1, 16)]), rA, rB, ALU.add)
        yield
        for hf in range(2):
            MM(PF(0, [(1, 512)]), V(WB, cT + 256, [(1, 128)]), V(ring, uq_off + 1536 + hf * 512, [(1, 512)]))
            yield
            ACT(V(WF, fb + 768, [(1, 512)]), PF(0, [(1, 512)]), AF.Square)
            yield
            RED(st(11 + hf * 4, 4), V(WF, fb + 768, [(128, 4), (1, 64)]))
            CP("act", V(VA, (b * 8 + hf * 4) * 66, [(66, 4), (1, 64)]), PF(64, [(128, 4), (1, 64)]))
            yield
            TS("dve", st(11 + hf * 4, 4), st(11 + hf * 4, 4), st(10), None, ALU.add)
            yield
            ACT(st(11 + hf * 4, 4), st(11 + hf * 4, 4), AF.Ln, scale=1.0 / 96, bias=epsc)
            yield
            ACT(st(11 + hf * 4, 4), st(11 + hf * 4, 4), AF.Exp, scale=-0.5)
            yield
            TT("dve", V(WF, fb + hf * 384, [(96, 4), (1, 64)]), PF(0, [(128, 4), (1, 64)]), V(stt, sc + 11 + hf * 4, [(1, 4), (0, 64)]), ALU.mult)
            yield
        kfb = bb + 768
        TT("dve", V(WB, kfb, [(96, 8), (1, 64)]), V(WF, fb, [(96, 8), (1, 64)]), V(vec, VO["gk"], [(0, 8), (1, 64)]), ALU.mult)
        TT("pool", V(WB, kfb + 64, [(96, 8), (1, 32)]), V(WF, kr, [(0, 8), (1, 32)]), V(stt, sc + 11, [(1, 8), (0, 32)]), ALU.mult)
        yield
        for h in range(8):
            TR(PBv(h * 128, [(1, 128)], 0, 96), V(WB, kfb + h * 96, [(1, 96)]), sig=(h == 7))
        yield
        CP("act", V(KT, b * 128, [(SEQ, 8), (1, 128)], 0, 96), PBv(0, [(128, 8), (1, 128)], 0, 96))

    norm_pre(0, NB, 2)
    norm_trs()
    proj_group(3)
    chains0 = [mla(j, j, 0) for j in range(NB)]
    proj_group(2); step_chains(chains0, 3)
    proj_group(0)
    uq_box[0] = get_granule(4)
    step_chains(chains0, 3)
    proj_group(1)
    while chains0:
        step_chains(chains0, 1)

    for ti in range(NTILE):
        t0 = ti * T
        xb = (ti % 2) * NB * D
        if ti + 1 < NTILE:
            xb2 = ((ti + 1) % 2) * NB * D
            for j in range(NB):
                S.dma("sp", V(xh, xb2 + j * D, [(1, D)]), dview(x_d, (t0 + T + j * 128) * D, [(D, 128), (1, D)]), xsem[(ti + 1) % 2][j])
        psG = psA[3]
        def gla(j):
            R3 = j * 432
            CP("dve", V(WB, 0, [(1, 16)]), V(r3, R3, [(1, 16)]))
            yield
            pt = bankT()
            yield
            yield
            TR(V(pt, 0, [(1, 128)], 0, 16), V(WB, 0, [(1, 16)]))
            yield
            CP("dve", V(WB, 16, [(1, 128)], 0, 16), V(pt, 0, [(1, 128)], 0, 16))
            yield
            yield
            yield
            MM(V(psG, 0, [(1, 256)]), V(WB, 16, [(1, 128)], 0, 16), V(w2b, 0, [(1, 256)], 0, 16))
            yield
            z = V(WF, 0, [(1, 256)]); l_ = V(WF, 256, [(1, 256)])
            TT("dve", z, V(psG, 0, [(1, 256)]), vcol("gb", 0, 256), ALU.add)
            yield
            ACT(z, z, AF.Exp, scale=-1.0)
            yield
            ACT(l_, z, AF.Ln, bias=1.0)
            yield
            yield
            yield
            MM(V(psG, 0, [(1, 256)]), U_f, l_)
            for h in range(4):
                MM(V(psG, 256 + h, [(1, 1)], 0, 64), V(WF, 256 + h * 64, [(1, 64)]), onesf, sig=(h == 3))
            yield
            Eq = V(WF, 512, [(1, 256)]); Ek = V(WF, 768, [(1, 256)])
            ACT(Eq, V(psG, 0, [(1, 256)]), AF.Exp, scale=-1.0 / 16)
            ACT(Ek, V(psG, 0, [(1, 256)]), AF.Exp, scale=1.0 / 16)
            ACT(V(stt, 8, [(1, 4)], 0, 64), V(psG, 256, [(1, 4)], 0, 64), AF.Exp, scale=-1.0 / 16)
            yield
            qd = V(WB, 256, [(1, 256)]); ki = V(WB, 512, [(1, 256)])
            TT("pool", qd, V(qk, j * 512, [(1, 256)]), Eq, ALU.mult)
            TT("pool", ki, V(qk, j * 512 + 256, [(1, 256)]), Ek, ALU.mult)
            yield
            pt = bankT()
            yield
            yield
            for h in range(8):
                TR(V(pt, h * 128, [(1, 128)], 0, 64), V(WB, 256 + h * 64, [(1, 64)]), sig=(h == 7))
            yield
            qkT = 768
            CP("act", V(WB, qkT, [(1, 1024)], 0, 64), V(pt, 0, [(1, 1024)], 0, 64))
            yield
            yield
            yield
            for h in range(4):
                MM(V(psG, h * 128, [(1, 128)]), V(WB, qkT + (4 + h) * 128, [(1, 128)], 0, 64), V(WB, qkT + h * 128, [(1, 128)], 0, 64), sig=(h == 3))
            yield
            ATm = 1792
            STT("dve", V(WB, ATm, [(128, 4), (1, 128)]), V(psG, 0, [(128, 4), (1, 128)]), 0.125, V(Ub, 0, [(0, 4), (1, 128)]), ALU.mult, ALU.mult)
            yield
            yield
            yield
            for h in range(4):
                MM(V(psG, h * 128, [(1, 128)]), V(WB, ATm + h * 128, [(1, 128)]), V(vtok, j * 512 + h * 128, [(1, 128)]), start=True, stop=False, sig=False)
                MM(V(psG, h * 128, [(1, 128)]), V(WB, qkT + h * 128, [(1, 128)], 0, 64), V(Sb, h * 128, [(1, 128)], 0, 64), start=False, stop=True, sig=(h == 3))
            yield
            osq = V(WF, 1024, [(1, 512)])
            ACT(osq, V(psG, 0, [(1, 512)]), AF.Square)
            yield
            so = V(stt, 12, [(1, 4)])
            RED(so, V(WF, 1024, [(128, 4), (1, 128)]))
            yield
            rstd_from(so, so, 128, 4)
            yield
            TT("dve", V(WF, 1536, [(128, 4), (1, 128)]), V(psG, 0, [(128, 4), (1, 128)]), V(stt, 12, [(1, 4), (0, 128)]), ALU.mult)
            yield
            ogv = V(xs, 0, [(1, 512)])
            TT("pool", ogv, V(WF, 1536, [(1, 512)]), V(Gt, j * 512, [(1, 512)]), ALU.mult)
            yield
            yield
            for h in range(4):
                MM(V(psG, h * 128, [(1, 128)], 0, 64), V(WB, 512 + h * 64, [(1, 64)]), V(vtok, j * 512 + h * 128, [(1, 128)]), sig=(h == 3))
            yield
            TT("dve", V(Sst, 0, [(1, 512)], 0, 64), V(psG, 0, [(1, 512)], 0, 64), V(Sst, 0, [(1, 512)], 0, 64), ALU.add)
            pt = bankT()
            for c in range(4):
                TR(V(pt, c * 128, [(1, 128)]), V(xs, c * 128, [(1, 128)]), sig=(c == 3))
            yield
            TT("dve", V(Sst, 0, [(128, 4), (1, 128)], 0, 64), V(Sst, 0, [(128, 4), (1, 128)], 0, 64), V(stt, 8, [(1, 4), (0, 128)], 0, 64), ALU.mult)
            CP("act", V(U, MIX0 + j * 128, [(T, 4), (1, 128)]), V(pt, 0, [(128, 4), (1, 128)]))
            yield
            TS("pool", V(Sb, 0, [(1, 512)], 0, 64), V(Sst, 0, [(1, 512)], 0, 64), 0.125, None, ALU.mult)
            yield

        gla_gen = itertools.chain(gla(0), gla(1)) if NB == 2 else itertools.chain(*[gla(j) for j in range(NB)])
        gla_done = [False]
        def gla_advance(k):
            for _ in range(k):
                if gla_done[0]:
                    return
                try:
                    next(gla_gen)
                except StopIteration:
                    gla_done[0] = True

        units = []
        for h in range(8):
            for kb in range(0, ti * NB, 2):
                units.append((h, "off", kb))
            units.append((h, "diag", ti * NB))
        AHEAD = 2
        pS_of = {}; pO_of = {}
        deferred = []

        def emit_S(ui):
            h, kind, kb = units[ui]
            pS = bankA(); pS_of[ui] = pS
            q_ = lambda c0, n: V(U, QT0 + h * T + c0, [(1, n)], 0, 96)
            k_ = lambda kbb: V(KT, h * SEQ + kbb * 128, [(1, 128)], 0, 96)
            if kind == "off":
                MM(V(pS, 0, [(1, T)]), k_(kb), q_(0, T), sig=False)
                MM(V(pS, T, [(1, T)]), k_(kb + 1), q_(0, T))
            else:
                MM(V(pS, 0, [(1, T)]), k_(kb), q_(0, T), sig=False)
                MM(V(pS, T, [(1, 128)]), k_(kb + 1), q_(128, 128))

        def emit_E(ui):
            h, kind, kb = units[ui]
            pS = pS_of[ui]; pto = (ui % 3) * 512
            n = 2 * T if kind == "off" else T + 128
            ACT(V(PT, pto, [(1, n)]), V(pS, 0, [(1, n)]), AF.Exp, scale=SC_MLA)
            if kind == "diag":
                TT("pool", V(PT, pto, [(T, 2), (1, 128)]), V(PT, pto, [(T, 2), (1, 128)]), V(Ub, 0, [(0, 2), (1, 128)]), ALU.mult)

        def emit_norm1(h, pO, ui):
            recrow = V(WF, 2048 + (h % 2) * T, [(1, T)], 64, 1)
            RECIP(recrow, V(pO, 0, [(1, T)], 64, 1))
            deferred.append([ui + 3, (lambda: emit_norm2(h, pO)), h])

        def emit_norm2(h, pO):
            recrow = V(WF, 2048 + (h % 2) * T, [(1, T)], 64, 1)
            pb = bankA()
            MM(V(pb, 0, [(1, T)], 0, 64), V(cst, 128 + 64, [(1, 64)], 64, 1), recrow)
            bcs = V(WF, 2048 + (h % 2) * T, [(1, T)], 0, 64)
            CP("dve", bcs, V(pb, 0, [(1, T)], 0, 64))
            TT("dve", V(U, MIX0 + (4 + h // 2) * T, [(1, T)], (h % 2) * 64, 64), V(pO, 0, [(1, T)], 0, 64), bcs, ALU.mult)

        def emit_PV(ui):
            h, kind, kb = units[ui]
            pto = (ui % 3) * 512
            first = (ui == 0) or (units[ui - 1][0] != h)
            if first:
                while any(d[2] <= h - 2 for d in deferred):
                    for d in list(deferred):
                        if d[2] <= h - 2:
                            deferred.remove(d); d[1]()
                pO_of[h] = bankB()
            pO = pO_of[h]
            va = lambda kbb: V(VA, (kbb * 8 + h) * 66, [(1, 128)])
            if kind == "off":
                MM(V(pO, 0, [(1, T)]), va(kb), V(PT, pto, [(1, T)]), start=first, stop=False, sig=False)
                MM(V(pO, 0, [(1, T)]), va(kb + 1), V(PT, pto + T, [(1, T)]), start=False, stop=False, sig=True)
            else:
                MM(V(pO, 0, [(1, T)]), va(kb), V(PT, pto, [(1, T)]), start=first, stop=False, sig=False)
                MM(V(pO, 128, [(1, 128)]), va(kb + 1), V(PT, pto + T, [(1, 128)]), start=False, stop=True, sig=True)
                deferred.append([ui + 2, (lambda hh=h, pp=pO, uu=ui: emit_norm1(hh, pp, uu + 2)), h])

        nu = len(units)
        gk_ = max(1, -(-86 // nu))
        for step in range(nu + AHEAD):
            if step < nu:
                emit_S(step); emit_E(step)
            if step - AHEAD >= 0:
                emit_PV(step - AHEAD)
            for d in list(deferred):
                if d[0] <= step - AHEAD:
                    deferred.remove(d); d[1]()
            gla_advance(gk_)
        while deferred:
            d = deferred.pop(0); d[1]()
        gla_advance(10 ** 6)

        wo = [get_granule(5), get_granule(6)]
        for j in range(NB):
            for hf in range(2):
                pp = bankA()
                for c in range(8):
                    MM(V(pp, 0, [(1, 512)]), V(U, MIX0 + c * T + j * 128, [(1, 128)]), V(ring, wo[c // 4] + (c % 4) * 1024 + hf * 512, [(1, 512)]), start=(c == 0), stop=(c == 7))
                xv = V(xh, xb + j * D + hf * 512, [(1, 512)])
                TT("dve", xv, V(pp, 0, [(1, 512)]), xv, ALU.add)
        xq_off = get_granule(7)
        norm_transpose_all("nxa", NB, xb)
        def xq_chain(j):
            pq = psA[j]
            for c in range(8):
                MM(V(pq, 0, [(1, 512)]), V(nT, c * T + j * 128, [(1, 128)]), V(ring, xq_off + c * 512, [(1, 512)]), start=(c == 0), stop=(c == 7))
            yield
            ACT(V(WF, j * 1024, [(1, 512)]), V(pq, 0, [(1, 512)]), AF.Square)
            yield
            s4_ = V(stt, 4 + 4 * j, [(1, 4)])
            RED(s4_, V(WF, j * 1024, [(128, 4), (1, 128)]))
            yield
            ACT(s4_, s4_, AF.Ln, scale=1.0 / 128, bias=epsc)
            yield
            ACT(s4_, s4_, AF.Exp, scale=-0.5)
            yield
            TT("dve", V(WF, j * 1024 + 512, [(128, 4), (1, 128)]), V(pq, 0, [(128, 4), (1, 128)]), V(stt, 4 + 4 * j, [(1, 4), (0, 128)]), ALU.mult)
            yield
            TT("pool", V(WB, j * 512, [(128, 4), (1, 128)]), V(WF, j * 1024 + 512, [(128, 4), (1, 128)]), V(vec, VO["xqn"], [(0, 4), (1, 128)]), ALU.mult)
            yield
            pt = psT[j]
            for hh in range(4):
                TR(V(pt, hh * 128, [(1, 128)]), V(WB, j * 512 + hh * 128, [(1, 128)]), sig=(hh == 3))
            yield
            CP("act", V(U, XQT0 + j * 128, [(T, 4), (1, 128)]), V(pt, 0, [(128, 4), (1, 128)]))
        run_chains([xq_chain(j) for j in range(NB)])

        def xS(hh):
            base = (hh % 2) * 2 * T
            for kb in range(2):
                pS = bankA()
                MM(V(pS, 0, [(1, T)]), V(KmT, hh * 256 + kb * 128, [(1, 128)]), V(U, XQT0 + hh * T, [(1, T)]))
                ACT(V(U, PTX0 + base + kb * T, [(1, T)]), V(pS, 0, [(1, T)]), AF.Exp, scale=SC_XA)
        def xPV(hh):
            base = (hh % 2) * 2 * T
            pb_ = psB[hh % 2]
            for kb in range(2):
                MM(V(pb_, 0, [(1, T)]), V(Vm, kb * 512 + hh * 128, [(1, 128)]), V(U, PTX0 + base + kb * T, [(1, T)]), start=(kb == 0), stop=(kb == 1))
            for kb in range(2):
                MM(V(pb_, T, [(1, T)]), V(onesb, 0, [(1, 128)]), V(U, PTX0 + base + kb * T, [(1, T)]), start=(kb == 0), stop=(kb == 1))
            rd = V(WF, 2048 + (hh % 2) * T, [(1, T)])
            ACT(rd, V(pb_, T, [(1, T)]), AF.Ln)
            ACT(rd, rd, AF.Exp, scale=-1.0)
            TT("dve", V(U, XOT0 + hh * T, [(1, T)]), V(pb_, 0, [(1, T)]), rd, ALU.mult)
        xS(0); xS(1); xPV(0); xS(2); xPV(1); xS(3); xPV(2); xPV(3)
        xo_off = get_granule(8)
        for j in range(NB):
            for hf in range(2):
                pp = bankA()
                for c in range(4):
                    MM(V(pp, 0, [(1, 512)]), V(U, XOT0 + c * T + j * 128, [(1, 128)]), V(ring, xo_off + c * 1024 + hf * 512, [(1, 512)]), start=(c == 0), stop=(c == 3))
                xv = V(xh, xb + j * D + hf * 512, [(1, 512)])
                TT("dve", xv, V(pp, 0, [(1, 512)]), xv, ALU.add)
        norm_transpose_all("nffn", NB, xb)
        for fc in range(NFC):
            if fc % 2 == 0:
                gu = get_granule(9 + fc // 2)
            sub = fc % 2
            pg = psA[0] if fc % 2 == 0 else psA[2]
            pu = psA[1] if fc % 2 == 0 else psA[3]
            for c in range(8):
                MM(V(pg, 0, [(1, T)]), V(ring, gu + sub * 2048 + c * 128, [(1, 128)]), V(nT, c * T, [(1, T)]), start=(c == 0), stop=(c == 7))
            for c in range(8):
                MM(V(pu, 0, [(1, T)]), V(ring, gu + sub * 2048 + 1024 + c * 128, [(1, 128)]), V(nT, c * T, [(1, T)]), start=(c == 0), stop=(c == 7))
            sl4 = fc % 4
            gb_ = sl4 * (T + 2)
            ub = V(WB, sl4 * T, [(1, T)])
            CP("pool", V(WF, gb_, [(1, 2)]), V(halo, fc * 2, [(1, 2)]))
            CP("act", V(WF, gb_ + 2, [(1, T)]), V(pg, 0, [(1, T)]))
            CP("act", ub, V(pu, 0, [(1, T)]))
            CP("pool", V(halo, fc * 2, [(1, 2)]), V(WF, gb_ + T, [(1, 2)]))
            tc_ = V(WF, 1032 + sl4 * T, [(1, T)])
            cw = lambda i: V(vec, VO["cw"] + fc * 3 + i, [(1, 1)])
            TS("pool", tc_, V(WF, gb_ + 2, [(1, T)]), cw(2), V(vec, VO["cb"] + fc, [(1, 1)]), ALU.mult, ALU.add)
            STT("dve", tc_, V(WF, gb_ + 1, [(1, T)]), cw(1), tc_, ALU.mult, ALU.add)
            STT("dve", tc_, V(WF, gb_, [(1, T)]), cw(0), tc_, ALU.mult, ALU.add)
            ACT(tc_, tc_, AF.Silu)
            TT("pool", V(U, fc * T, [(1, T)]), tc_, ub, ALU.mult)
        nxt = ti + 1 < NTILE
        if nxt:
            norm_pre(((ti + 1) % 2) * NB * D, NB, 2)
        def dn_group(k):
            dn = get_granule(20 + k)
            nj = 4 if k < 5 else 2
            for jj in range(nj):
                fc = 4 * k + jj
                for j in range(NB):
                    for hf in range(2):
                        MM(V(psA[j * 2 + hf], 0, [(1, 512)]), V(U, fc * T + j * 128, [(1, 128)]), V(ring, dn + jj * 1024 + hf * 512, [(1, 512)]), start=(fc == 0), stop=(fc == NFC - 1), sig=(j == NB - 1 and hf == 1))
        if nxt:
            dn_group(0); dn_group(1)
            norm_trs()
            proj_group(3)
            chains = [mla(j, j, ti + 1) for j in range(NB)]
            dn_group(2); step_chains(chains, 3)
            proj_group(2); step_chains(chains, 2)
            dn_group(3)
            proj_group(0)
            uq_box[0] = get_granule(4)
            step_chains(chains, 8)
            dn_group(4); step_chains(chains, 8)
            proj_group(1)
            while chains:
                step_chains(chains, 1)
            dn_group(5)
        else:
            for k in range(6):
                dn_group(k)
        for j in range(NB):
            for hf in range(2):
                xv = V(xh, xb + j * D + hf * 512, [(1, 512)])
                TT("dve", xv, V(psA[j * 2 + hf], 0, [(1, 512)]), xv, ALU.add)
            out_toks.append(S.dma("sp", dview(out_d, (t0 + j * 128) * D, [(D, 128), (1, D)]), V(xh, xb + j * D, [(1, D)]), osem[j]))
    S.wait_all("sp", [[o_, S.dma_cum[o_]] for o_ in osem])
    return nc, S


_CACHE = {}

def kernel(x, mem, positions, norm_mix, w_in, gla_gate_w2, gla_gate_b, gla_out_norm,
           mla_q_a_norm, mla_w_uq, mla_kv_a_norm, mla_w_ukv, mla_q_norm, mla_k_norm, w_out,
           norm_xa, norm_mem, xa_w_q, xa_w_kv, xa_q_norm, xa_k_norm, xa_w_o,
           norm_ffn, ffn_w_gate, ffn_w_up, ffn_conv_w, ffn_conv_b, ffn_w_down):
    f = lambda a: np.ascontiguousarray(np.asarray(a, dtype=np.float32))
    x = f(x); mem = f(mem); positions = np.asarray(positions).astype(np.int32)
    fm = lambda v, c: f(v).reshape(c, 128).T
    rep = lambda v: np.broadcast_to(f(v).reshape(1, -1), (128, f(v).size))
    cols = [fm(norm_mix[0], 8), fm(norm_xa[0], 8), fm(norm_ffn[0], 8), fm(norm_mem[0], 8), fm(mla_q_a_norm[0], 2), fm(mla_kv_a_norm[0], 1),
            rep(gla_gate_b[0]), rep(gla_out_norm[0]), rep(mla_q_norm[0]), rep(mla_k_norm[0]), rep(xa_q_norm[0]), rep(xa_k_norm[0]),
            f(ffn_conv_w[0]).reshape(3, NFC, 128).transpose(2, 1, 0).reshape(128, 66), fm(ffn_conv_b[0], NFC)]
    vec = np.ascontiguousarray(np.concatenate(cols, axis=1).astype(np.float32))
    assert vec.shape == (128, NV)
    ident = np.eye(128, dtype=np.float32)
    Utri = np.triu(np.ones((128, 128), dtype=np.float32))
    inv = (10000.0 ** (-np.arange(16, dtype=np.float32) / 16)).astype(np.float32)
    cst = np.ascontiguousarray(np.concatenate([ident, Utri, np.broadcast_to(inv[None, :], (128, 16))], axis=1).astype(np.float32))
    if "nc" not in _CACHE:
        _CACHE["nc"] = build_program()[0]
    nc = _CACHE["nc"]
    shared = {"cst": cst, "vec": vec, "w_in": f(w_in[0]), "w2": f(gla_gate_w2[0]), "w_uq": f(mla_w_uq[0]), "w_ukv": f(mla_w_ukv[0]),
              "w_out": f(w_out[0]), "xa_w_q": f(xa_w_q[0]), "xa_w_kv": f(xa_w_kv[0]), "xa_w_o": f(xa_w_o[0]),
              "w_gate": f(ffn_w_gate[0]), "w_up": f(ffn_w_up[0]), "w_down": f(ffn_w_down[0])}
    in_maps = []
    for c in range(8):
        m = dict(shared)
        m["x"] = x[c]; m["mem"] = mem[c]
        m["pos"] = np.ascontiguousarray(positions[c].reshape(32, 128).T)
        in_maps.append(m)
    res = run_bass_kernel_spmd(nc, in_maps, core_ids=list(range(8)))
    return np.stack([np.asarray(r["out"]).reshape(SEQ, D) for r in res.results], axis=0).astype(np.float32)
```

```python
import numpy as np
import concourse.bass as bass
import concourse.mybir as mybir

F32 = mybir.dt.float32
BF = mybir.dt.bfloat16
I32 = mybir.dt.int32
ALU = mybir.AluOpType
AF = mybir.ActivationFunctionType
AX = mybir.AxisListType


def _prod(xs):
    r = 1
    for v in xs:
        r *= int(v)
    return r


class Sched:
    def __init__(self, nc):
        self.nc = nc
        self.eng = dict(pe=nc.tensor, act=nc.scalar, dve=nc.vector, pool=nc.gpsimd, sp=nc.sync)
        self.sem = {}
        self.cnt = {}
        for e in self.eng:
            self.sem[e] = nc.alloc_semaphore("cs_" + e)
            self.cnt[e] = 0
        self.semh = {("cs_" + e): self.sem[e] for e in self.eng}
        self.seen = {e: {} for e in self.eng}
        self.hist = {}
        self.dma_cum = {}
        self.n_wait = 0
        self.n_ops = 0

    def new_sem(self, name):
        h = self.nc.alloc_semaphore(name)
        self.semh[name] = h
        self.dma_cum[name] = 0
        return name

    def _acc(self, ap):
        t = ap.tensor
        name = t.name
        space = str(ap.space)
        pat = ap.ap
        off = int(ap.offset)
        if "PSUM" in space:
            return (name, 0, 1 << 30, 0, 128, True)
        if "SB" in space:
            psz = _prod(t.shape[1:])
            p0 = off // psz
            f0 = off % psz
            npart = pat[0][1]
            span = sum((c - 1) * abs(s) for s, c in pat[1:]) + 1
            return (name, f0, f0 + span, p0, p0 + npart, False)
        span = sum((c - 1) * abs(s) for s, c in pat) + 1
        return (name, off, off + span, 0, 1, False)

    def _deps(self, e, is_dma, accs, skip_tok=None):
        need = {}
        for (key, lo, hi, plo, phi, excl), w in accs:
            lst = self.hist.get(key)
            if not lst:
                continue
            for h in lst:
                hlo, hhi, hplo, hphi, he, hdma, tok, hw = h
                if hhi <= lo or hi <= hlo or hphi <= plo or phi <= hplo:
                    continue
                if skip_tok is not None and tok is skip_tok:
                    continue
                if he == e and not is_dma and not hdma:
                    if e == "pe":
                        continue
                    if not (hw or w):
                        continue
                else:
                    if not (hw or w or excl):
                        continue
                sn, val = tok[0], tok[1]
                assert val is not None, "unsealed dma token used"
                if need.get(sn, 0) < val:
                    need[sn] = val
        return need

    def _record(self, e, is_dma, accs, tok):
        for (key, lo, hi, plo, phi, excl), w in accs:
            lst = self.hist.setdefault(key, [])
            keep = []
            for h in lst:
                hlo, hhi, hplo, hphi, he, hdma, htok, hw = h
                contained = lo <= hlo and hhi <= hi and plo <= hplo and hphi <= phi
                if contained:
                    if w:
                        continue
                    if excl and (he != e or hdma or is_dma):
                        continue
                    if (not hw) and he == e and not is_dma and not hdma:
                        continue
                keep.append(h)
            keep.append((lo, hi, plo, phi, e, is_dma, tok, w))
            self.hist[key] = keep

    def _emit_waits(self, e, need):
        eng = self.eng[e]
        seen = self.seen[e]
        for sn, val in need.items():
            if seen.get(sn, 0) < val:
                eng.wait_ge(self.semh[sn], val)
                seen[sn] = val
                self.n_wait += 1

    def op(self, e, fn, kwargs, reads, writes, sig=True):
        accs = [(self._acc(a), False) for a in reads] + [(self._acc(a), True) for a in writes]
        need = self._deps(e, False, accs)
        self._emit_waits(e, need)
        ins = fn(**kwargs)
        if sig:
            self.cnt[e] += 1
            ins.then_inc(self.sem[e], 1)
            tok = ("cs_" + e, self.cnt[e])
        else:
            tok = ("cs_" + e, self.cnt[e] + 1)
        self._record(e, False, accs, tok)
        self.n_ops += 1
        return ins

    def dma(self, e, out, in_, sem, tok=None, **kw):
        accs = [(self._acc(in_), False), (self._acc(out), True)]
        need = self._deps(e, True, accs, skip_tok=tok)
        self._emit_waits(e, need)
        ins = self.eng[e].dma_start(out=out, in_=in_, **kw)
        ins.then_inc(self.semh[sem], 16)
        self.dma_cum[sem] += 16
        if tok is None:
            tok = [sem, self.dma_cum[sem]]
        self._record(e, True, accs, tok)
        self.n_ops += 1
        return tok

    def group_tok(self, sem):
        return [sem, None]

    def seal(self, tok):
        tok[1] = self.dma_cum[tok[0]]

    def wait_all(self, e, toks):
        need = {}
        for sn, val in toks:
            if need.get(sn, 0) < val:
                need[sn] = val
        self._emit_waits(e, need)


def view(t, p0, npart, f0, dims):
    psz = _prod(t.shape[1:])
    return bass.AP(t, p0 * psz + f0, [[psz, npart]] + [[int(s), int(c)] for s, c in dims])


def dview(t, off, dims):
    return bass.AP(t, int(off), [[int(s), int(c)] for s, c in dims])


import math
from concourse.bass_utils import run_bass_kernel_spmd

D = 1024; SEQ = 4096; NBLK = 32; T = 256; NB = T // 128; NTILE = SEQ // T
FF = 2816; NFC = 22; EPS = 1e-6
GSZ = 4096
R_SLOTS = 3
VO = {}
def _vo():
    o = 0
    for n, w in [("nm", 8), ("nxa", 8), ("nffn", 8), ("nmem", 8), ("qan", 2), ("kvan", 1), ("gb", 256), ("gon", 128),
                 ("gq", 96), ("gk", 96), ("xqn", 128), ("xkn", 128), ("cw", 66), ("cb", 22)]:
        VO[n] = o; o += w
    return o
NV = _vo()


def build_program(dbg=False):
    nc = bass.Bass("TRN2", target_bir_lowering=False)
    S = Sched(nc)
    dt_in = lambda n, shp, dt=F32: nc.dram_tensor(n, shp, dt, kind="ExternalInput")
    x_d = dt_in("x", [SEQ, D]); mem_d = dt_in("mem", [256, D]); pos_d = dt_in("pos", [128, 32], I32)
    cst_d = dt_in("cst", [128, 272]); vec_d = dt_in("vec", [128, NV])
    w_in_d = dt_in("w_in", [D, 1968]); w2_d = dt_in("w2", [16, 256]); w_uq_d = dt_in("w_uq", [256, 768]); w_ukv_d = dt_in("w_ukv", [128, 1024])
    w_out_d = dt_in("w_out", [D, D]); xwq_d = dt_in("xa_w_q", [D, 512]); xwkv_d = dt_in("xa_w_kv", [D, 1024]); xwo_d = dt_in("xa_w_o", [512, D])
    wg_d = dt_in("w_gate", [D, FF]); wu_d = dt_in("w_up", [D, FF]); wd_d = dt_in("w_down", [FF, D])
    out_d = nc.dram_tensor("out", [SEQ, D], F32, kind="ExternalOutput")
    NG = 26
    wsc = nc.dram_tensor("wsc", [NG * 128, GSZ], BF, kind="Internal")

    def sb(name, n, dt):
        return nc.alloc_sbuf_tensor("s_" + name, [128, n], dt)
    KT = sb("KT", 8 * SEQ, BF); VA = sb("VA", NBLK * 8 * 66 + 64, BF)
    cst = sb("cst", 272, F32); identb = sb("identb", 128, BF); Ub = sb("Ub", 128, BF); onesb = sb("onesb", 128, BF); small = sb("small", 8, F32)
    vec = sb("vec", NV, F32); COS = sb("COS", 512, F32); SIN = sb("SIN", 512, F32); w2b = sb("w2b", 256, BF)
    KmT = sb("KmT", 4 * 256, BF); Vm = sb("Vm", 2 * 512, BF); Sst = sb("Sst", 512, F32); Sb = sb("Sb", 512, BF); halo = sb("halo", 44, F32)
    xh = sb("xh", 2 * NB * D, F32); xs = sb("xs", D, BF); nT = sb("nT", 8 * T, BF)
    qk = sb("qk", NB * 512, F32); vtok = sb("vtok", NB * 512, BF); Gt = sb("Gt", NB * 512, BF); r3 = sb("r3", NB * 432, F32)
    WF = sb("WF", 3520, F32)
    WB = sb("WB", 3072, BF)
    U = sb("U", NFC * T, BF)
    PT = sb("PT", 3 * 512, BF)
    stt = sb("stt", 64, F32)
    ring = sb("ring", R_SLOTS * GSZ, BF)
    posi = sb("posi", 32, I32)
    psA = [nc.alloc_psum_tensor("psA%d" % i, [128, 512], F32) for i in range(4)]
    psB = [nc.alloc_psum_tensor("psB%d" % i, [128, 512], F32) for i in range(2)]
    psT = [nc.alloc_psum_tensor("psT%d" % i, [128, 1024], BF) for i in range(2)]
    rot = {"A": 0, "B": 0, "T": 0}
    def bankA():
        rot["A"] += 1; return psA[rot["A"] % 3]
    def bankB():
        rot["B"] += 1; return psB[rot["B"] % 2]
    def bankT():
        rot["T"] += 1; return psT[rot["T"] % 2]

    V = lambda t, f0, dims, p0=0, np_=128: view(t, p0, np_, f0, dims)
    isap = lambda a: isinstance(a, bass.AP)
    E = S.eng

    def ACT(out, in_, func, **kw):
        reads = [in_] + [v for k, v in kw.items() if isap(v) and k != "accum_out"]
        writes = [out] + ([kw["accum_out"]] if "accum_out" in kw else [])
        S.op("act", nc.scalar.activation, dict(out=out, in_=in_, func=func, **kw), reads, writes)
    def TT(e, out, in0, in1, op):
        S.op(e, E[e].tensor_tensor, dict(out=out, in0=in0, in1=in1, op=op), [in0, in1], [out])
    def TS(e, out, in0, s1, s2, op0, op1=None):
        kw = dict(out=out, in0=in0, scalar1=s1, scalar2=s2, op0=op0)
        if op1 is not None: kw["op1"] = op1
        S.op(e, E[e].tensor_scalar, kw, [in0] + [a for a in (s1, s2) if isap(a)], [out])
    def STT(e, out, in0, sc, in1, op0, op1):
        S.op(e, E[e].scalar_tensor_tensor, dict(out=out, in0=in0, scalar=sc, in1=in1, op0=op0, op1=op1), [in0, in1] + ([sc] if isap(sc) else []), [out])
    def CP(e, out, in_):
        if e == "act":
            S.op(e, nc.scalar.copy, dict(out=out, in_=in_), [in_], [out])
        else:
            S.op(e, E[e].tensor_copy, dict(out=out, in_=in_), [in_], [out])
    def RED(out, in_):
        S.op("dve", nc.vector.tensor_reduce, dict(out=out, in_=in_, axis=AX.X, op=ALU.add), [in_], [out])
    def RECIP(out, in_):
        S.op("dve", nc.vector.reciprocal, dict(out=out, in_=in_), [in_], [out])
    def MEMSET(e, ap, c):
        S.op(e, E[e].memset, dict(ap=ap, constant=c), [], [ap])
    def MM(out, lhsT, rhs, start=True, stop=True, sig=None, **kw):
        S.op("pe", nc.tensor.matmul, dict(out=out, lhsT=lhsT, rhs=rhs, start=start, stop=stop, **kw), [lhsT, rhs], [out], sig=(stop if sig is None else sig))
    def TR(out, in_, sig=True):
        idn = V(identb, 0, [(1, 128)])
        S.op("pe", nc.tensor.transpose, dict(out=out, in_=in_, identity=idn), [in_, idn], [out], sig=sig)

    ident_f = V(cst, 0, [(1, 128)]); U_f = V(cst, 128, [(1, 128)])
    epsc = V(small, 0, [(1, 1)]); halfpi = V(small, 1, [(1, 1)]); onesf = V(small, 2, [(1, 1)])
    vcol = lambda n, i=0, w=1: V(vec, VO[n] + i, [(1, w)])

    def rstd_from(ss, out, n, k=1):
        ACT(out, ss, AF.Ln, scale=1.0 / n, bias=epsc)
        ACT(out, out, AF.Exp, scale=-0.5)

    semc = [S.new_sem("pro%d" % i) for i in range(6)]
    S.dma("sp", V(cst, 0, [(1, 272)]), dview(cst_d, 0, [(272, 128), (1, 272)]), semc[0])
    S.dma("sp", V(vec, 0, [(1, NV)]), dview(vec_d, 0, [(NV, 128), (1, NV)]), semc[1])
    S.dma("sp", V(posi, 0, [(1, 32)]), dview(pos_d, 0, [(32, 128), (1, 32)]), semc[2])
    S.dma("pool", V(w2b, 0, [(1, 256)], 0, 16), dview(w2_d, 0, [(256, 16), (1, 256)]), semc[3])
    gsem = [S.new_sem("gs%d" % g) for g in range(NG)]
    G = 128 * GSZ
    def cast(g, dst_off, dst_dims, src_t, src_off, src_dims, tok):
        S.dma("pool", dview(wsc, g * G + dst_off, [(GSZ, 128)] + dst_dims), dview(src_t, src_off, src_dims), gsem[g], tok=tok)
    def cast_group(g, items):
        tok = S.group_tok(gsem[g])
        for it in items:
            cast(g, *it, tok)
        S.seal(tok)
    WIN_ORDER = [3, 2, 0, 1]
    CAST = {}
    CAST[2] = [(0, [(512, 8), (1, 512)], w_in_d, 1040, [(1968, 128), (128 * 1968, 8), (1, 512)])]
    CAST[0] = [(0, [(512, 8), (1, 512)], w_in_d, 0, [(1968, 128), (128 * 1968, 8), (1, 512)])]
    CAST[1] = [(0, [(512, 8), (1, 512)], w_in_d, 512, [(1968, 128), (128 * 1968, 8), (1, 512)])]
    CAST[3] = [(0, [(512, 8), (1, 16)], w_in_d, 1024, [(1968, 128), (128 * 1968, 8), (1, 16)]),
               (16, [(512, 8), (1, 416)], w_in_d, 1552, [(1968, 128), (128 * 1968, 8), (1, 416)])]
    CAST[4] = [(0, [(768, 2), (1, 768)], w_uq_d, 0, [(768, 128), (128 * 768, 2), (1, 768)]),
               (1536, [(1, 1024)], w_ukv_d, 0, [(1024, 128), (1, 1024)])]
    for gi in range(2):
        CAST[5 + gi] = [(0, [(1024, 4), (1, 1024)], w_out_d, gi * 4 * 128 * 1024, [(1024, 128), (128 * 1024, 4), (1, 1024)])]
    CAST[7] = [(0, [(512, 8), (1, 512)], xwq_d, 0, [(512, 128), (128 * 512, 8), (1, 512)])]
    CAST[8] = [(0, [(1024, 4), (1, 1024)], xwo_d, 0, [(1024, 128), (128 * 1024, 4), (1, 1024)])]
    for k in range(11):
        items = []
        for sub in range(2):
            for m, wt in enumerate((wg_d, wu_d)):
                items.append((sub * 2048 + m * 1024, [(128, 8), (1, 128)], wt, (2 * k + sub) * 128, [(FF, 128), (128 * FF, 8), (1, 128)]))
        CAST[9 + k] = items
    for k in range(6):
        nj = 4 if k < 5 else 2
        CAST[20 + k] = [(0, [(1024, nj), (1, 1024)], wd_d, 4 * k * 128 * 1024, [(1024, 128), (128 * 1024, nj), (1, 1024)])]
    def emit_casts(gs):
        for g in gs:
            cast_group(g, CAST[g])
    emit_casts([3, 4, 2, 0, 1])

    MEMSET("dve", epsc, EPS); MEMSET("dve", halfpi, math.pi / 2); MEMSET("dve", onesf, 1.0)
    MEMSET("dve", V(onesb, 0, [(1, 128)]), 1.0)
    CP("dve", V(identb, 0, [(1, 128)]), ident_f); CP("dve", V(Ub, 0, [(1, 128)]), U_f)
    MEMSET("pool", V(VA, 0, [(1, NBLK * 8 * 66 + 64)]), 1.0)
    TT("dve", vcol("gk", 0, 64), vcol("gk", 0, 64), vcol("gq", 0, 64), ALU.mult)
    MEMSET("pool", V(Sst, 0, [(1, 512)]), 0.0); MEMSET("pool", V(Sb, 0, [(1, 512)]), 0.0); MEMSET("pool", V(halo, 0, [(1, 44)]), 0.0)
    posf = V(WF, 0, [(1, 32)]); ang = V(WF, 32, [(1, 512)]); uu = V(WF, 544, [(1, 512)]); kf_ = V(WF, 1056, [(1, 512)]); s4 = V(WF, 1568, [(1, 512)]); c4 = V(WF, 2080, [(1, 480)])
    c4 = V(COS, 0, [(1, 512)])
    s4 = V(SIN, 0, [(1, 512)])
    CP("dve", posf, V(posi, 0, [(1, 32)]))
    TT("dve", V(WF, 32, [(16, 32), (1, 16)]), V(WF, 0, [(1, 32), (0, 16)]), V(cst, 256, [(0, 32), (1, 16)]), ALU.mult)
    TS("dve", uu, ang, 1.0 / (2 * math.pi), None, ALU.mult)
    CP("dve", V(WF, 2080, [(1, 512)]).bitcast(I32), uu)
    CP("dve", kf_, V(WF, 2080, [(1, 512)]).bitcast(I32))
    STT("dve", uu, kf_, -2 * math.pi, ang, ALU.mult, ALU.add)
    ACT(s4, uu, AF.Sin, scale=0.25)
    ACT(c4, uu, AF.Sin, scale=0.25, bias=halfpi)
    sh = V(WF, 1056, [(1, 512)]); ch = V(WF, 1568, [(1, 512)])
    TT("dve", sh, s4, c4, ALU.mult)
    TS("dve", sh, sh, 2.0, None, ALU.mult)
    TT("dve", ch, s4, s4, ALU.mult)
    TS("dve", ch, ch, -2.0, 1.0, ALU.mult, ALU.add)
    TT("dve", s4, sh, ch, ALU.mult)
    TS("dve", s4, s4, 2.0, None, ALU.mult)
    TT("dve", c4, sh, sh, ALU.mult)
    TS("dve", c4, c4, -2.0, 1.0, ALU.mult, ALU.add)

    ring_pos = [0]
    GUSE = {3: [(512, 8), (1, 432)], 4: [(1, 2560)], 25: [(1, 2048)]}
    rsem = [S.new_sem("rs%d" % i) for i in range(R_SLOTS)]
    def load_granule(g):
        s = ring_pos[0] % R_SLOTS; ring_pos[0] += 1
        use = GUSE.get(g, [(1, GSZ)])
        S.dma("sp", V(ring, s * GSZ, use), dview(wsc, g * G, [(GSZ, 128)] + use), rsem[s])
        return s * GSZ

    def norm_pre(xb, nblk=NB, sc0=0):
        for j in range(nblk):
            ACT(V(WF, 0, [(1, D)]), V(xh, xb + j * D, [(1, D)]), AF.Square, accum_out=V(stt, sc0 + j, [(1, 1)]))
        rstd_from(V(stt, sc0, [(1, nblk)]), V(stt, sc0, [(1, nblk)]), D)
        xsv = [V(xs, 0, [(1, D)]), V(vtok, 0, [(1, D)])]
        for j in range(nblk):
            if j == 0:
                ACT(xsv[j], V(xh, xb + j * D, [(1, D)]), AF.Copy, scale=V(stt, sc0 + j, [(1, 1)]))
            else:
                TS("dve", xsv[j], V(xh, xb + j * D, [(1, D)]), V(stt, sc0 + j, [(1, 1)]), None, ALU.mult)
    def norm_post(gname, nblk=NB):
        xso = [xs, vtok]
        for j in range(nblk):
            pt = psT[j]
            for c in range(8):
                TR(V(pt, c * 128, [(1, 128)]), V(xso[j], c * 128, [(1, 128)]), sig=(c == 7))
        for j in range(nblk):
            TT("dve", V(nT, j * 128, [(T, 8), (1, 128)]), V(psT[j], 0, [(128, 8), (1, 128)]), V(vec, VO[gname], [(1, 8), (0, 128)]), ALU.mult)
    def norm_transpose_all(gname, nblk=NB, xb=0):
        norm_pre(xb, nblk); norm_post(gname, nblk)

    msem = S.new_sem("msem")
    S.dma("sp", V(xh, 0, [(D, 2), (1, D)]), dview(mem_d, 0, [(D, 128), (128 * D, 2), (1, D)]), msem)
    S.dma("pool", V(ring, 0, [(512, 8), (1, 512)]), dview(xwkv_d, 0, [(1024, 128), (128 * 1024, 8), (1, 512)]), semc[4])
    S.dma("pool", V(ring, GSZ, [(512, 8), (1, 512)]), dview(xwkv_d, 512, [(1024, 128), (128 * 1024, 8), (1, 512)]), semc[5])
    norm_transpose_all("nmem", 2, 0)
    for j in range(2):
        pk = bankA()
        for c in range(8):
            MM(V(pk, 0, [(1, 512)]), V(nT, c * T + j * 128, [(1, 128)]), V(ring, c * 512, [(1, 512)]), start=(c == 0), stop=(c == 7))
        pv = bankA()
        for c in range(8):
            MM(V(pv, 0, [(1, 512)]), V(nT, c * T + j * 128, [(1, 128)]), V(ring, GSZ + c * 512, [(1, 512)]), start=(c == 0), stop=(c == 7))
        CP("act", V(Vm, j * 512, [(1, 512)]), V(pv, 0, [(1, 512)]))
        sq = V(WF, 0, [(1, 512)])
        ACT(sq, V(pk, 0, [(1, 512)]), AF.Square)
        ss4 = V(stt, 4, [(1, 4)])
        RED(ss4, V(WF, 0, [(128, 4), (1, 128)]))
        rstd_from(ss4, ss4, 128, 4)
        kn = V(WF, 512, [(1, 512)])
        TT("dve", V(WF, 512, [(128, 4), (1, 128)]), V(pk, 0, [(128, 4), (1, 128)]), V(stt, 4, [(1, 4), (0, 128)]), ALU.mult)
        TT("pool", V(WB, 0, [(128, 4), (1, 128)]), V(WF, 512, [(128, 4), (1, 128)]), V(vec, VO["xkn"], [(0, 4), (1, 128)]), ALU.mult)
        pt = bankT()
        for h in range(4):
            TR(V(pt, h * 128, [(1, 128)]), V(WB, h * 128, [(1, 128)]), sig=(h == 3))
        CP("act", V(KmT, j * 128, [(256, 4), (1, 128)]), V(pt, 0, [(128, 4), (1, 128)]))

    QT0 = 0; MIX0 = 8 * T; XQT0 = 0; XOT0 = 4 * T; PTX0 = 8 * T
    SC_MLA = 96 ** -0.5; SC_XA = 128 ** -0.5
    xsem = [[S.new_sem("xsem%d_%d" % (p_, j)) for j in range(NB)] for p_ in range(2)]; osem = [S.new_sem("osem%d" % j) for j in range(NB)]
    out_toks = []

    import itertools
    def run_chains(chains):
        chains = list(chains)
        while chains:
            for c_ in list(chains):
                try:
                    next(c_)
                except StopIteration:
                    chains.remove(c_)

    prefetched = {}
    def get_granule(g):
        if g in prefetched:
            return prefetched.pop(g)
        return load_granule(g)

    for j in range(NB):
        S.dma("sp", V(xh, j * D, [(1, D)]), dview(x_d, j * 128 * D, [(D, 128), (1, D)]), xsem[0][j])

    uq_box = [None]

    def step_chains(chs, k):
        for _ in range(k):
            for c_ in list(chs):
                try:
                    next(c_)
                except StopIteration:
                    chs.remove(c_)

    def norm_trs():
        xso = [xs, vtok]
        for j in range(NB):
            for c in range(8):
                TR(V(psT[j], c * 128, [(1, 128)]), V(xso[j], c * 128, [(1, 128)]), sig=(c == 7))
        for j in range(NB):
            TT("dve", V(nT, j * 128, [(T, 8), (1, 128)]), V(psT[j], 0, [(128, 8), (1, 128)]), V(vec, VO["nm"], [(1, 8), (0, 128)]), ALU.mult)

    def proj_group(g):
        go = get_granule(g)
        gw = 432 if g == 3 else 512
        for j in range(NB):
            pp = bankB()
            for c in range(8):
                MM(V(pp, 0, [(1, gw)]), V(nT, c * T + j * 128, [(1, 128)]), V(ring, go + c * 512, [(1, gw)]), start=(c == 0), stop=(c == 7))
            if g == 2:
                ACT(V(Gt, j * 512, [(1, 512)]), V(pp, 0, [(1, 512)]), AF.Silu)
                TT("pool", V(Gt, j * 512, [(128, 4), (1, 128)]), V(Gt, j * 512, [(128, 4), (1, 128)]), V(vec, VO["gon"], [(0, 4), (1, 128)]), ALU.mult)
            elif g == 0:
                CP("act", V(qk, j * 512, [(1, 512)]), V(pp, 0, [(1, 512)]))
            elif g == 1:
                CP("dve", V(vtok, j * 512, [(1, 512)]), V(pp, 0, [(1, 512)]))
            else:
                CP("dve", V(r3, j * 432, [(1, 432)]), V(pp, 0, [(1, 432)]))

    def mla(j, c, tn):
        b = tn * NB + j; R3 = j * 432; fb = c * 1760; bb = c * 1536; sc = 16 + c * 24
        px = psT[c]
        def PF(f0, dims, p0=0, np_=128):
            bd = [(2 * s_, n_) for (s_, n_) in dims[:-1]] + [(1, 2 * dims[-1][1])]
            return V(px, 2 * f0, bd, p0, np_).bitcast(F32)
        PBv = lambda f0, dims, p0=0, np_=128: V(px, f0, dims, p0, np_)
        cq = V(r3, R3 + 16, [(1, 256)]); ckv = V(r3, R3 + 272, [(1, 128)]); kpe = V(r3, R3 + 400, [(1, 32)])
        st = lambda i, w=1: V(stt, sc + i, [(1, w)])
        ACT(V(WF, fb + 768, [(1, 256)]), cq, AF.Square, accum_out=st(0))
        ACT(V(WF, fb + 1024, [(1, 128)]), ckv, AF.Square, accum_out=st(1))
        ACT(V(WF, fb + 1632, [(1, 32)]), kpe, AF.Square, accum_out=st(10))
        yield
        ACT(st(0), st(0), AF.Ln, scale=1.0 / 256, bias=epsc)
        ACT(st(1), st(1), AF.Ln, scale=1.0 / 128, bias=epsc)
        ACT(st(0, 2), st(0, 2), AF.Exp, scale=-0.5)
        yield
        ACT(V(WB, bb, [(1, 256)]), cq, AF.Copy, scale=st(0))
        ACT(V(WB, bb + 256, [(1, 128)]), ckv, AF.Copy, scale=st(1))
        yield
        for cc in range(3):
            TR(PBv(cc * 128, [(1, 128)]), V(WB, bb + cc * 128, [(1, 128)]), sig=(cc == 2))
        yield
        cT = bb + 384
        TT("dve", V(WB, cT, [(128, 3), (1, 128)]), PBv(0, [(128, 3), (1, 128)]), V(vec, VO["qan"], [(1, 3), (0, 128)]), ALU.mult)
        yield
        uq_off = uq_box[0]
        qf = bb + 768
        for hf in range(2):
            for cc in range(2):
                MM(PF(0, [(1, 384)]), V(WB, cT + cc * 128, [(1, 128)]), V(ring, uq_off + cc * 768 + hf * 384, [(1, 384)]), start=(cc == 0), stop=(cc == 1))
            yield
            ACT(V(WF, fb + 768, [(1, 384)]), PF(0, [(1, 384)]), AF.Square)
            yield
            RED(st(2 + hf * 4, 4), V(WF, fb + 768, [(96, 4), (1, 96)]))
            yield
            ACT(st(2 + hf * 4, 4), st(2 + hf * 4, 4), AF.Ln, scale=1.0 / 96, bias=epsc)
            yield
            ACT(st(2 + hf * 4, 4), st(2 + hf * 4, 4), AF.Exp, scale=-0.5)
            yield
            TT("dve", V(WB, qf + hf * 384, [(96, 4), (1, 64)]), PF(0, [(96, 4), (1, 64)]), V(stt, sc + 2 + hf * 4, [(1, 4), (0, 64)]), ALU.mult)
            TT("dve", V(WF, fb + hf * 128, [(32, 4), (1, 32)]), PF(64, [(96, 4), (1, 32)]), V(stt, sc + 2 + hf * 4, [(1, 4), (0, 32)]), ALU.mult)
            yield
        TT("pool", V(WF, fb, [(32, 8), (1, 32)]), V(WF, fb, [(32, 8), (1, 32)]), V(vec, VO["gq"] + 64, [(0, 8), (1, 32)]), ALU.mult)
        yield
        cosb = V(COS, b * 16, [(0, 8), (1, 16)]); sinb = V(SIN, b * 16, [(0, 8), (1, 16)])
        x1 = V(WF, fb, [(32, 8), (1, 16)]); x2 = V(WF, fb + 16, [(32, 8), (1, 16)])
        tA = V(WF, fb + 1280, [(16, 8), (1, 16)]); tB = V(WF, fb + 1408, [(16, 8), (1, 16)])
        TT("pool", tA, x1, cosb, ALU.mult); TT("pool", tB, x2, sinb, ALU.mult)
        yield
        TT("pool", V(WB, qf + 64, [(96, 8), (1, 16)]), tA, tB, ALU.subtract)
        yield
        TT("pool", tA, x2, cosb, ALU.mult); TT("pool", tB, x1, sinb, ALU.mult)
        yield
        TT("pool", V(WB, qf + 80, [(96, 8), (1, 16)]), tA, tB, ALU.add)
        yield
        for h in range(8):
            TR(PBv(h * 128, [(1, 128)], 0, 96), V(WB, qf + h * 96, [(1, 96)]), sig=(h == 7))
        yield
        CP("act", V(U, QT0 + j * 128, [(T, 8), (1, 128)], 0, 96), PBv(0, [(128, 8), (1, 128)], 0, 96))
        kp = V(WF, fb + 1536, [(1, 32)])
        TT("pool", kp, kpe, vcol("gk", 64, 32), ALU.mult)
        yield
        c1 = V(COS, b * 16, [(1, 16)]); s1 = V(SIN, b * 16, [(1, 16)])
        k1 = V(WF, fb + 1536, [(1, 16)]); k2 = V(WF, fb + 1552, [(1, 16)])
        rA = V(WF, fb + 1568, [(1, 16)]); rB = V(WF, fb + 1584, [(1, 16)])
        kr = fb + 1600
        TT("pool", rA, k1, c1, ALU.mult); TT("pool", rB, k2, s1, ALU.mult)
        yield
        TT("pool", V(WF, kr, [(1, 16)]), rA, rB, ALU.subtract)
        yield
        TT("pool", rA, k2, c1, ALU.mult); TT("pool", rB, k1, s1, ALU.mult)
        yield
        TT("pool", V(WF, kr + 16, [(1, 16)]), rA, rB, ALU.add)
        yield
        for hf in range(2):
            MM(PF(0, [(1, 512)]), V(WB, cT + 256, [(1, 128)]), V(ring, uq_off + 1536 + hf * 512, [(1, 512)]))
            yield
            ACT(V(WF, fb + 768, [(1, 512)]), PF(0, [(1, 512)]), AF.Square)
            yield
            RED(st(11 + hf * 4, 4), V(WF, fb + 768, [(128, 4), (1, 64)]))
            CP("act", V(VA, (b * 8 + hf * 4) * 66, [(66, 4), (1, 64)]), PF(64, [(128, 4), (1, 64)]))
            yield
            TS("dve", st(11 + hf * 4, 4), st(11 + hf * 4, 4), st(10), None, ALU.add)
            yield
            ACT(st(11 + hf * 4, 4), st(11 + hf * 4, 4), AF.Ln, scale=1.0 / 96, bias=epsc)
            yield
            ACT(st(11 + hf * 4, 4), st(11 + hf * 4, 4), AF.Exp, scale=-0.5)
            yield
            TT("dve", V(WF, fb + hf * 384, [(96, 4), (1, 64)]), PF(0, [(128, 4), (1, 64)]), V(stt, sc + 11 + hf * 4, [(1, 4), (0, 64)]), ALU.mult)
            yield
        kfb = bb + 768
        TT("dve", V(WB, kfb, [(96, 8), (1, 64)]), V(WF, fb, [(96, 8), (1, 64)]), V(vec, VO["gk"], [(0, 8), (1, 64)]), ALU.mult)
        TT("pool", V(WB, kfb + 64, [(96, 8), (1, 32)]), V(WF, kr, [(0, 8), (1, 32)]), V(stt, sc + 11, [(1, 8), (0, 32)]), ALU.mult)
        yield
        for h in range(8):
            TR(PBv(h * 128, [(1, 128)], 0, 96), V(WB, kfb + h * 96, [(1, 96)]), sig=(h == 7))
        yield
        CP("act", V(KT, b * 128, [(SEQ, 8), (1, 128)], 0, 96), PBv(0, [(128, 8), (1, 128)], 0, 96))

    norm_pre(0, NB, 2)
    norm_trs()
    proj_group(3)
    emit_casts([5, 6, 7, 8])
    chains0 = [mla(j, j, 0) for j in range(NB)]
    proj_group(2); step_chains(chains0, 3)
    proj_group(0)
    uq_box[0] = get_granule(4)
    step_chains(chains0, 3)
    proj_group(1)
    while chains0:
        step_chains(chains0, 1)
    emit_casts(list(range(9, 20)))

    for ti in range(NTILE):
        t0 = ti * T
        xb = (ti % 2) * NB * D
        if ti + 1 < NTILE:
            xb2 = ((ti + 1) % 2) * NB * D
            for j in range(NB):
                S.dma("sp", V(xh, xb2 + j * D, [(1, D)]), dview(x_d, (t0 + T + j * 128) * D, [(D, 128), (1, D)]), xsem[(ti + 1) % 2][j])
        psG = psA[3]
        def gla(j):
            R3 = j * 432
            CP("dve", V(WB, 0, [(1, 16)]), V(r3, R3, [(1, 16)]))
            yield
            pt = bankT()
            yield
            yield
            TR(V(pt, 0, [(1, 128)], 0, 16), V(WB, 0, [(1, 16)]))
            yield
            CP("dve", V(WB, 16, [(1, 128)], 0, 16), V(pt, 0, [(1, 128)], 0, 16))
            yield
            yield
            yield
            MM(V(psG, 0, [(1, 256)]), V(WB, 16, [(1, 128)], 0, 16), V(w2b, 0, [(1, 256)], 0, 16))
            yield
            z = V(WF, 0, [(1, 256)]); l_ = V(WF, 256, [(1, 256)])
            TT("dve", z, V(psG, 0, [(1, 256)]), vcol("gb", 0, 256), ALU.add)
            yield
            ACT(z, z, AF.Exp, scale=-1.0)
            yield
            ACT(l_, z, AF.Ln, bias=1.0)
            yield
            yield
            yield
            MM(V(psG, 0, [(1, 256)]), U_f, l_)
            for h in range(4):
                MM(V(psG, 256 + h, [(1, 1)], 0, 64), V(WF, 256 + h * 64, [(1, 64)]), onesf, sig=(h == 3))
            yield
            Eq = V(WF, 512, [(1, 256)]); Ek = V(WF, 768, [(1, 256)])
            ACT(Eq, V(psG, 0, [(1, 256)]), AF.Exp, scale=-1.0 / 16)
            ACT(Ek, V(psG, 0, [(1, 256)]), AF.Exp, scale=1.0 / 16)
            ACT(V(stt, 8, [(1, 4)], 0, 64), V(psG, 256, [(1, 4)], 0, 64), AF.Exp, scale=-1.0 / 16)
            yield
            qd = V(WB, 256, [(1, 256)]); ki = V(WB, 512, [(1, 256)])
            TT("pool", qd, V(qk, j * 512, [(1, 256)]), Eq, ALU.mult)
            TT("pool", ki, V(qk, j * 512 + 256, [(1, 256)]), Ek, ALU.mult)
            yield
            pt = bankT()
            yield
            yield
            for h in range(8):
                TR(V(pt, h * 128, [(1, 128)], 0, 64), V(WB, 256 + h * 64, [(1, 64)]), sig=(h == 7))
            yield
            qkT = 768
            CP("dve", V(WB, qkT, [(1, 1024)], 0, 64), V(pt, 0, [(1, 1024)], 0, 64))
            yield
            yield
            yield
            for h in range(4):
                MM(V(psG, h * 128, [(1, 128)]), V(WB, qkT + (4 + h) * 128, [(1, 128)], 0, 64), V(WB, qkT + h * 128, [(1, 128)], 0, 64), sig=(h == 3))
            yield
            ATm = 1792
            STT("dve", V(WB, ATm, [(128, 4), (1, 128)]), V(psG, 0, [(128, 4), (1, 128)]), 0.125, V(Ub, 0, [(0, 4), (1, 128)]), ALU.mult, ALU.mult)
            yield
            yield
            yield
            for h in range(4):
                MM(V(psG, h * 128, [(1, 128)]), V(WB, ATm + h * 128, [(1, 128)]), V(vtok, j * 512 + h * 128, [(1, 128)]), start=True, stop=False, sig=False)
                MM(V(psG, h * 128, [(1, 128)]), V(WB, qkT + h * 128, [(1, 128)], 0, 64), V(Sb, h * 128, [(1, 128)], 0, 64), start=False, stop=True, sig=(h == 3))
            yield
            osq = V(WF, 1024, [(1, 512)])
            ACT(osq, V(psG, 0, [(1, 512)]), AF.Square)
            yield
            so = V(stt, 12, [(1, 4)])
            RED(so, V(WF, 1024, [(128, 4), (1, 128)]))
            yield
            rstd_from(so, so, 128, 4)
            yield
            TT("dve", V(WF, 1536, [(128, 4), (1, 128)]), V(psG, 0, [(128, 4), (1, 128)]), V(stt, 12, [(1, 4), (0, 128)]), ALU.mult)
            yield
            ogv = V(xs, 0, [(1, 512)])
            TT("pool", ogv, V(WF, 1536, [(1, 512)]), V(Gt, j * 512, [(1, 512)]), ALU.mult)
            yield
            yield
            for h in range(4):
                MM(V(psG, h * 128, [(1, 128)], 0, 64), V(WB, 512 + h * 64, [(1, 64)]), V(vtok, j * 512 + h * 128, [(1, 128)]), sig=(h == 3))
            yield
            TT("dve", V(Sst, 0, [(1, 512)], 0, 64), V(psG, 0, [(1, 512)], 0, 64), V(Sst, 0, [(1, 512)], 0, 64), ALU.add)
            pt = bankT()
            for c in range(4):
                TR(V(pt, c * 128, [(1, 128)]), V(xs, c * 128, [(1, 128)]), sig=(c == 3))
            yield
            TT("dve", V(Sst, 0, [(128, 4), (1, 128)], 0, 64), V(Sst, 0, [(128, 4), (1, 128)], 0, 64), V(stt, 8, [(1, 4), (0, 128)], 0, 64), ALU.mult)
            CP("dve", V(U, MIX0 + j * 128, [(T, 4), (1, 128)]), V(pt, 0, [(128, 4), (1, 128)]))
            yield
            TS("pool", V(Sb, 0, [(1, 512)], 0, 64), V(Sst, 0, [(1, 512)], 0, 64), 0.125, None, ALU.mult)
            yield

        gla_gen = itertools.chain(gla(0), gla(1)) if NB == 2 else itertools.chain(*[gla(j) for j in range(NB)])
        gla_done = [False]
        def gla_advance(k):
            for _ in range(k):
                if gla_done[0]:
                    return
                try:
                    next(gla_gen)
                except StopIteration:
                    gla_done[0] = True

        units = []
        for h in range(8):
            for kb in range(0, ti * NB, 2):
                units.append((h, "off", kb))
            units.append((h, "diag", ti * NB))
        AHEAD = 2
        pS_of = {}; pO_of = {}
        deferred = []

        def emit_S(ui):
            h, kind, kb = units[ui]
            pS = bankA(); pS_of[ui] = pS
            q_ = lambda c0, n: V(U, QT0 + h * T + c0, [(1, n)], 0, 96)
            k_ = lambda kbb: V(KT, h * SEQ + kbb * 128, [(1, 128)], 0, 96)
            if kind == "off":
                MM(V(pS, 0, [(1, T)]), k_(kb), q_(0, T), sig=False)
                MM(V(pS, T, [(1, T)]), k_(kb + 1), q_(0, T))
            else:
                MM(V(pS, 0, [(1, T)]), k_(kb), q_(0, T), sig=False)
                MM(V(pS, T, [(1, 128)]), k_(kb + 1), q_(128, 128))

        def emit_E(ui):
            h, kind, kb = units[ui]
            pS = pS_of[ui]; pto = (ui % 3) * 512
            n = 2 * T if kind == "off" else T + 128
            ACT(V(PT, pto, [(1, n)]), V(pS, 0, [(1, n)]), AF.Exp, scale=SC_MLA)
            if kind == "diag":
                TT("pool", V(PT, pto, [(T, 2), (1, 128)]), V(PT, pto, [(T, 2), (1, 128)]), V(Ub, 0, [(0, 2), (1, 128)]), ALU.mult)

        def emit_norm1(h, pO, ui):
            recrow = V(WF, 2048 + (h % 2) * T, [(1, T)], 64, 1)
            RECIP(recrow, V(pO, 0, [(1, T)], 64, 1))
            deferred.append([ui + 3, (lambda: emit_norm2(h, pO)), h])

        def emit_norm2(h, pO):
            recrow = V(WF, 2048 + (h % 2) * T, [(1, T)], 64, 1)
            pb = bankA()
            MM(V(pb, 0, [(1, T)], 0, 64), V(cst, 128 + 64, [(1, 64)], 64, 1), recrow)
            bcs = V(WF, 2048 + (h % 2) * T, [(1, T)], 0, 64)
            CP("dve", bcs, V(pb, 0, [(1, T)], 0, 64))
            TT("dve", V(U, MIX0 + (4 + h // 2) * T, [(1, T)], (h % 2) * 64, 64), V(pO, 0, [(1, T)], 0, 64), bcs, ALU.mult)

        def emit_PV(ui):
            h, kind, kb = units[ui]
            pto = (ui % 3) * 512
            first = (ui == 0) or (units[ui - 1][0] != h)
            if first:
                while any(d[2] <= h - 2 for d in deferred):
                    for d in list(deferred):
                        if d[2] <= h - 2:
                            deferred.remove(d); d[1]()
                pO_of[h] = bankB()
            pO = pO_of[h]
            va = lambda kbb: V(VA, (kbb * 8 + h) * 66, [(1, 128)])
            if kind == "off":
                MM(V(pO, 0, [(1, T)]), va(kb), V(PT, pto, [(1, T)]), start=first, stop=False, sig=False)
                MM(V(pO, 0, [(1, T)]), va(kb + 1), V(PT, pto + T, [(1, T)]), start=False, stop=False, sig=True)
            else:
                MM(V(pO, 0, [(1, T)]), va(kb), V(PT, pto, [(1, T)]), start=first, stop=False, sig=False)
                MM(V(pO, 128, [(1, 128)]), va(kb + 1), V(PT, pto + T, [(1, 128)]), start=False, stop=True, sig=True)
                deferred.append([ui + 2, (lambda hh=h, pp=pO, uu=ui: emit_norm1(hh, pp, uu + 2)), h])

        nu = len(units)
        gk_ = max(1, -(-86 // nu))
        for step in range(nu + AHEAD):
            if step < nu:
                emit_S(step); emit_E(step)
            if step - AHEAD >= 0:
                emit_PV(step - AHEAD)
            for d in list(deferred):
                if d[0] <= step - AHEAD:
                    deferred.remove(d); d[1]()
            gla_advance(gk_)
        while deferred:
            d = deferred.pop(0); d[1]()
        gla_advance(10 ** 6)

        if ti == 0:
            emit_casts(list(range(20, 26)))
        wo = [get_granule(5), get_granule(6)]
        for j in range(NB):
            for hf in range(2):
                pp = bankA()
                for c in range(8):
                    MM(V(pp, 0, [(1, 512)]), V(U, MIX0 + c * T + j * 128, [(1, 128)]), V(ring, wo[c // 4] + (c % 4) * 1024 + hf * 512, [(1, 512)]), start=(c == 0), stop=(c == 7))
                xv = V(xh, xb + j * D + hf * 512, [(1, 512)])
                TT("dve", xv, V(pp, 0, [(1, 512)]), xv, ALU.add)
        xq_off = get_granule(7)
        norm_transpose_all("nxa", NB, xb)
        def xq_chain(j):
            pq = psA[j]
            for c in range(8):
                MM(V(pq, 0, [(1, 512)]), V(nT, c * T + j * 128, [(1, 128)]), V(ring, xq_off + c * 512, [(1, 512)]), start=(c == 0), stop=(c == 7))
            yield
            ACT(V(WF, j * 1024, [(1, 512)]), V(pq, 0, [(1, 512)]), AF.Square)
            yield
            s4_ = V(stt, 4 + 4 * j, [(1, 4)])
            RED(s4_, V(WF, j * 1024, [(128, 4), (1, 128)]))
            yield
            ACT(s4_, s4_, AF.Ln, scale=1.0 / 128, bias=epsc)
            yield
            ACT(s4_, s4_, AF.Exp, scale=-0.5)
            yield
            TT("dve", V(WF, j * 1024 + 512, [(128, 4), (1, 128)]), V(pq, 0, [(128, 4), (1, 128)]), V(stt, 4 + 4 * j, [(1, 4), (0, 128)]), ALU.mult)
            yield
            TT("pool", V(WB, j * 512, [(128, 4), (1, 128)]), V(WF, j * 1024 + 512, [(128, 4), (1, 128)]), V(vec, VO["xqn"], [(0, 4), (1, 128)]), ALU.mult)
            yield
            pt = psT[j]
            for hh in range(4):
                TR(V(pt, hh * 128, [(1, 128)]), V(WB, j * 512 + hh * 128, [(1, 128)]), sig=(hh == 3))
            yield
            CP("act", V(U, XQT0 + j * 128, [(T, 4), (1, 128)]), V(pt, 0, [(128, 4), (1, 128)]))
        run_chains([xq_chain(j) for j in range(NB)])

        def xS(hh):
            base = (hh % 2) * 2 * T
            for kb in range(2):
                pS = bankA()
                MM(V(pS, 0, [(1, T)]), V(KmT, hh * 256 + kb * 128, [(1, 128)]), V(U, XQT0 + hh * T, [(1, T)]))
                ACT(V(U, PTX0 + base + kb * T, [(1, T)]), V(pS, 0, [(1, T)]), AF.Exp, scale=SC_XA)
        def xPV(hh):
            base = (hh % 2) * 2 * T
            pb_ = psB[hh % 2]
            for kb in range(2):
                MM(V(pb_, 0, [(1, T)]), V(Vm, kb * 512 + hh * 128, [(1, 128)]), V(U, PTX0 + base + kb * T, [(1, T)]), start=(kb == 0), stop=(kb == 1))
            for kb in range(2):
                MM(V(pb_, T, [(1, T)]), V(onesb, 0, [(1, 128)]), V(U, PTX0 + base + kb * T, [(1, T)]), start=(kb == 0), stop=(kb == 1))
            rd = V(WF, 2048 + (hh % 2) * T, [(1, T)])
            ACT(rd, V(pb_, T, [(1, T)]), AF.Ln)
            ACT(rd, rd, AF.Exp, scale=-1.0)
            TT("dve", V(U, XOT0 + hh * T, [(1, T)]), V(pb_, 0, [(1, T)]), rd, ALU.mult)
        xS(0); xS(1); xPV(0); xS(2); xPV(1); xS(3); xPV(2); xPV(3)
        xo_off = get_granule(8)
        for j in range(NB):
            for hf in range(2):
                pp = bankA()
                for c in range(4):
                    MM(V(pp, 0, [(1, 512)]), V(U, XOT0 + c * T + j * 128, [(1, 128)]), V(ring, xo_off + c * 1024 + hf * 512, [(1, 512)]), start=(c == 0), stop=(c == 3))
                xv = V(xh, xb + j * D + hf * 512, [(1, 512)])
                TT("dve", xv, V(pp, 0, [(1, 512)]), xv, ALU.add)
        norm_transpose_all("nffn", NB, xb)
        for fc in range(NFC):
            if fc % 2 == 0:
                gu = get_granule(9 + fc // 2)
            sub = fc % 2
            pg = psA[0] if fc % 2 == 0 else psA[2]
            pu = psA[1] if fc % 2 == 0 else psA[3]
            for c in range(8):
                MM(V(pg, 0, [(1, T)]), V(ring, gu + sub * 2048 + c * 128, [(1, 128)]), V(nT, c * T, [(1, T)]), start=(c == 0), stop=(c == 7))
            for c in range(8):
                MM(V(pu, 0, [(1, T)]), V(ring, gu + sub * 2048 + 1024 + c * 128, [(1, 128)]), V(nT, c * T, [(1, T)]), start=(c == 0), stop=(c == 7))
            sl4 = fc % 4
            gb_ = sl4 * (T + 2)
            ub = V(WB, sl4 * T, [(1, T)])
            CP("pool", V(WF, gb_, [(1, 2)]), V(halo, fc * 2, [(1, 2)]))
            CP("act", V(WF, gb_ + 2, [(1, T)]), V(pg, 0, [(1, T)]))
            CP("act", ub, V(pu, 0, [(1, T)]))
            CP("pool", V(halo, fc * 2, [(1, 2)]), V(WF, gb_ + T, [(1, 2)]))
            tc_ = V(WF, 1032 + sl4 * T, [(1, T)])
            cw = lambda i: V(vec, VO["cw"] + fc * 3 + i, [(1, 1)])
            TS("pool", tc_, V(WF, gb_ + 2, [(1, T)]), cw(2), V(vec, VO["cb"] + fc, [(1, 1)]), ALU.mult, ALU.add)
            STT("dve", tc_, V(WF, gb_ + 1, [(1, T)]), cw(1), tc_, ALU.mult, ALU.add)
            STT("dve", tc_, V(WF, gb_, [(1, T)]), cw(0), tc_, ALU.mult, ALU.add)
            ACT(tc_, tc_, AF.Silu)
            TT("pool", V(U, fc * T, [(1, T)]), tc_, ub, ALU.mult)
        nxt = ti + 1 < NTILE
        if nxt:
            norm_pre(((ti + 1) % 2) * NB * D, NB, 2)
        def dn_group(k):
            dn = get_granule(20 + k)
            nj = 4 if k < 5 else 2
            for jj in range(nj):
                fc = 4 * k + jj
                for j in range(NB):
                    for hf in range(2):
                        MM(V(psA[j * 2 + hf], 0, [(1, 512)]), V(U, fc * T + j * 128, [(1, 128)]), V(ring, dn + jj * 1024 + hf * 512, [(1, 512)]), start=(fc == 0), stop=(fc == NFC - 1), sig=(j == NB - 1 and hf == 1))
        if nxt:
            dn_group(0); dn_group(1)
            norm_trs()
            proj_group(3)
            chains = [mla(j, j, ti + 1) for j in range(NB)]
            dn_group(2); step_chains(chains, 3)
            proj_group(2); step_chains(chains, 2)
            dn_group(3)
            proj_group(0)
            uq_box[0] = get_granule(4)
            step_chains(chains, 8)
            dn_group(4); step_chains(chains, 8)
            proj_group(1)
            while chains:
                step_chains(chains, 1)
            dn_group(5)
        else:
            for k in range(6):
                dn_group(k)
        for j in range(NB):
            for hf in range(2):
                xv = V(xh, xb + j * D + hf * 512, [(1, 512)])
                TT("dve", xv, V(psA[j * 2 + hf], 0, [(1, 512)]), xv, ALU.add)
            out_toks.append(S.dma("sp", dview(out_d, (t0 + j * 128) * D, [(D, 128), (1, D)]), V(xh, xb + j * D, [(1, D)]), osem[j]))
    S.wait_all("sp", [[o_, S.dma_cum[o_]] for o_ in osem])
    return nc, S


_CACHE = {}

def kernel(x, mem, positions, norm_mix, w_in, gla_gate_w2, gla_gate_b, gla_out_norm,
           mla_q_a_norm, mla_w_uq, mla_kv_a_norm, mla_w_ukv, mla_q_norm, mla_k_norm, w_out,
           norm_xa, norm_mem, xa_w_q, xa_w_kv, xa_q_norm, xa_k_norm, xa_w_o,
           norm_ffn, ffn_w_gate, ffn_w_up, ffn_conv_w, ffn_conv_b, ffn_w_down):
    f = lambda a: np.ascontiguousarray(np.asarray(a, dtype=np.float32))
    x = f(x); mem = f(mem); positions = np.asarray(positions).astype(np.int32)
    fm = lambda v, c: f(v).reshape(c, 128).T
    rep = lambda v: np.broadcast_to(f(v).reshape(1, -1), (128, f(v).size))
    cols = [fm(norm_mix[0], 8), fm(norm_xa[0], 8), fm(norm_ffn[0], 8), fm(norm_mem[0], 8), fm(mla_q_a_norm[0], 2), fm(mla_kv_a_norm[0], 1),
            rep(gla_gate_b[0]), rep(gla_out_norm[0]), rep(mla_q_norm[0]), rep(mla_k_norm[0]), rep(xa_q_norm[0]), rep(xa_k_norm[0]),
            f(ffn_conv_w[0]).reshape(3, NFC, 128).transpose(2, 1, 0).reshape(128, 66), fm(ffn_conv_b[0], NFC)]
    vec = np.ascontiguousarray(np.concatenate(cols, axis=1).astype(np.float32))
    assert vec.shape == (128, NV)
    ident = np.eye(128, dtype=np.float32)
    Utri = np.triu(np.ones((128, 128), dtype=np.float32))
    inv = (10000.0 ** (-np.arange(16, dtype=np.float32) / 16)).astype(np.float32)
    cst = np.ascontiguousarray(np.concatenate([ident, Utri, np.broadcast_to(inv[None, :], (128, 16))], axis=1).astype(np.float32))
    if "nc" not in _CACHE:
        _CACHE["nc"] = build_program()[0]
    nc = _CACHE["nc"]
    shared = {"cst": cst, "vec": vec, "w_in": f(w_in[0]), "w2": f(gla_gate_w2[0]), "w_uq": f(mla_w_uq[0]), "w_ukv": f(mla_w_ukv[0]),
              "w_out": f(w_out[0]), "xa_w_q": f(xa_w_q[0]), "xa_w_kv": f(xa_w_kv[0]), "xa_w_o": f(xa_w_o[0]),
              "w_gate": f(ffn_w_gate[0]), "w_up": f(ffn_w_up[0]), "w_down": f(ffn_w_down[0])}
    in_maps = []
    for c in range(8):
        m = dict(shared)
        m["x"] = x[c]; m["mem"] = mem[c]
        m["pos"] = np.ascontiguousarray(positions[c].reshape(32, 128).T)
        in_maps.append(m)
    res = run_bass_kernel_spmd(nc, in_maps, core_ids=list(range(8)))
    return np.stack([np.asarray(r["out"]).reshape(SEQ, D) for r in res.results], axis=0).astype(np.float32)
```

```python
import numpy as np
import concourse.bass as bass
import concourse.mybir as mybir

F32 = mybir.dt.float32
BF = mybir.dt.bfloat16
I32 = mybir.dt.int32
ALU = mybir.AluOpType
AF = mybir.ActivationFunctionType
AX = mybir.AxisListType


def _prod(xs):
    r = 1
    for v in xs:
        r *= int(v)
    return r


class Sched:
    def __init__(self, nc):
        self.nc = nc
        self.eng = dict(pe=nc.tensor, act=nc.scalar, dve=nc.vector, pool=nc.gpsimd, sp=nc.sync)
        self.sem = {}
        self.cnt = {}
        for e in self.eng:
            self.sem[e] = nc.alloc_semaphore("cs_" + e)
            self.cnt[e] = 0
        self.semh = {("cs_" + e): self.sem[e] for e in self.eng}
        self.seen = {e: {} for e in self.eng}
        self.hist = {}
        self.dma_cum = {}
        self.n_wait = 0
        self.n_ops = 0

    def new_sem(self, name):
        h = self.nc.alloc_semaphore(name)
        self.semh[name] = h
        self.dma_cum[name] = 0
        return name

    def _acc(self, ap):
        t = ap.tensor
        name = t.name
        space = str(ap.space)
        pat = ap.ap
        off = int(ap.offset)
        if "PSUM" in space:
            return (name, 0, 1 << 30, 0, 128, True)
        if "SB" in space:
            psz = _prod(t.shape[1:])
            p0 = off // psz
            f0 = off % psz
            npart = pat[0][1]
            span = sum((c - 1) * abs(s) for s, c in pat[1:]) + 1
            return (name, f0, f0 + span, p0, p0 + npart, False)
        span = sum((c - 1) * abs(s) for s, c in pat) + 1
        return (name, off, off + span, 0, 1, False)

    def _deps(self, e, is_dma, accs, skip_tok=None):
        need = {}
        for (key, lo, hi, plo, phi, excl), w in accs:
            lst = self.hist.get(key)
            if not lst:
                continue
            for h in lst:
                hlo, hhi, hplo, hphi, he, hdma, tok, hw = h
                if hhi <= lo or hi <= hlo or hphi <= plo or phi <= hplo:
                    continue
                if skip_tok is not None and tok is skip_tok:
                    continue
                if he == e and not is_dma and not hdma:
                    if e == "pe":
                        continue
                    if not (hw or w):
                        continue
                else:
                    if not (hw or w or excl):
                        continue
                sn, val = tok[0], tok[1]
                assert val is not None, "unsealed dma token used"
                if need.get(sn, 0) < val:
                    need[sn] = val
        return need

    def _record(self, e, is_dma, accs, tok):
        for (key, lo, hi, plo, phi, excl), w in accs:
            lst = self.hist.setdefault(key, [])
            keep = []
            for h in lst:
                hlo, hhi, hplo, hphi, he, hdma, htok, hw = h
                contained = lo <= hlo and hhi <= hi and plo <= hplo and hphi <= phi
                if contained:
                    if w:
                        continue
                    if excl and (he != e or hdma or is_dma):
                        continue
                    if (not hw) and he == e and not is_dma and not hdma:
                        continue
                keep.append(h)
            keep.append((lo, hi, plo, phi, e, is_dma, tok, w))
            self.hist[key] = keep

    def _emit_waits(self, e, need):
        eng = self.eng[e]
        seen = self.seen[e]
        for sn, val in need.items():
            if seen.get(sn, 0) < val:
                eng.wait_ge(self.semh[sn], val)
                seen[sn] = val
                self.n_wait += 1

    def op(self, e, fn, kwargs, reads, writes, sig=True):
        accs = [(self._acc(a), False) for a in reads] + [(self._acc(a), True) for a in writes]
        need = self._deps(e, False, accs)
        self._emit_waits(e, need)
        ins = fn(**kwargs)
        if sig:
            self.cnt[e] += 1
            ins.then_inc(self.sem[e], 1)
            tok = ("cs_" + e, self.cnt[e])
        else:
            tok = ("cs_" + e, self.cnt[e] + 1)
        self._record(e, False, accs, tok)
        self.n_ops += 1
        return ins

    def dma(self, e, out, in_, sem, tok=None, **kw):
        accs = [(self._acc(in_), False), (self._acc(out), True)]
        need = self._deps(e, True, accs, skip_tok=tok)
        self._emit_waits(e, need)
        ins = self.eng[e].dma_start(out=out, in_=in_, **kw)
        ins.then_inc(self.semh[sem], 16)
        self.dma_cum[sem] += 16
        if tok is None:
            tok = [sem, self.dma_cum[sem]]
        self._record(e, True, accs, tok)
        self.n_ops += 1
        return tok

    def group_tok(self, sem):
        return [sem, None]

    def seal(self, tok):
        tok[1] = self.dma_cum[tok[0]]

    def wait_all(self, e, toks):
        need = {}
        for sn, val in toks:
            if need.get(sn, 0) < val:
                need[sn] = val
        self._emit_waits(e, need)


def view(t, p0, npart, f0, dims):
    psz = _prod(t.shape[1:])
    return bass.AP(t, p0 * psz + f0, [[psz, npart]] + [[int(s), int(c)] for s, c in dims])


def dview(t, off, dims):
    return bass.AP(t, int(off), [[int(s), int(c)] for s, c in dims])


import math
from concourse.bass_utils import run_bass_kernel_spmd

D = 1024; SEQ = 4096; NBLK = 32; T = 256; NB = T // 128; NTILE = SEQ // T
FF = 2816; NFC = 22; EPS = 1e-6
GSZ = 4096
R_SLOTS = 3
VO = {}
def _vo():
    o = 0
    for n, w in [("nm", 8), ("nxa", 8), ("nffn", 8), ("nmem", 8), ("qan", 2), ("kvan", 1), ("gb", 256), ("gon", 128),
                 ("gq", 96), ("gk", 96), ("xqn", 128), ("xkn", 128), ("cw", 66), ("cb", 22)]:
        VO[n] = o; o += w
    return o
NV = _vo()


def build_program(dbg=False):
    nc = bass.Bass("TRN2", target_bir_lowering=False)
    S = Sched(nc)
    dt_in = lambda n, shp, dt=F32: nc.dram_tensor(n, shp, dt, kind="ExternalInput")
    x_d = dt_in("x", [SEQ, D]); mem_d = dt_in("mem", [256, D]); pos_d = dt_in("pos", [128, 32], I32)
    cst_d = dt_in("cst", [128, 272]); vec_d = dt_in("vec", [128, NV])
    w_in_d = dt_in("w_in", [D, 1968]); w2_d = dt_in("w2", [16, 256]); w_uq_d = dt_in("w_uq", [256, 768]); w_ukv_d = dt_in("w_ukv", [128, 1024])
    w_out_d = dt_in("w_out", [D, D]); xwq_d = dt_in("xa_w_q", [D, 512]); xwkv_d = dt_in("xa_w_kv", [D, 1024]); xwo_d = dt_in("xa_w_o", [512, D])
    wg_d = dt_in("w_gate", [D, FF]); wu_d = dt_in("w_up", [D, FF]); wd_d = dt_in("w_down", [FF, D])
    out_d = nc.dram_tensor("out", [SEQ, D], F32, kind="ExternalOutput")
    NG = 26
    wsc = nc.dram_tensor("wsc", [NG * 128, GSZ], BF, kind="Internal")

    def sb(name, n, dt):
        return nc.alloc_sbuf_tensor("s_" + name, [128, n], dt)
    KT = sb("KT", 8 * SEQ, BF); VA = sb("VA", NBLK * 8 * 66 + 64, BF)
    cst = sb("cst", 272, F32); identb = sb("identb", 128, BF); Ub = sb("Ub", 128, BF); onesb = sb("onesb", 128, BF); small = sb("small", 8, F32)
    vec = sb("vec", NV, F32); COS = sb("COS", 512, F32); SIN = sb("SIN", 512, F32); w2b = sb("w2b", 256, BF)
    KmT = sb("KmT", 4 * 256, BF); Vm = sb("Vm", 2 * 512, BF); Sst = sb("Sst", 512, F32); Sb = sb("Sb", 512, BF); halo = sb("halo", 44, F32)
    xh = sb("xh", 2 * NB * D, F32); xs = sb("xs", D, BF); nT = sb("nT", 8 * T, BF)
    qk = sb("qk", NB * 512, F32); vtok = sb("vtok", NB * 512, BF); Gt = sb("Gt", NB * 512, BF); r3 = sb("r3", NB * 432, F32)
    WF = sb("WF", 3520, F32)
    WB = sb("WB", 3072, BF)
    U = sb("U", NFC * T, BF)
    PT = sb("PT", 4 * 512, BF)
    stt = sb("stt", 64, F32)
    ring = sb("ring", R_SLOTS * GSZ, BF)
    posi = sb("posi", 32, I32)
    psA = [nc.alloc_psum_tensor("psA%d" % i, [128, 512], F32) for i in range(4)]
    psB = [nc.alloc_psum_tensor("psB%d" % i, [128, 512], F32) for i in range(2)]
    psT = [nc.alloc_psum_tensor("psT%d" % i, [128, 1024], BF) for i in range(2)]
    rot = {"A": 0, "B": 0, "T": 0}
    def bankA():
        rot["A"] += 1; return psA[rot["A"] % 3]
    def bankB():
        rot["B"] += 1; return psB[rot["B"] % 2]
    def bankT():
        rot["T"] += 1; return psT[rot["T"] % 2]

    V = lambda t, f0, dims, p0=0, np_=128: view(t, p0, np_, f0, dims)
    isap = lambda a: isinstance(a, bass.AP)
    E = S.eng

    def ACT(out, in_, func, **kw):
        reads = [in_] + [v for k, v in kw.items() if isap(v) and k != "accum_out"]
        writes = [out] + ([kw["accum_out"]] if "accum_out" in kw else [])
        S.op("act", nc.scalar.activation, dict(out=out, in_=in_, func=func, **kw), reads, writes)
    def TT(e, out, in0, in1, op):
        S.op(e, E[e].tensor_tensor, dict(out=out, in0=in0, in1=in1, op=op), [in0, in1], [out])
    def TS(e, out, in0, s1, s2, op0, op1=None):
        kw = dict(out=out, in0=in0, scalar1=s1, scalar2=s2, op0=op0)
        if op1 is not None: kw["op1"] = op1
        S.op(e, E[e].tensor_scalar, kw, [in0] + [a for a in (s1, s2) if isap(a)], [out])
    def STT(e, out, in0, sc, in1, op0, op1):
        S.op(e, E[e].scalar_tensor_tensor, dict(out=out, in0=in0, scalar=sc, in1=in1, op0=op0, op1=op1), [in0, in1] + ([sc] if isap(sc) else []), [out])
    def CP(e, out, in_):
        if e == "act":
            S.op(e, nc.scalar.copy, dict(out=out, in_=in_), [in_], [out])
        else:
            S.op(e, E[e].tensor_copy, dict(out=out, in_=in_), [in_], [out])
    def RED(out, in_):
        S.op("dve", nc.vector.tensor_reduce, dict(out=out, in_=in_, axis=AX.X, op=ALU.add), [in_], [out])
    def RECIP(out, in_):
        S.op("dve", nc.vector.reciprocal, dict(out=out, in_=in_), [in_], [out])
    def MEMSET(e, ap, c):
        S.op(e, E[e].memset, dict(ap=ap, constant=c), [], [ap])
    def MM(out, lhsT, rhs, start=True, stop=True, sig=None, **kw):
        S.op("pe", nc.tensor.matmul, dict(out=out, lhsT=lhsT, rhs=rhs, start=start, stop=stop, **kw), [lhsT, rhs], [out], sig=(stop if sig is None else sig))
    def TR(out, in_, sig=True):
        idn = V(identb, 0, [(1, 128)])
        S.op("pe", nc.tensor.transpose, dict(out=out, in_=in_, identity=idn), [in_, idn], [out], sig=sig)

    ident_f = V(cst, 0, [(1, 128)]); U_f = V(cst, 128, [(1, 128)])
    epsc = V(small, 0, [(1, 1)]); halfpi = V(small, 1, [(1, 1)]); onesf = V(small, 2, [(1, 1)])
    vcol = lambda n, i=0, w=1: V(vec, VO[n] + i, [(1, w)])

    def rstd_from(ss, out, n, k=1):
        ACT(out, ss, AF.Ln, scale=1.0 / n, bias=epsc)
        ACT(out, out, AF.Exp, scale=-0.5)

    semc = [S.new_sem("pro%d" % i) for i in range(6)]
    S.dma("sp", V(cst, 0, [(1, 272)]), dview(cst_d, 0, [(272, 128), (1, 272)]), semc[0])
    S.dma("sp", V(vec, 0, [(1, NV)]), dview(vec_d, 0, [(NV, 128), (1, NV)]), semc[1])
    S.dma("sp", V(posi, 0, [(1, 32)]), dview(pos_d, 0, [(32, 128), (1, 32)]), semc[2])
    S.dma("pool", V(w2b, 0, [(1, 256)], 0, 16), dview(w2_d, 0, [(256, 16), (1, 256)]), semc[3])
    gsem = [S.new_sem("gs%d" % g) for g in range(NG)]
    G = 128 * GSZ
    def cast(g, dst_off, dst_dims, src_t, src_off, src_dims, tok):
        S.dma("pool", dview(wsc, g * G + dst_off, [(GSZ, 128)] + dst_dims), dview(src_t, src_off, src_dims), gsem[g], tok=tok)
    def cast_group(g, items):
        tok = S.group_tok(gsem[g])
        for it in items:
            cast(g, *it, tok)
        S.seal(tok)
    WIN_ORDER = [3, 2, 0, 1]
    CAST = {}
    CAST[2] = [(0, [(512, 8), (1, 512)], w_in_d, 1040, [(1968, 128), (128 * 1968, 8), (1, 512)])]
    CAST[0] = [(0, [(512, 8), (1, 512)], w_in_d, 0, [(1968, 128), (128 * 1968, 8), (1, 512)])]
    CAST[1] = [(0, [(512, 8), (1, 512)], w_in_d, 512, [(1968, 128), (128 * 1968, 8), (1, 512)])]
    CAST[3] = [(0, [(512, 8), (1, 16)], w_in_d, 1024, [(1968, 128), (128 * 1968, 8), (1, 16)]),
               (16, [(512, 8), (1, 416)], w_in_d, 1552, [(1968, 128), (128 * 1968, 8), (1, 416)])]
    CAST[4] = [(0, [(768, 2), (1, 768)], w_uq_d, 0, [(768, 128), (128 * 768, 2), (1, 768)]),
               (1536, [(1, 1024)], w_ukv_d, 0, [(1024, 128), (1, 1024)])]
    for gi in range(2):
        CAST[5 + gi] = [(0, [(1024, 4), (1, 1024)], w_out_d, gi * 4 * 128 * 1024, [(1024, 128), (128 * 1024, 4), (1, 1024)])]
    CAST[7] = [(0, [(512, 8), (1, 512)], xwq_d, 0, [(512, 128), (128 * 512, 8), (1, 512)])]
    CAST[8] = [(0, [(1024, 4), (1, 1024)], xwo_d, 0, [(1024, 128), (128 * 1024, 4), (1, 1024)])]
    for k in range(11):
        items = []
        for sub in range(2):
            for m, wt in enumerate((wg_d, wu_d)):
                items.append((sub * 2048 + m * 1024, [(128, 8), (1, 128)], wt, (2 * k + sub) * 128, [(FF, 128), (128 * FF, 8), (1, 128)]))
        CAST[9 + k] = items
    for k in range(6):
        nj = 4 if k < 5 else 2
        CAST[20 + k] = [(0, [(1024, nj), (1, 1024)], wd_d, 4 * k * 128 * 1024, [(1024, 128), (128 * 1024, nj), (1, 1024)])]
    def emit_casts(gs):
        for g in gs:
            cast_group(g, CAST[g])
    emit_casts([3, 4, 2, 0, 1])

    MEMSET("dve", epsc, EPS); MEMSET("dve", halfpi, math.pi / 2); MEMSET("dve", onesf, 1.0)
    MEMSET("dve", V(onesb, 0, [(1, 128)]), 1.0)
    CP("dve", V(identb, 0, [(1, 128)]), ident_f); CP("dve", V(Ub, 0, [(1, 128)]), U_f)
    MEMSET("pool", V(VA, 0, [(1, NBLK * 8 * 66 + 64)]), 1.0)
    TT("dve", vcol("gk", 0, 64), vcol("gk", 0, 64), vcol("gq", 0, 64), ALU.mult)
    MEMSET("pool", V(Sst, 0, [(1, 512)]), 0.0); MEMSET("pool", V(Sb, 0, [(1, 512)]), 0.0); MEMSET("pool", V(halo, 0, [(1, 44)]), 0.0)
    posf = V(WF, 0, [(1, 32)]); ang = V(WF, 32, [(1, 512)]); uu = V(WF, 544, [(1, 512)]); kf_ = V(WF, 1056, [(1, 512)]); s4 = V(WF, 1568, [(1, 512)]); c4 = V(WF, 2080, [(1, 480)])
    c4 = V(COS, 0, [(1, 512)])
    s4 = V(SIN, 0, [(1, 512)])
    CP("dve", posf, V(posi, 0, [(1, 32)]))
    TT("dve", V(WF, 32, [(16, 32), (1, 16)]), V(WF, 0, [(1, 32), (0, 16)]), V(cst, 256, [(0, 32), (1, 16)]), ALU.mult)
    TS("dve", uu, ang, 1.0 / (2 * math.pi), None, ALU.mult)
    CP("dve", V(WF, 2080, [(1, 512)]).bitcast(I32), uu)
    CP("dve", kf_, V(WF, 2080, [(1, 512)]).bitcast(I32))
    STT("dve", uu, kf_, -2 * math.pi, ang, ALU.mult, ALU.add)
    ACT(s4, uu, AF.Sin, scale=0.25)
    ACT(c4, uu, AF.Sin, scale=0.25, bias=halfpi)
    sh = V(WF, 1056, [(1, 512)]); ch = V(WF, 1568, [(1, 512)])
    TT("dve", sh, s4, c4, ALU.mult)
    TS("dve", sh, sh, 2.0, None, ALU.mult)
    TT("dve", ch, s4, s4, ALU.mult)
    TS("dve", ch, ch, -2.0, 1.0, ALU.mult, ALU.add)
    TT("dve", s4, sh, ch, ALU.mult)
    TS("dve", s4, s4, 2.0, None, ALU.mult)
    TT("dve", c4, sh, sh, ALU.mult)
    TS("dve", c4, c4, -2.0, 1.0, ALU.mult, ALU.add)

    ring_pos = [0]
    GUSE = {3: [(512, 8), (1, 432)], 4: [(1, 2560)], 25: [(1, 2048)]}
    rsem = [S.new_sem("rs%d" % i) for i in range(R_SLOTS)]
    def load_granule(g):
        s = ring_pos[0] % R_SLOTS; ring_pos[0] += 1
        use = GUSE.get(g, [(1, GSZ)])
        S.dma("sp", V(ring, s * GSZ, use), dview(wsc, g * G, [(GSZ, 128)] + use), rsem[s])
        return s * GSZ

    def norm_pre(xb, nblk=NB, sc0=0):
        for j in range(nblk):
            ACT(V(WF, 0, [(1, D)]), V(xh, xb + j * D, [(1, D)]), AF.Square, accum_out=V(stt, sc0 + j, [(1, 1)]))
        rstd_from(V(stt, sc0, [(1, nblk)]), V(stt, sc0, [(1, nblk)]), D)
        xsv = [V(xs, 0, [(1, D)]), V(vtok, 0, [(1, D)])]
        for j in range(nblk):
            if j == 0:
                ACT(xsv[j], V(xh, xb + j * D, [(1, D)]), AF.Copy, scale=V(stt, sc0 + j, [(1, 1)]))
            else:
                TS("dve", xsv[j], V(xh, xb + j * D, [(1, D)]), V(stt, sc0 + j, [(1, 1)]), None, ALU.mult)
    def norm_post(gname, nblk=NB):
        xso = [xs, vtok]
        for j in range(nblk):
            pt = psT[j]
            for c in range(8):
                TR(V(pt, c * 128, [(1, 128)]), V(xso[j], c * 128, [(1, 128)]), sig=(c == 7))
        for j in range(nblk):
            TT("dve", V(nT, j * 128, [(T, 8), (1, 128)]), V(psT[j], 0, [(128, 8), (1, 128)]), V(vec, VO[gname], [(1, 8), (0, 128)]), ALU.mult)
    def norm_transpose_all(gname, nblk=NB, xb=0):
        norm_pre(xb, nblk); norm_post(gname, nblk)

    msem = S.new_sem("msem")
    S.dma("sp", V(xh, 0, [(D, 2), (1, D)]), dview(mem_d, 0, [(D, 128), (128 * D, 2), (1, D)]), msem)
    S.dma("pool", V(ring, 0, [(512, 8), (1, 512)]), dview(xwkv_d, 0, [(1024, 128), (128 * 1024, 8), (1, 512)]), semc[4])
    S.dma("pool", V(ring, GSZ, [(512, 8), (1, 512)]), dview(xwkv_d, 512, [(1024, 128), (128 * 1024, 8), (1, 512)]), semc[5])
    norm_transpose_all("nmem", 2, 0)
    for j in range(2):
        pk = bankA()
        for c in range(8):
            MM(V(pk, 0, [(1, 512)]), V(nT, c * T + j * 128, [(1, 128)]), V(ring, c * 512, [(1, 512)]), start=(c == 0), stop=(c == 7))
        pv = bankA()
        for c in range(8):
            MM(V(pv, 0, [(1, 512)]), V(nT, c * T + j * 128, [(1, 128)]), V(ring, GSZ + c * 512, [(1, 512)]), start=(c == 0), stop=(c == 7))
        CP("act", V(Vm, j * 512, [(1, 512)]), V(pv, 0, [(1, 512)]))
        sq = V(WF, 0, [(1, 512)])
        ACT(sq, V(pk, 0, [(1, 512)]), AF.Square)
        ss4 = V(stt, 4, [(1, 4)])
        RED(ss4, V(WF, 0, [(128, 4), (1, 128)]))
        rstd_from(ss4, ss4, 128, 4)
        kn = V(WF, 512, [(1, 512)])
        TT("dve", V(WF, 512, [(128, 4), (1, 128)]), V(pk, 0, [(128, 4), (1, 128)]), V(stt, 4, [(1, 4), (0, 128)]), ALU.mult)
        TT("pool", V(WB, 0, [(128, 4), (1, 128)]), V(WF, 512, [(128, 4), (1, 128)]), V(vec, VO["xkn"], [(0, 4), (1, 128)]), ALU.mult)
        pt = bankT()
        for h in range(4):
            TR(V(pt, h * 128, [(1, 128)]), V(WB, h * 128, [(1, 128)]), sig=(h == 3))
        CP("act", V(KmT, j * 128, [(256, 4), (1, 128)]), V(pt, 0, [(128, 4), (1, 128)]))

    QT0 = 0; MIX0 = 8 * T; XQT0 = 0; XOT0 = 4 * T; PTX0 = 8 * T
    SC_MLA = 96 ** -0.5; SC_XA = 128 ** -0.5
    xsem = [[S.new_sem("xsem%d_%d" % (p_, j)) for j in range(NB)] for p_ in range(2)]; osem = [S.new_sem("osem%d" % j) for j in range(NB)]
    out_toks = []

    import itertools
    def run_chains(chains):
        chains = list(chains)
        while chains:
            for c_ in list(chains):
                try:
                    next(c_)
                except StopIteration:
                    chains.remove(c_)

    prefetched = {}
    def get_granule(g):
        if g in prefetched:
            return prefetched.pop(g)
        return load_granule(g)

    for j in range(NB):
        S.dma("sp", V(xh, j * D, [(1, D)]), dview(x_d, j * 128 * D, [(D, 128), (1, D)]), xsem[0][j])

    uq_box = [None]

    def step_chains(chs, k):
        for _ in range(k):
            for c_ in list(chs):
                try:
                    next(c_)
                except StopIteration:
                    chs.remove(c_)

    def norm_trs():
        xso = [xs, vtok]
        for j in range(NB):
            for c in range(8):
                TR(V(psT[j], c * 128, [(1, 128)]), V(xso[j], c * 128, [(1, 128)]), sig=(c == 7))
        for j in range(NB):
            TT("dve", V(nT, j * 128, [(T, 8), (1, 128)]), V(psT[j], 0, [(128, 8), (1, 128)]), V(vec, VO["nm"], [(1, 8), (0, 128)]), ALU.mult)

    def proj_group(g):
        go = get_granule(g)
        gw = 432 if g == 3 else 512
        for j in range(NB):
            pp = bankB()
            for c in range(8):
                MM(V(pp, 0, [(1, gw)]), V(nT, c * T + j * 128, [(1, 128)]), V(ring, go + c * 512, [(1, gw)]), start=(c == 0), stop=(c == 7))
            if g == 2:
                ACT(V(Gt, j * 512, [(1, 512)]), V(pp, 0, [(1, 512)]), AF.Silu)
                TT("pool", V(Gt, j * 512, [(128, 4), (1, 128)]), V(Gt, j * 512, [(128, 4), (1, 128)]), V(vec, VO["gon"], [(0, 4), (1, 128)]), ALU.mult)
            elif g == 0:
                CP("act", V(qk, j * 512, [(1, 512)]), V(pp, 0, [(1, 512)]))
            elif g == 1:
                CP("dve", V(vtok, j * 512, [(1, 512)]), V(pp, 0, [(1, 512)]))
            else:
                CP("dve", V(r3, j * 432, [(1, 432)]), V(pp, 0, [(1, 432)]))

    def mla(j, c, tn):
        b = tn * NB + j; R3 = j * 432; fb = c * 1760; bb = c * 1536; sc = 16 + c * 24
        px = psT[c]
        def PF(f0, dims, p0=0, np_=128):
            bd = [(2 * s_, n_) for (s_, n_) in dims[:-1]] + [(1, 2 * dims[-1][1])]
            return V(px, 2 * f0, bd, p0, np_).bitcast(F32)
        PBv = lambda f0, dims, p0=0, np_=128: V(px, f0, dims, p0, np_)
        cq = V(r3, R3 + 16, [(1, 256)]); ckv = V(r3, R3 + 272, [(1, 128)]); kpe = V(r3, R3 + 400, [(1, 32)])
        st = lambda i, w=1: V(stt, sc + i, [(1, w)])
        ACT(V(WF, fb + 768, [(1, 256)]), cq, AF.Square, accum_out=st(0))
        ACT(V(WF, fb + 1024, [(1, 128)]), ckv, AF.Square, accum_out=st(1))
        ACT(V(WF, fb + 1632, [(1, 32)]), kpe, AF.Square, accum_out=st(10))
        yield
        ACT(st(0), st(0), AF.Ln, scale=1.0 / 256, bias=epsc)
        ACT(st(1), st(1), AF.Ln, scale=1.0 / 128, bias=epsc)
        ACT(st(0, 2), st(0, 2), AF.Exp, scale=-0.5)
        yield
        ACT(V(WB, bb, [(1, 256)]), cq, AF.Copy, scale=st(0))
        ACT(V(WB, bb + 256, [(1, 128)]), ckv, AF.Copy, scale=st(1))
        yield
        for cc in range(3):
            TR(PBv(cc * 128, [(1, 128)]), V(WB, bb + cc * 128, [(1, 128)]), sig=(cc == 2))
        yield
        cT = bb + 384
        TT("dve", V(WB, cT, [(128, 3), (1, 128)]), PBv(0, [(128, 3), (1, 128)]), V(vec, VO["qan"], [(1, 3), (0, 128)]), ALU.mult)
        yield
        uq_off = uq_box[0]
        qf = bb + 768
        for hf in range(2):
            for cc in range(2):
                MM(PF(0, [(1, 384)]), V(WB, cT + cc * 128, [(1, 128)]), V(ring, uq_off + cc * 768 + hf * 384, [(1, 384)]), start=(cc == 0), stop=(cc == 1))
            yield
            ACT(V(WF, fb + 768, [(1, 384)]), PF(0, [(1, 384)]), AF.Square)
            yield
            RED(st(2 + hf * 4, 4), V(WF, fb + 768, [(96, 4), (1, 96)]))
            yield
            ACT(st(2 + hf * 4, 4), st(2 + hf * 4, 4), AF.Ln, scale=1.0 / 96, bias=epsc)
            yield
            ACT(st(2 + hf * 4, 4), st(2 + hf * 4, 4), AF.Exp, scale=-0.5)
            yield
            TT("dve", V(WB, qf + hf * 384, [(96, 4), (1, 64)]), PF(0, [(96, 4), (1, 64)]), V(stt, sc + 2 + hf * 4, [(1, 4), (0, 64)]), ALU.mult)
            TT("dve", V(WF, fb + hf * 128, [(32, 4), (1, 32)]), PF(64, [(96, 4), (1, 32)]), V(stt, sc + 2 + hf * 4, [(1, 4), (0, 32)]), ALU.mult)
            yield
        TT("pool", V(WF, fb, [(32, 8), (1, 32)]), V(WF, fb, [(32, 8), (1, 32)]), V(vec, VO["gq"] + 64, [(0, 8), (1, 32)]), ALU.mult)
        yield
        cosb = V(COS, b * 16, [(0, 8), (1, 16)]); sinb = V(SIN, b * 16, [(0, 8), (1, 16)])
        x1 = V(WF, fb, [(32, 8), (1, 16)]); x2 = V(WF, fb + 16, [(32, 8), (1, 16)])
        tA = V(WF, fb + 1280, [(16, 8), (1, 16)]); tB = V(WF, fb + 1408, [(16, 8), (1, 16)])
        TT("pool", tA, x1, cosb, ALU.mult); TT("pool", tB, x2, sinb, ALU.mult)
        yield
        TT("pool", V(WB, qf + 64, [(96, 8), (1, 16)]), tA, tB, ALU.subtract)
        yield
        TT("pool", tA, x2, cosb, ALU.mult); TT("pool", tB, x1, sinb, ALU.mult)
        yield
        TT("pool", V(WB, qf + 80, [(96, 8), (1, 16)]), tA, tB, ALU.add)
        yield
        for h in range(8):
            TR(PBv(h * 128, [(1, 128)], 0, 96), V(WB, qf + h * 96, [(1, 96)]), sig=(h == 7))
        yield
        CP("act", V(U, QT0 + j * 128, [(T, 8), (1, 128)], 0, 96), PBv(0, [(128, 8), (1, 128)], 0, 96))
        kp = V(WF, fb + 1536, [(1, 32)])
        TT("pool", kp, kpe, vcol("gk", 64, 32), ALU.mult)
        yield
        c1 = V(COS, b * 16, [(1, 16)]); s1 = V(SIN, b * 16, [(1, 16)])
        k1 = V(WF, fb + 1536, [(1, 16)]); k2 = V(WF, fb + 1552, [(1, 16)])
        rA = V(WF, fb + 1568, [(1, 16)]); rB = V(WF, fb + 1584, [(1, 16)])
        kr = fb + 1600
        TT("pool", rA, k1, c1, ALU.mult); TT("pool", rB, k2, s1, ALU.mult)
        yield
        TT("pool", V(WF, kr, [(1, 16)]), rA, rB, ALU.subtract)
        yield
        TT("pool", rA, k2, c1, ALU.mult); TT("pool", rB, k1, s1, ALU.mult)
        yield
        TT("pool", V(WF, kr + 16, [(1, 16)]), rA, rB, ALU.add)
        yield
        for hf in range(2):
            MM(PF(0, [(1, 512)]), V(WB, cT + 256, [(1, 128)]), V(ring, uq_off + 1536 + hf * 512, [(1, 512)]))
            yield
            ACT(V(WF, fb + 768, [(1, 512)]), PF(0, [(1, 512)]), AF.Square)
            yield
            RED(st(11 + hf * 4, 4), V(WF, fb + 768, [(128, 4), (1, 64)]))
            CP("act", V(VA, (b * 8 + hf * 4) * 66, [(66, 4), (1, 64)]), PF(64, [(128, 4), (1, 64)]))
            yield
            TS("dve", st(11 + hf * 4, 4), st(11 + hf * 4, 4), st(10), None, ALU.add)
            yield
            ACT(st(11 + hf * 4, 4), st(11 + hf * 4, 4), AF.Ln, scale=1.0 / 96, bias=epsc)
            yield
            ACT(st(11 + hf * 4, 4), st(11 + hf * 4, 4), AF.Exp, scale=-0.5)
            yield
            TT("dve", V(WF, fb + hf * 384, [(96, 4), (1, 64)]), PF(0, [(128, 4), (1, 64)]), V(stt, sc + 11 + hf * 4, [(1, 4), (0, 64)]), ALU.mult)
            yield
        kfb = bb + 768
        TT("dve", V(WB, kfb, [(96, 8), (1, 64)]), V(WF, fb, [(96, 8), (1, 64)]), V(vec, VO["gk"], [(0, 8), (1, 64)]), ALU.mult)
        TT("pool", V(WB, kfb + 64, [(96, 8), (1, 32)]), V(WF, kr, [(0, 8), (1, 32)]), V(stt, sc + 11, [(1, 8), (0, 32)]), ALU.mult)
        yield
        for h in range(8):
            TR(PBv(h * 128, [(1, 128)], 0, 96), V(WB, kfb + h * 96, [(1, 96)]), sig=(h == 7))
        yield
        CP("act", V(KT, b * 128, [(SEQ, 8), (1, 128)], 0, 96), PBv(0, [(128, 8), (1, 128)], 0, 96))

    norm_pre(0, NB, 2)
    norm_trs()
    proj_group(3)
    emit_casts([5, 6, 7, 8])
    chains0 = [mla(j, j, 0) for j in range(NB)]
    proj_group(2); step_chains(chains0, 3)
    proj_group(0)
    uq_box[0] = get_granule(4)
    step_chains(chains0, 3)
    proj_group(1)
    while chains0:
        step_chains(chains0, 1)
    emit_casts(list(range(9, 20)))

    for ti in range(NTILE):
        t0 = ti * T
        xb = (ti % 2) * NB * D
        if ti + 1 < NTILE:
            xb2 = ((ti + 1) % 2) * NB * D
            for j in range(NB):
                S.dma("sp", V(xh, xb2 + j * D, [(1, D)]), dview(x_d, (t0 + T + j * 128) * D, [(D, 128), (1, D)]), xsem[(ti + 1) % 2][j])
        psG = psA[3]
        def gla(j):
            R3 = j * 432
            CP("dve", V(WB, 0, [(1, 16)]), V(r3, R3, [(1, 16)]))
            yield
            pt = bankT()
            yield
            yield
            TR(V(pt, 0, [(1, 128)], 0, 16), V(WB, 0, [(1, 16)]))
            yield
            CP("dve", V(WB, 16, [(1, 128)], 0, 16), V(pt, 0, [(1, 128)], 0, 16))
            yield
            yield
            yield
            MM(V(psG, 0, [(1, 256)]), V(WB, 16, [(1, 128)], 0, 16), V(w2b, 0, [(1, 256)], 0, 16))
            yield
            z = V(WF, 0, [(1, 256)]); l_ = V(WF, 256, [(1, 256)])
            TT("dve", z, V(psG, 0, [(1, 256)]), vcol("gb", 0, 256), ALU.add)
            yield
            ACT(z, z, AF.Exp, scale=-1.0)
            yield
            ACT(l_, z, AF.Ln, bias=1.0)
            yield
            yield
            yield
            MM(V(psG, 0, [(1, 256)]), U_f, l_)
            for h in range(4):
                MM(V(psG, 256 + h, [(1, 1)], 0, 64), V(WF, 256 + h * 64, [(1, 64)]), onesf, sig=(h == 3))
            yield
            Eq = V(WF, 512, [(1, 256)]); Ek = V(WF, 768, [(1, 256)])
            ACT(Eq, V(psG, 0, [(1, 256)]), AF.Exp, scale=-1.0 / 16)
            ACT(Ek, V(psG, 0, [(1, 256)]), AF.Exp, scale=1.0 / 16)
            ACT(V(stt, 8, [(1, 4)], 0, 64), V(psG, 256, [(1, 4)], 0, 64), AF.Exp, scale=-1.0 / 16)
            yield
            qd = V(WB, 256, [(1, 256)]); ki = V(WB, 512, [(1, 256)])
            TT("pool", qd, V(qk, j * 512, [(1, 256)]), Eq, ALU.mult)
            TT("pool", ki, V(qk, j * 512 + 256, [(1, 256)]), Ek, ALU.mult)
            yield
            pt = bankT()
            yield
            yield
            for h in range(8):
                TR(V(pt, h * 128, [(1, 128)], 0, 64), V(WB, 256 + h * 64, [(1, 64)]), sig=(h == 7))
            yield
            qkT = 768
            CP("act", V(WB, qkT, [(1, 1024)], 0, 64), V(pt, 0, [(1, 1024)], 0, 64))
            yield
            yield
            yield
            for h in range(4):
                MM(V(psG, h * 128, [(1, 128)]), V(WB, qkT + (4 + h) * 128, [(1, 128)], 0, 64), V(WB, qkT + h * 128, [(1, 128)], 0, 64), sig=(h == 3))
            yield
            ATm = 1792
            STT("dve", V(WB, ATm, [(128, 4), (1, 128)]), V(psG, 0, [(128, 4), (1, 128)]), 0.125, V(Ub, 0, [(0, 4), (1, 128)]), ALU.mult, ALU.mult)
            yield
            yield
            yield
            for h in range(4):
                MM(V(psG, h * 128, [(1, 128)]), V(WB, ATm + h * 128, [(1, 128)]), V(vtok, j * 512 + h * 128, [(1, 128)]), start=True, stop=False, sig=False)
                MM(V(psG, h * 128, [(1, 128)]), V(WB, qkT + h * 128, [(1, 128)], 0, 64), V(Sb, h * 128, [(1, 128)], 0, 64), start=False, stop=True, sig=(h == 3))
            yield
            osq = V(WF, 1024, [(1, 512)])
            ACT(osq, V(psG, 0, [(1, 512)]), AF.Square)
            yield
            so = V(stt, 12, [(1, 4)])
            RED(so, V(WF, 1024, [(128, 4), (1, 128)]))
            yield
            rstd_from(so, so, 128, 4)
            yield
            TT("dve", V(WF, 1536, [(128, 4), (1, 128)]), V(psG, 0, [(128, 4), (1, 128)]), V(stt, 12, [(1, 4), (0, 128)]), ALU.mult)
            yield
            ogv = V(xs, 0, [(1, 512)])
            TT("pool", ogv, V(WF, 1536, [(1, 512)]), V(Gt, j * 512, [(1, 512)]), ALU.mult)
            yield
            yield
            for h in range(4):
                MM(V(psG, h * 128, [(1, 128)], 0, 64), V(WB, 512 + h * 64, [(1, 64)]), V(vtok, j * 512 + h * 128, [(1, 128)]), sig=(h == 3))
            yield
            TT("dve", V(Sst, 0, [(1, 512)], 0, 64), V(psG, 0, [(1, 512)], 0, 64), V(Sst, 0, [(1, 512)], 0, 64), ALU.add)
            pt = bankT()
            for c in range(4):
                TR(V(pt, c * 128, [(1, 128)]), V(xs, c * 128, [(1, 128)]), sig=(c == 3))
            yield
            TT("dve", V(Sst, 0, [(128, 4), (1, 128)], 0, 64), V(Sst, 0, [(128, 4), (1, 128)], 0, 64), V(stt, 8, [(1, 4), (0, 128)], 0, 64), ALU.mult)
            CP("act", V(U, MIX0 + j * 128, [(T, 4), (1, 128)]), V(pt, 0, [(128, 4), (1, 128)]))
            yield
            TS("pool", V(Sb, 0, [(1, 512)], 0, 64), V(Sst, 0, [(1, 512)], 0, 64), 0.125, None, ALU.mult)
            yield

        gla_gen = itertools.chain(gla(0), gla(1)) if NB == 2 else itertools.chain(*[gla(j) for j in range(NB)])
        gla_done = [False]
        def gla_advance(k):
            for _ in range(k):
                if gla_done[0]:
                    return
                try:
                    next(gla_gen)
                except StopIteration:
                    gla_done[0] = True

        units = []
        for h in range(8):
            for kb in range(0, ti * NB, 2):
                units.append((h, "off", kb))
            units.append((h, "diag", ti * NB))
        AHEAD = 2
        pS_of = {}; pO_of = {}
        deferred = []

        def emit_S(ui):
            h, kind, kb = units[ui]
            pS = bankA(); pS_of[ui] = pS
            q_ = lambda c0, n: V(U, QT0 + h * T + c0, [(1, n)], 0, 96)
            k_ = lambda kbb: V(KT, h * SEQ + kbb * 128, [(1, 128)], 0, 96)
            if kind == "off":
                MM(V(pS, 0, [(1, T)]), k_(kb), q_(0, T), sig=False)
                MM(V(pS, T, [(1, T)]), k_(kb + 1), q_(0, T))
            else:
                MM(V(pS, 0, [(1, T)]), k_(kb), q_(0, T), sig=False)
                MM(V(pS, T, [(1, 128)]), k_(kb + 1), q_(128, 128))

        def emit_E(ui):
            h, kind, kb = units[ui]
            pS = pS_of[ui]; pto = (ui % 4) * 512
            n = 2 * T if kind == "off" else T + 128
            ACT(V(PT, pto, [(1, n)]), V(pS, 0, [(1, n)]), AF.Exp, scale=SC_MLA)
            if kind == "diag":
                TT("pool", V(PT, pto, [(T, 2), (1, 128)]), V(PT, pto, [(T, 2), (1, 128)]), V(Ub, 0, [(0, 2), (1, 128)]), ALU.mult)

        def emit_norm1(h, pO, ui):
            recrow = V(WF, 2048 + (h % 2) * T, [(1, T)], 64, 1)
            RECIP(recrow, V(pO, 0, [(1, T)], 64, 1))
            deferred.append([ui + 3, (lambda: emit_norm2(h, pO)), h])

        def emit_norm2(h, pO):
            recrow = V(WF, 2048 + (h % 2) * T, [(1, T)], 64, 1)
            pb = bankA()
            MM(V(pb, 0, [(1, T)], 0, 64), V(cst, 128 + 64, [(1, 64)], 64, 1), recrow)
            bcs = V(WF, 2048 + (h % 2) * T, [(1, T)], 0, 64)
            CP("dve", bcs, V(pb, 0, [(1, T)], 0, 64))
            TT("dve", V(U, MIX0 + (4 + h // 2) * T, [(1, T)], (h % 2) * 64, 64), V(pO, 0, [(1, T)], 0, 64), bcs, ALU.mult)

        def emit_PV(ui):
            h, kind, kb = units[ui]
            pto = (ui % 4) * 512
            first = (ui == 0) or (units[ui - 1][0] != h)
            if first:
                while any(d[2] <= h - 2 for d in deferred):
                    for d in list(deferred):
                        if d[2] <= h - 2:
                            deferred.remove(d); d[1]()
                pO_of[h] = bankB()
            pO = pO_of[h]
            va = lambda kbb: V(VA, (kbb * 8 + h) * 66, [(1, 128)])
            if kind == "off":
                MM(V(pO, 0, [(1, T)]), va(kb), V(PT, pto, [(1, T)]), start=first, stop=False, sig=False)
                MM(V(pO, 0, [(1, T)]), va(kb + 1), V(PT, pto + T, [(1, T)]), start=False, stop=False, sig=True)
            else:
                MM(V(pO, 0, [(1, T)]), va(kb), V(PT, pto, [(1, T)]), start=first, stop=False, sig=False)
                MM(V(pO, 128, [(1, 128)]), va(kb + 1), V(PT, pto + T, [(1, 128)]), start=False, stop=True, sig=True)
                deferred.append([ui + 2, (lambda hh=h, pp=pO, uu=ui: emit_norm1(hh, pp, uu + 2)), h])

        nu = len(units)
        gk_ = max(1, -(-86 // nu))
        for step in range(nu + AHEAD):
            if step < nu:
                emit_S(step); emit_E(step)
            if step - AHEAD >= 0:
                emit_PV(step - AHEAD)
            for d in list(deferred):
                if d[0] <= step - AHEAD:
                    deferred.remove(d); d[1]()
            gla_advance(gk_)
        while deferred:
            d = deferred.pop(0); d[1]()
        gla_advance(10 ** 6)

        if ti == 0:
            emit_casts(list(range(20, 26)))
        wo = [get_granule(5), get_granule(6)]
        for j in range(NB):
            for hf in range(2):
                pp = bankA()
                for c in range(8):
                    MM(V(pp, 0, [(1, 512)]), V(U, MIX0 + c * T + j * 128, [(1, 128)]), V(ring, wo[c // 4] + (c % 4) * 1024 + hf * 512, [(1, 512)]), start=(c == 0), stop=(c == 7))
                xv = V(xh, xb + j * D + hf * 512, [(1, 512)])
                TT("dve", xv, V(pp, 0, [(1, 512)]), xv, ALU.add)
        xq_off = get_granule(7)
        norm_transpose_all("nxa", NB, xb)
        def xq_chain(j):
            pq = psA[j]
            for c in range(8):
                MM(V(pq, 0, [(1, 512)]), V(nT, c * T + j * 128, [(1, 128)]), V(ring, xq_off + c * 512, [(1, 512)]), start=(c == 0), stop=(c == 7))
            yield
            ACT(V(WF, j * 1024, [(1, 512)]), V(pq, 0, [(1, 512)]), AF.Square)
            yield
            s4_ = V(stt, 4 + 4 * j, [(1, 4)])
            RED(s4_, V(WF, j * 1024, [(128, 4), (1, 128)]))
            yield
            ACT(s4_, s4_, AF.Ln, scale=1.0 / 128, bias=epsc)
            yield
            ACT(s4_, s4_, AF.Exp, scale=-0.5)
            yield
            TT("dve", V(WF, j * 1024 + 512, [(128, 4), (1, 128)]), V(pq, 0, [(128, 4), (1, 128)]), V(stt, 4 + 4 * j, [(1, 4), (0, 128)]), ALU.mult)
            yield
            TT("pool", V(WB, j * 512, [(128, 4), (1, 128)]), V(WF, j * 1024 + 512, [(128, 4), (1, 128)]), V(vec, VO["xqn"], [(0, 4), (1, 128)]), ALU.mult)
            yield
            pt = psT[j]
            for hh in range(4):
                TR(V(pt, hh * 128, [(1, 128)]), V(WB, j * 512 + hh * 128, [(1, 128)]), sig=(hh == 3))
            yield
            CP("act", V(U, XQT0 + j * 128, [(T, 4), (1, 128)]), V(pt, 0, [(128, 4), (1, 128)]))
        run_chains([xq_chain(j) for j in range(NB)])

        def xS(hh):
            base = (hh % 2) * 2 * T
            for kb in range(2):
                pS = bankA()
                MM(V(pS, 0, [(1, T)]), V(KmT, hh * 256 + kb * 128, [(1, 128)]), V(U, XQT0 + hh * T, [(1, T)]))
                ACT(V(U, PTX0 + base + kb * T, [(1, T)]), V(pS, 0, [(1, T)]), AF.Exp, scale=SC_XA)
        def xPV(hh):
            base = (hh % 2) * 2 * T
            pb_ = psB[hh % 2]
            for kb in range(2):
                MM(V(pb_, 0, [(1, T)]), V(Vm, kb * 512 + hh * 128, [(1, 128)]), V(U, PTX0 + base + kb * T, [(1, T)]), start=(kb == 0), stop=(kb == 1))
            for kb in range(2):
                MM(V(pb_, T, [(1, T)]), V(onesb, 0, [(1, 128)]), V(U, PTX0 + base + kb * T, [(1, T)]), start=(kb == 0), stop=(kb == 1))
            rd = V(WF, 2048 + (hh % 2) * T, [(1, T)])
            ACT(rd, V(pb_, T, [(1, T)]), AF.Ln)
            ACT(rd, rd, AF.Exp, scale=-1.0)
            TT("dve", V(U, XOT0 + hh * T, [(1, T)]), V(pb_, 0, [(1, T)]), rd, ALU.mult)
        xS(0); xS(1); xPV(0); xS(2); xPV(1); xS(3); xPV(2); xPV(3)
        xo_off = get_granule(8)
        for j in range(NB):
            for hf in range(2):
                pp = bankA()
                for c in range(4):
                    MM(V(pp, 0, [(1, 512)]), V(U, XOT0 + c * T + j * 128, [(1, 128)]), V(ring, xo_off + c * 1024 + hf * 512, [(1, 512)]), start=(c == 0), stop=(c == 3))
                xv = V(xh, xb + j * D + hf * 512, [(1, 512)])
                TT("dve", xv, V(pp, 0, [(1, 512)]), xv, ALU.add)
        norm_transpose_all("nffn", NB, xb)
        for fc in range(NFC):
            if fc % 2 == 0:
                gu = get_granule(9 + fc // 2)
            sub = fc % 2
            pg = psA[0] if fc % 2 == 0 else psA[2]
            pu = psA[1] if fc % 2 == 0 else psA[3]
            for c in range(8):
                MM(V(pg, 0, [(1, T)]), V(ring, gu + sub * 2048 + c * 128, [(1, 128)]), V(nT, c * T, [(1, T)]), start=(c == 0), stop=(c == 7))
            for c in range(8):
                MM(V(pu, 0, [(1, T)]), V(ring, gu + sub * 2048 + 1024 + c * 128, [(1, 128)]), V(nT, c * T, [(1, T)]), start=(c == 0), stop=(c == 7))
            sl4 = fc % 4
            gb_ = sl4 * (T + 2)
            ub = V(WB, sl4 * T, [(1, T)])
            CP("pool", V(WF, gb_, [(1, 2)]), V(halo, fc * 2, [(1, 2)]))
            CP("act", V(WF, gb_ + 2, [(1, T)]), V(pg, 0, [(1, T)]))
            CP("act", ub, V(pu, 0, [(1, T)]))
            CP("pool", V(halo, fc * 2, [(1, 2)]), V(WF, gb_ + T, [(1, 2)]))
            tc_ = V(WF, 1032 + sl4 * T, [(1, T)])
            cw = lambda i: V(vec, VO["cw"] + fc * 3 + i, [(1, 1)])
            TS("pool", tc_, V(WF, gb_ + 2, [(1, T)]), cw(2), V(vec, VO["cb"] + fc, [(1, 1)]), ALU.mult, ALU.add)
            STT("dve", tc_, V(WF, gb_ + 1, [(1, T)]), cw(1), tc_, ALU.mult, ALU.add)
            STT("dve", tc_, V(WF, gb_, [(1, T)]), cw(0), tc_, ALU.mult, ALU.add)
            ACT(tc_, tc_, AF.Silu)
            TT("pool", V(U, fc * T, [(1, T)]), tc_, ub, ALU.mult)
        nxt = ti + 1 < NTILE
        if nxt:
            norm_pre(((ti + 1) % 2) * NB * D, NB, 2)
        def dn_group(k):
            dn = get_granule(20 + k)
            nj = 4 if k < 5 else 2
            for jj in range(nj):
                fc = 4 * k + jj
                for j in range(NB):
                    for hf in range(2):
                        MM(V(psA[j * 2 + hf], 0, [(1, 512)]), V(U, fc * T + j * 128, [(1, 128)]), V(ring, dn + jj * 1024 + hf * 512, [(1, 512)]), start=(fc == 0), stop=(fc == NFC - 1), sig=(j == NB - 1 and hf == 1))
        if nxt:
            dn_group(0); dn_group(1)
            norm_trs()
            proj_group(3)
            chains = [mla(j, j, ti + 1) for j in range(NB)]
            dn_group(2); step_chains(chains, 3)
            proj_group(2); step_chains(chains, 2)
            dn_group(3)
            proj_group(0)
            uq_box[0] = get_granule(4)
            step_chains(chains, 8)
            dn_group(4); step_chains(chains, 8)
            proj_group(1)
            while chains:
                step_chains(chains, 1)
            dn_group(5)
        else:
            for k in range(6):
                dn_group(k)
        for j in range(NB):
            for hf in range(2):
                xv = V(xh, xb + j * D + hf * 512, [(1, 512)])
                TT("dve", xv, V(psA[j * 2 + hf], 0, [(1, 512)]), xv, ALU.add)
            out_toks.append(S.dma("sp", dview(out_d, (t0 + j * 128) * D, [(D, 128), (1, D)]), V(xh, xb + j * D, [(1, D)]), osem[j]))
    S.wait_all("sp", [[o_, S.dma_cum[o_]] for o_ in osem])
    return nc, S


_CACHE = {}

def kernel(x, mem, positions, norm_mix, w_in, gla_gate_w2, gla_gate_b, gla_out_norm,
           mla_q_a_norm, mla_w_uq, mla_kv_a_norm, mla_w_ukv, mla_q_norm, mla_k_norm, w_out,
           norm_xa, norm_mem, xa_w_q, xa_w_kv, xa_q_norm, xa_k_norm, xa_w_o,
           norm_ffn, ffn_w_gate, ffn_w_up, ffn_conv_w, ffn_conv_b, ffn_w_down):
    f = lambda a: np.ascontiguousarray(np.asarray(a, dtype=np.float32))
    x = f(x); mem = f(mem); positions = np.asarray(positions).astype(np.int32)
    fm = lambda v, c: f(v).reshape(c, 128).T
    rep = lambda v: np.broadcast_to(f(v).reshape(1, -1), (128, f(v).size))
    cols = [fm(norm_mix[0], 8), fm(norm_xa[0], 8), fm(norm_ffn[0], 8), fm(norm_mem[0], 8), fm(mla_q_a_norm[0], 2), fm(mla_kv_a_norm[0], 1),
            rep(gla_gate_b[0]), rep(gla_out_norm[0]), rep(mla_q_norm[0]), rep(mla_k_norm[0]), rep(xa_q_norm[0]), rep(xa_k_norm[0]),
            f(ffn_conv_w[0]).reshape(3, NFC, 128).transpose(2, 1, 0).reshape(128, 66), fm(ffn_conv_b[0], NFC)]
    vec = np.ascontiguousarray(np.concatenate(cols, axis=1).astype(np.float32))
    assert vec.shape == (128, NV)
    ident = np.eye(128, dtype=np.float32)
    Utri = np.triu(np.ones((128, 128), dtype=np.float32))
    inv = (10000.0 ** (-np.arange(16, dtype=np.float32) / 16)).astype(np.float32)
    cst = np.ascontiguousarray(np.concatenate([ident, Utri, np.broadcast_to(inv[None, :], (128, 16))], axis=1).astype(np.float32))
    if "nc" not in _CACHE:
        _CACHE["nc"] = build_program()[0]
    nc = _CACHE["nc"]
    shared = {"cst": cst, "vec": vec, "w_in": f(w_in[0]), "w2": f(gla_gate_w2[0]), "w_uq": f(mla_w_uq[0]), "w_ukv": f(mla_w_ukv[0]),
              "w_out": f(w_out[0]), "xa_w_q": f(xa_w_q[0]), "xa_w_kv": f(xa_w_kv[0]), "xa_w_o": f(xa_w_o[0]),
              "w_gate": f(ffn_w_gate[0]), "w_up": f(ffn_w_up[0]), "w_down": f(ffn_w_down[0])}
    in_maps = []
    for c in range(8):
        m = dict(shared)
        m["x"] = x[c]; m["mem"] = mem[c]
        m["pos"] = np.ascontiguousarray(positions[c].reshape(32, 128).T)
        in_maps.append(m)
    res = run_bass_kernel_spmd(nc, in_maps, core_ids=list(range(8)))
    return np.stack([np.asarray(r["out"]).reshape(SEQ, D) for r in res.results], axis=0).astype(np.float32)
```

```python
import numpy as np
import concourse.bass as bass
import concourse.mybir as mybir

F32 = mybir.dt.float32
BF = mybir.dt.bfloat16
I32 = mybir.dt.int32
ALU = mybir.AluOpType
AF = mybir.ActivationFunctionType
AX = mybir.AxisListType


def _prod(xs):
    r = 1
    for v in xs:
        r *= int(v)
    return r


class Sched:
    def __init__(self, nc):
        self.nc = nc
        self.eng = dict(pe=nc.tensor, act=nc.scalar, dve=nc.vector, pool=nc.gpsimd, sp=nc.sync)
        self.sem = {}
        self.cnt = {}
        for e in self.eng:
            self.sem[e] = nc.alloc_semaphore("cs_" + e)
            self.cnt[e] = 0
        self.semh = {("cs_" + e): self.sem[e] for e in self.eng}
        self.seen = {e: {} for e in self.eng}
        self.hist = {}
        self.dma_cum = {}
        self.n_wait = 0
        self.n_ops = 0

    def new_sem(self, name):
        h = self.nc.alloc_semaphore(name)
        self.semh[name] = h
        self.dma_cum[name] = 0
        return name

    def _acc(self, ap):
        t = ap.tensor
        name = t.name
        space = str(ap.space)
        pat = ap.ap
        off = int(ap.offset)
        if "PSUM" in space:
            return (name, 0, 1 << 30, 0, 128, True)
        if "SB" in space:
            psz = _prod(t.shape[1:])
            p0 = off // psz
            f0 = off % psz
            npart = pat[0][1]
            span = sum((c - 1) * abs(s) for s, c in pat[1:]) + 1
            return (name, f0, f0 + span, p0, p0 + npart, False)
        span = sum((c - 1) * abs(s) for s, c in pat) + 1
        return (name, off, off + span, 0, 1, False)

    def _deps(self, e, is_dma, accs, skip_tok=None):
        need = {}
        for (key, lo, hi, plo, phi, excl), w in accs:
            lst = self.hist.get(key)
            if not lst:
                continue
            for h in lst:
                hlo, hhi, hplo, hphi, he, hdma, tok, hw = h
                if hhi <= lo or hi <= hlo or hphi <= plo or phi <= hplo:
                    continue
                if skip_tok is not None and tok is skip_tok:
                    continue
                if he == e and not is_dma and not hdma:
                    if e == "pe":
                        continue
                    if not (hw or w):
                        continue
                else:
                    if not (hw or w or excl):
                        continue
                sn, val = tok[0], tok[1]
                assert val is not None, "unsealed dma token used"
                if need.get(sn, 0) < val:
                    need[sn] = val
        return need

    def _record(self, e, is_dma, accs, tok):
        for (key, lo, hi, plo, phi, excl), w in accs:
            lst = self.hist.setdefault(key, [])
            keep = []
            for h in lst:
                hlo, hhi, hplo, hphi, he, hdma, htok, hw = h
                contained = lo <= hlo and hhi <= hi and plo <= hplo and hphi <= phi
                if contained:
                    if w:
                        continue
                    if excl and (he != e or hdma or is_dma):
                        continue
                    if (not hw) and he == e and not is_dma and not hdma:
                        continue
                keep.append(h)
            keep.append((lo, hi, plo, phi, e, is_dma, tok, w))
            self.hist[key] = keep

    def _emit_waits(self, e, need):
        eng = self.eng[e]
        seen = self.seen[e]
        for sn, val in need.items():
            if seen.get(sn, 0) < val:
                eng.wait_ge(self.semh[sn], val)
                seen[sn] = val
                self.n_wait += 1

    def op(self, e, fn, kwargs, reads, writes, sig=True):
        accs = [(self._acc(a), False) for a in reads] + [(self._acc(a), True) for a in writes]
        need = self._deps(e, False, accs)
        self._emit_waits(e, need)
        ins = fn(**kwargs)
        if sig:
            self.cnt[e] += 1
            ins.then_inc(self.sem[e], 1)
            tok = ("cs_" + e, self.cnt[e])
        else:
            tok = ("cs_" + e, self.cnt[e] + 1)
        self._record(e, False, accs, tok)
        self.n_ops += 1
        return ins

    def dma(self, e, out, in_, sem, tok=None, **kw):
        accs = [(self._acc(in_), False), (self._acc(out), True)]
        need = self._deps(e, True, accs, skip_tok=tok)
        self._emit_waits(e, need)
        ins = self.eng[e].dma_start(out=out, in_=in_, **kw)
        ins.then_inc(self.semh[sem], 16)
        self.dma_cum[sem] += 16
        if tok is None:
            tok = [sem, self.dma_cum[sem]]
        self._record(e, True, accs, tok)
        self.n_ops += 1
        return tok

    def group_tok(self, sem):
        return [sem, None]

    def seal(self, tok):
        tok[1] = self.dma_cum[tok[0]]

    def wait_all(self, e, toks):
        need = {}
        for sn, val in toks:
            if need.get(sn, 0) < val:
                need[sn] = val
        self._emit_waits(e, need)


def view(t, p0, npart, f0, dims):
    psz = _prod(t.shape[1:])
    return bass.AP(t, p0 * psz + f0, [[psz, npart]] + [[int(s), int(c)] for s, c in dims])


def dview(t, off, dims):
    return bass.AP(t, int(off), [[int(s), int(c)] for s, c in dims])


import math
from concourse.bass_utils import run_bass_kernel_spmd

D = 1024; SEQ = 4096; NBLK = 32; T = 256; NB = T // 128; NTILE = SEQ // T
FF = 2816; NFC = 22; EPS = 1e-6
GSZ = 4096
R_SLOTS = 3
VO = {}
def _vo():
    o = 0
    for n, w in [("nm", 8), ("nxa", 8), ("nffn", 8), ("nmem", 8), ("qan", 2), ("kvan", 1), ("gb", 256), ("gon", 128),
                 ("gq", 96), ("gk", 96), ("xqn", 128), ("xkn", 128), ("cw", 66), ("cb", 22)]:
        VO[n] = o; o += w
    return o
NV = _vo()


def build_program(dbg=False):
    nc = bass.Bass("TRN2", target_bir_lowering=False)
    S = Sched(nc)
    dt_in = lambda n, shp, dt=F32: nc.dram_tensor(n, shp, dt, kind="ExternalInput")
    x_d = dt_in("x", [SEQ, D]); mem_d = dt_in("mem", [256, D]); pos_d = dt_in("pos", [128, 32], I32)
    cst_d = dt_in("cst", [128, 272]); vec_d = dt_in("vec", [128, NV])
    w_in_d = dt_in("w_in", [D, 1968]); w2_d = dt_in("w2", [16, 256]); w_uq_d = dt_in("w_uq", [256, 768]); w_ukv_d = dt_in("w_ukv", [128, 1024])
    w_out_d = dt_in("w_out", [D, D]); xwq_d = dt_in("xa_w_q", [D, 512]); xwkv_d = dt_in("xa_w_kv", [D, 1024]); xwo_d = dt_in("xa_w_o", [512, D])
    wg_d = dt_in("w_gate", [D, FF]); wu_d = dt_in("w_up", [D, FF]); wd_d = dt_in("w_down", [FF, D])
    out_d = nc.dram_tensor("out", [SEQ, D], F32, kind="ExternalOutput")
    NG = 26
    wsc = nc.dram_tensor("wsc", [NG * 128, GSZ], BF, kind="Internal")

    def sb(name, n, dt):
        return nc.alloc_sbuf_tensor("s_" + name, [128, n], dt)
    KT = sb("KT", 8 * SEQ, BF); VA = sb("VA", NBLK * 8 * 66 + 64, BF)
    cst = sb("cst", 272, F32); identb = sb("identb", 128, BF); Ub = sb("Ub", 128, BF); onesb = sb("onesb", 128, BF); small = sb("small", 8, F32)
    vec = sb("vec", NV, F32); COS = sb("COS", 512, F32); SIN = sb("SIN", 512, F32); w2b = sb("w2b", 256, BF)
    KmT = sb("KmT", 4 * 256, BF); Vm = sb("Vm", 2 * 512, BF); Sst = sb("Sst", 512, F32); Sb = sb("Sb", 512, BF); halo = sb("halo", 44, F32)
    xh = sb("xh", 2 * NB * D, F32); xs = sb("xs", D, BF); nT = sb("nT", 8 * T, BF)
    qk = sb("qk", NB * 512, F32); vtok = sb("vtok", NB * 512, BF); Gt = sb("Gt", NB * 512, BF); r3 = sb("r3", NB * 432, F32)
    WF = sb("WF", 3520, F32)
    WB = sb("WB", 3072, BF)
    U = sb("U", NFC * T, BF)
    PT = sb("PT", 3 * 512, BF)
    stt = sb("stt", 64, F32)
    ring = sb("ring", R_SLOTS * GSZ, BF)
    posi = sb("posi", 32, I32)
    psA = [nc.alloc_psum_tensor("psA%d" % i, [128, 512], F32) for i in range(4)]
    psB = [nc.alloc_psum_tensor("psB%d" % i, [128, 512], F32) for i in range(2)]
    psT = [nc.alloc_psum_tensor("psT%d" % i, [128, 1024], BF) for i in range(2)]
    rot = {"A": 0, "B": 0, "T": 0}
    def bankA():
        rot["A"] += 1; return psA[rot["A"] % 3]
    def bankB():
        rot["B"] += 1; return psB[rot["B"] % 2]
    def bankT():
        rot["T"] += 1; return psT[rot["T"] % 2]

    V = lambda t, f0, dims, p0=0, np_=128: view(t, p0, np_, f0, dims)
    isap = lambda a: isinstance(a, bass.AP)
    E = S.eng

    def ACT(out, in_, func, **kw):
        reads = [in_] + [v for k, v in kw.items() if isap(v) and k != "accum_out"]
        writes = [out] + ([kw["accum_out"]] if "accum_out" in kw else [])
        S.op("act", nc.scalar.activation, dict(out=out, in_=in_, func=func, **kw), reads, writes)
    def TT(e, out, in0, in1, op):
        S.op(e, E[e].tensor_tensor, dict(out=out, in0=in0, in1=in1, op=op), [in0, in1], [out])
    def TS(e, out, in0, s1, s2, op0, op1=None):
        kw = dict(out=out, in0=in0, scalar1=s1, scalar2=s2, op0=op0)
        if op1 is not None: kw["op1"] = op1
        S.op(e, E[e].tensor_scalar, kw, [in0] + [a for a in (s1, s2) if isap(a)], [out])
    def STT(e, out, in0, sc, in1, op0, op1):
        S.op(e, E[e].scalar_tensor_tensor, dict(out=out, in0=in0, scalar=sc, in1=in1, op0=op0, op1=op1), [in0, in1] + ([sc] if isap(sc) else []), [out])
    def CP(e, out, in_):
        if e == "act":
            S.op(e, nc.scalar.copy, dict(out=out, in_=in_), [in_], [out])
        else:
            S.op(e, E[e].tensor_copy, dict(out=out, in_=in_), [in_], [out])
    def RED(out, in_):
        S.op("dve", nc.vector.tensor_reduce, dict(out=out, in_=in_, axis=AX.X, op=ALU.add), [in_], [out])
    def RECIP(out, in_):
        S.op("dve", nc.vector.reciprocal, dict(out=out, in_=in_), [in_], [out])
    def MEMSET(e, ap, c):
        S.op(e, E[e].memset, dict(ap=ap, constant=c), [], [ap])
    def MM(out, lhsT, rhs, start=True, stop=True, sig=None, **kw):
        S.op("pe", nc.tensor.matmul, dict(out=out, lhsT=lhsT, rhs=rhs, start=start, stop=stop, **kw), [lhsT, rhs], [out], sig=(stop if sig is None else sig))
    def TR(out, in_, sig=True):
        idn = V(identb, 0, [(1, 128)])
        S.op("pe", nc.tensor.transpose, dict(out=out, in_=in_, identity=idn), [in_, idn], [out], sig=sig)

    ident_f = V(cst, 0, [(1, 128)]); U_f = V(cst, 128, [(1, 128)])
    epsc = V(small, 0, [(1, 1)]); halfpi = V(small, 1, [(1, 1)]); onesf = V(small, 2, [(1, 1)])
    vcol = lambda n, i=0, w=1: V(vec, VO[n] + i, [(1, w)])

    def rstd_from(ss, out, n, k=1):
        ACT(out, ss, AF.Ln, scale=1.0 / n, bias=epsc)
        ACT(out, out, AF.Exp, scale=-0.5)

    semc = [S.new_sem("pro%d" % i) for i in range(6)]
    S.dma("sp", V(cst, 0, [(1, 272)]), dview(cst_d, 0, [(272, 128), (1, 272)]), semc[0])
    S.dma("sp", V(vec, 0, [(1, NV)]), dview(vec_d, 0, [(NV, 128), (1, NV)]), semc[1])
    S.dma("sp", V(posi, 0, [(1, 32)]), dview(pos_d, 0, [(32, 128), (1, 32)]), semc[2])
    S.dma("pool", V(w2b, 0, [(1, 256)], 0, 16), dview(w2_d, 0, [(256, 16), (1, 256)]), semc[3])
    gsem = [S.new_sem("gs%d" % g) for g in range(NG)]
    G = 128 * GSZ
    def cast(g, dst_off, dst_dims, src_t, src_off, src_dims, tok):
        S.dma("pool", dview(wsc, g * G + dst_off, [(GSZ, 128)] + dst_dims), dview(src_t, src_off, src_dims), gsem[g], tok=tok)
    def cast_group(g, items):
        tok = S.group_tok(gsem[g])
        for it in items:
            cast(g, *it, tok)
        S.seal(tok)
    WIN_ORDER = [3, 2, 0, 1]
    CAST = {}
    CAST[2] = [(0, [(512, 8), (1, 512)], w_in_d, 1040, [(1968, 128), (128 * 1968, 8), (1, 512)])]
    CAST[0] = [(0, [(512, 8), (1, 512)], w_in_d, 0, [(1968, 128), (128 * 1968, 8), (1, 512)])]
    CAST[1] = [(0, [(512, 8), (1, 512)], w_in_d, 512, [(1968, 128), (128 * 1968, 8), (1, 512)])]
    CAST[3] = [(0, [(512, 8), (1, 16)], w_in_d, 1024, [(1968, 128), (128 * 1968, 8), (1, 16)]),
               (16, [(512, 8), (1, 416)], w_in_d, 1552, [(1968, 128), (128 * 1968, 8), (1, 416)])]
    CAST[4] = [(0, [(768, 2), (1, 768)], w_uq_d, 0, [(768, 128), (128 * 768, 2), (1, 768)]),
               (1536, [(1, 1024)], w_ukv_d, 0, [(1024, 128), (1, 1024)])]
    for gi in range(2):
        CAST[5 + gi] = [(0, [(1024, 4), (1, 1024)], w_out_d, gi * 4 * 128 * 1024, [(1024, 128), (128 * 1024, 4), (1, 1024)])]
    CAST[7] = [(0, [(512, 8), (1, 512)], xwq_d, 0, [(512, 128), (128 * 512, 8), (1, 512)])]
    CAST[8] = [(0, [(1024, 4), (1, 1024)], xwo_d, 0, [(1024, 128), (128 * 1024, 4), (1, 1024)])]
    for k in range(11):
        items = []
        for sub in range(2):
            for m, wt in enumerate((wg_d, wu_d)):
                items.append((sub * 2048 + m * 1024, [(128, 8), (1, 128)], wt, (2 * k + sub) * 128, [(FF, 128), (128 * FF, 8), (1, 128)]))
        CAST[9 + k] = items
    for k in range(6):
        nj = 4 if k < 5 else 2
        CAST[20 + k] = [(0, [(1024, nj), (1, 1024)], wd_d, 4 * k * 128 * 1024, [(1024, 128), (128 * 1024, nj), (1, 1024)])]
    def emit_casts(gs):
        for g in gs:
            cast_group(g, CAST[g])
    emit_casts([3, 4, 2, 0, 1])

    MEMSET("dve", epsc, EPS); MEMSET("dve", halfpi, math.pi / 2); MEMSET("dve", onesf, 1.0)
    MEMSET("dve", V(onesb, 0, [(1, 128)]), 1.0)
    CP("dve", V(identb, 0, [(1, 128)]), ident_f); CP("dve", V(Ub, 0, [(1, 128)]), U_f)
    MEMSET("pool", V(VA, 0, [(1, NBLK * 8 * 66 + 64)]), 1.0)
    TT("dve", vcol("gk", 0, 64), vcol("gk", 0, 64), vcol("gq", 0, 64), ALU.mult)
    MEMSET("pool", V(Sst, 0, [(1, 512)]), 0.0); MEMSET("pool", V(Sb, 0, [(1, 512)]), 0.0); MEMSET("pool", V(halo, 0, [(1, 44)]), 0.0)
    posf = V(WF, 0, [(1, 32)]); ang = V(WF, 32, [(1, 512)]); uu = V(WF, 544, [(1, 512)]); kf_ = V(WF, 1056, [(1, 512)]); s4 = V(WF, 1568, [(1, 512)]); c4 = V(WF, 2080, [(1, 480)])
    c4 = V(COS, 0, [(1, 512)])
    s4 = V(SIN, 0, [(1, 512)])
    CP("dve", posf, V(posi, 0, [(1, 32)]))
    TT("dve", V(WF, 32, [(16, 32), (1, 16)]), V(WF, 0, [(1, 32), (0, 16)]), V(cst, 256, [(0, 32), (1, 16)]), ALU.mult)
    TS("dve", uu, ang, 1.0 / (2 * math.pi), None, ALU.mult)
    CP("dve", V(WF, 2080, [(1, 512)]).bitcast(I32), uu)
    CP("dve", kf_, V(WF, 2080, [(1, 512)]).bitcast(I32))
    STT("dve", uu, kf_, -2 * math.pi, ang, ALU.mult, ALU.add)
    ACT(s4, uu, AF.Sin, scale=0.25)
    ACT(c4, uu, AF.Sin, scale=0.25, bias=halfpi)
    sh = V(WF, 1056, [(1, 512)]); ch = V(WF, 1568, [(1, 512)])
    TT("dve", sh, s4, c4, ALU.mult)
    TS("dve", sh, sh, 2.0, None, ALU.mult)
    TT("dve", ch, s4, s4, ALU.mult)
    TS("dve", ch, ch, -2.0, 1.0, ALU.mult, ALU.add)
    TT("dve", s4, sh, ch, ALU.mult)
    TS("dve", s4, s4, 2.0, None, ALU.mult)
    TT("dve", c4, sh, sh, ALU.mult)
    TS("dve", c4, c4, -2.0, 1.0, ALU.mult, ALU.add)

    ring_pos = [0]
    GUSE = {3: [(512, 8), (1, 432)], 4: [(1, 2560)], 25: [(1, 2048)]}
    rsem = [S.new_sem("rs%d" % i) for i in range(R_SLOTS)]
    def load_granule(g):
        s = ring_pos[0] % R_SLOTS; ring_pos[0] += 1
        use = GUSE.get(g, [(1, GSZ)])
        S.dma("sp", V(ring, s * GSZ, use), dview(wsc, g * G, [(GSZ, 128)] + use), rsem[s])
        return s * GSZ

    def norm_pre(xb, nblk=NB, sc0=0):
        for j in range(nblk):
            ACT(V(WF, 0, [(1, D)]), V(xh, xb + j * D, [(1, D)]), AF.Square, accum_out=V(stt, sc0 + j, [(1, 1)]))
        rstd_from(V(stt, sc0, [(1, nblk)]), V(stt, sc0, [(1, nblk)]), D)
        xsv = [V(xs, 0, [(1, D)]), V(vtok, 0, [(1, D)])]
        for j in range(nblk):
            if j == 0:
                ACT(xsv[j], V(xh, xb + j * D, [(1, D)]), AF.Copy, scale=V(stt, sc0 + j, [(1, 1)]))
            else:
                TS("dve", xsv[j], V(xh, xb + j * D, [(1, D)]), V(stt, sc0 + j, [(1, 1)]), None, ALU.mult)
    def norm_post(gname, nblk=NB):
        xso = [xs, vtok]
        for j in range(nblk):
            pt = psT[j]
            for c in range(8):
                TR(V(pt, c * 128, [(1, 128)]), V(xso[j], c * 128, [(1, 128)]), sig=(c == 7))
        for j in range(nblk):
            TT("dve", V(nT, j * 128, [(T, 8), (1, 128)]), V(psT[j], 0, [(128, 8), (1, 128)]), V(vec, VO[gname], [(1, 8), (0, 128)]), ALU.mult)
    def norm_transpose_all(gname, nblk=NB, xb=0):
        norm_pre(xb, nblk); norm_post(gname, nblk)

    msem = S.new_sem("msem")
    S.dma("sp", V(xh, 0, [(D, 2), (1, D)]), dview(mem_d, 0, [(D, 128), (128 * D, 2), (1, D)]), msem)
    S.dma("pool", V(ring, 0, [(512, 8), (1, 512)]), dview(xwkv_d, 0, [(1024, 128), (128 * 1024, 8), (1, 512)]), semc[4])
    S.dma("pool", V(ring, GSZ, [(512, 8), (1, 512)]), dview(xwkv_d, 512, [(1024, 128), (128 * 1024, 8), (1, 512)]), semc[5])
    norm_transpose_all("nmem", 2, 0)
    for j in range(2):
        pk = bankA()
        for c in range(8):
            MM(V(pk, 0, [(1, 512)]), V(nT, c * T + j * 128, [(1, 128)]), V(ring, c * 512, [(1, 512)]), start=(c == 0), stop=(c == 7))
        pv = bankA()
        for c in range(8):
            MM(V(pv, 0, [(1, 512)]), V(nT, c * T + j * 128, [(1, 128)]), V(ring, GSZ + c * 512, [(1, 512)]), start=(c == 0), stop=(c == 7))
        CP("act", V(Vm, j * 512, [(1, 512)]), V(pv, 0, [(1, 512)]))
        sq = V(WF, 0, [(1, 512)])
        ACT(sq, V(pk, 0, [(1, 512)]), AF.Square)
        ss4 = V(stt, 4, [(1, 4)])
        RED(ss4, V(WF, 0, [(128, 4), (1, 128)]))
        rstd_from(ss4, ss4, 128, 4)
        kn = V(WF, 512, [(1, 512)])
        TT("dve", V(WF, 512, [(128, 4), (1, 128)]), V(pk, 0, [(128, 4), (1, 128)]), V(stt, 4, [(1, 4), (0, 128)]), ALU.mult)
        TT("pool", V(WB, 0, [(128, 4), (1, 128)]), V(WF, 512, [(128, 4), (1, 128)]), V(vec, VO["xkn"], [(0, 4), (1, 128)]), ALU.mult)
        pt = bankT()
        for h in range(4):
            TR(V(pt, h * 128, [(1, 128)]), V(WB, h * 128, [(1, 128)]), sig=(h == 3))
        CP("act", V(KmT, j * 128, [(256, 4), (1, 128)]), V(pt, 0, [(128, 4), (1, 128)]))

    QT0 = 0; MIX0 = 8 * T; XQT0 = 0; XOT0 = 4 * T; PTX0 = 8 * T
    SC_MLA = 96 ** -0.5; SC_XA = 128 ** -0.5
    xsem = [[S.new_sem("xsem%d_%d" % (p_, j)) for j in range(NB)] for p_ in range(2)]; osem = [S.new_sem("osem%d" % j) for j in range(NB)]
    out_toks = []

    import itertools
    def run_chains(chains):
        chains = list(chains)
        while chains:
            for c_ in list(chains):
                try:
                    next(c_)
                except StopIteration:
                    chains.remove(c_)

    prefetched = {}
    def get_granule(g):
        if g in prefetched:
            return prefetched.pop(g)
        return load_granule(g)

    for j in range(NB):
        S.dma("sp", V(xh, j * D, [(1, D)]), dview(x_d, j * 128 * D, [(D, 128), (1, D)]), xsem[0][j])

    uq_box = [None]

    def step_chains(chs, k):
        for _ in range(k):
            for c_ in list(chs):
                try:
                    next(c_)
                except StopIteration:
                    chs.remove(c_)

    def norm_trs():
        xso = [xs, vtok]
        for j in range(NB):
            for c in range(8):
                TR(V(psT[j], c * 128, [(1, 128)]), V(xso[j], c * 128, [(1, 128)]), sig=(c == 7))
        for j in range(NB):
            TT("dve", V(nT, j * 128, [(T, 8), (1, 128)]), V(psT[j], 0, [(128, 8), (1, 128)]), V(vec, VO["nm"], [(1, 8), (0, 128)]), ALU.mult)

    def proj_group(g):
        go = get_granule(g)
        gw = 432 if g == 3 else 512
        for j in range(NB):
            pp = bankB()
            for c in range(8):
                MM(V(pp, 0, [(1, gw)]), V(nT, c * T + j * 128, [(1, 128)]), V(ring, go + c * 512, [(1, gw)]), start=(c == 0), stop=(c == 7))
            if g == 2:
                ACT(V(Gt, j * 512, [(1, 512)]), V(pp, 0, [(1, 512)]), AF.Silu)
                TT("pool", V(Gt, j * 512, [(128, 4), (1, 128)]), V(Gt, j * 512, [(128, 4), (1, 128)]), V(vec, VO["gon"], [(0, 4), (1, 128)]), ALU.mult)
            elif g == 0:
                CP("act", V(qk, j * 512, [(1, 512)]), V(pp, 0, [(1, 512)]))
            elif g == 1:
                CP("dve", V(vtok, j * 512, [(1, 512)]), V(pp, 0, [(1, 512)]))
            else:
                CP("dve", V(r3, j * 432, [(1, 432)]), V(pp, 0, [(1, 432)]))

    def mla(j, c, tn):
        b = tn * NB + j; R3 = j * 432; fb = c * 1760; bb = c * 1536; sc = 16 + c * 24
        px = psT[c]
        def PF(f0, dims, p0=0, np_=128):
            bd = [(2 * s_, n_) for (s_, n_) in dims[:-1]] + [(1, 2 * dims[-1][1])]
            return V(px, 2 * f0, bd, p0, np_).bitcast(F32)
        PBv = lambda f0, dims, p0=0, np_=128: V(px, f0, dims, p0, np_)
        cq = V(r3, R3 + 16, [(1, 256)]); ckv = V(r3, R3 + 272, [(1, 128)]); kpe = V(r3, R3 + 400, [(1, 32)])
        st = lambda i, w=1: V(stt, sc + i, [(1, w)])
        ACT(V(WF, fb + 768, [(1, 256)]), cq, AF.Square, accum_out=st(0))
        ACT(V(WF, fb + 1024, [(1, 128)]), ckv, AF.Square, accum_out=st(1))
        ACT(V(WF, fb + 1632, [(1, 32)]), kpe, AF.Square, accum_out=st(10))
        yield
        ACT(st(0), st(0), AF.Ln, scale=1.0 / 256, bias=epsc)
        ACT(st(1), st(1), AF.Ln, scale=1.0 / 128, bias=epsc)
        ACT(st(0, 2), st(0, 2), AF.Exp, scale=-0.5)
        yield
        ACT(V(WB, bb, [(1, 256)]), cq, AF.Copy, scale=st(0))
        ACT(V(WB, bb + 256, [(1, 128)]), ckv, AF.Copy, scale=st(1))
        yield
        for cc in range(3):
            TR(PBv(cc * 128, [(1, 128)]), V(WB, bb + cc * 128, [(1, 128)]), sig=(cc == 2))
        yield
        cT = bb + 384
        TT("dve", V(WB, cT, [(128, 3), (1, 128)]), PBv(0, [(128, 3), (1, 128)]), V(vec, VO["qan"], [(1, 3), (0, 128)]), ALU.mult)
        yield
        uq_off = uq_box[0]
        qf = bb + 768
        for hf in range(2):
            for cc in range(2):
                MM(PF(0, [(1, 384)]), V(WB, cT + cc * 128, [(1, 128)]), V(ring, uq_off + cc * 768 + hf * 384, [(1, 384)]), start=(cc == 0), stop=(cc == 1))
            yield
            ACT(V(WF, fb + 768, [(1, 384)]), PF(0, [(1, 384)]), AF.Square)
            yield
            RED(st(2 + hf * 4, 4), V(WF, fb + 768, [(96, 4), (1, 96)]))
            yield
            ACT(st(2 + hf * 4, 4), st(2 + hf * 4, 4), AF.Ln, scale=1.0 / 96, bias=epsc)
            yield
            ACT(st(2 + hf * 4, 4), st(2 + hf * 4, 4), AF.Exp, scale=-0.5)
            yield
            TT("dve", V(WB, qf + hf * 384, [(96, 4), (1, 64)]), PF(0, [(96, 4), (1, 64)]), V(stt, sc + 2 + hf * 4, [(1, 4), (0, 64)]), ALU.mult)
            TT("dve", V(WF, fb + hf * 128, [(32, 4), (1, 32)]), PF(64, [(96, 4), (1, 32)]), V(stt, sc + 2 + hf * 4, [(1, 4), (0, 32)]), ALU.mult)
            yield
        TT("pool", V(WF, fb, [(32, 8), (1, 32)]), V(WF, fb, [(32, 8), (1, 32)]), V(vec, VO["gq"] + 64, [(0, 8), (1, 32)]), ALU.mult)
        yield
        cosb = V(COS, b * 16, [(0, 8), (1, 16)]); sinb = V(SIN, b * 16, [(0, 8), (1, 16)])
        x1 = V(WF, fb, [(32, 8), (1, 16)]); x2 = V(WF, fb + 16, [(32, 8), (1, 16)])
        tA = V(WF, fb + 1280, [(16, 8), (1, 16)]); tB = V(WF, fb + 1408, [(16, 8), (1, 16)])
        TT("pool", tA, x1, cosb, ALU.mult); TT("pool", tB, x2, sinb, ALU.mult)
        yield
        TT("pool", V(WB, qf + 64, [(96, 8), (1, 16)]), tA, tB, ALU.subtract)
        yield
        TT("pool", tA, x2, cosb, ALU.mult); TT("pool", tB, x1, sinb, ALU.mult)
        yield
        TT("pool", V(WB, qf + 80, [(96, 8), (1, 16)]), tA, tB, ALU.add)
        yield
        for h in range(8):
            TR(PBv(h * 128, [(1, 128)], 0, 96), V(WB, qf + h * 96, [(1, 96)]), sig=(h == 7))
        yield
        CP("act", V(U, QT0 + j * 128, [(T, 8), (1, 128)], 0, 96), PBv(0, [(128, 8), (1, 128)], 0, 96))
        kp = V(WF, fb + 1536, [(1, 32)])
        TT("pool", kp, kpe, vcol("gk", 64, 32), ALU.mult)
        yield
        c1 = V(COS, b * 16, [(1, 16)]); s1 = V(SIN, b * 16, [(1, 16)])
        k1 = V(WF, fb + 1536, [(1, 16)]); k2 = V(WF, fb + 1552, [(1, 16)])
        rA = V(WF, fb + 1568, [(1, 16)]); rB = V(WF, fb + 1584, [(1, 16)])
        kr = fb + 1600
        TT("pool", rA, k1, c1, ALU.mult); TT("pool", rB, k2, s1, ALU.mult)
        yield
        TT("pool", V(WF, kr, [(1, 16)]), rA, rB, ALU.subtract)
        yield
        TT("pool", rA, k2, c1, ALU.mult); TT("pool", rB, k1, s1, ALU.mult)
        yield
        TT("pool", V(WF, kr + 16, [(1, 16)]), rA, rB, ALU.add)
        yield
        for hf in range(2):
            MM(PF(0, [(1, 512)]), V(WB, cT + 256, [(1, 128)]), V(ring, uq_off + 1536 + hf * 512, [(1, 512)]))
            yield
            ACT(V(WF, fb + 768, [(1, 512)]), PF(0, [(1, 512)]), AF.Square)
            yield
            RED(st(11 + hf * 4, 4), V(WF, fb + 768, [(128, 4), (1, 64)]))
            CP("act", V(VA, (b * 8 + hf * 4) * 66, [(66, 4), (1, 64)]), PF(64, [(128, 4), (1, 64)]))
            yield
            TS("dve", st(11 + hf * 4, 4), st(11 + hf * 4, 4), st(10), None, ALU.add)
            yield
            ACT(st(11 + hf * 4, 4), st(11 + hf * 4, 4), AF.Ln, scale=1.0 / 96, bias=epsc)
            yield
            ACT(st(11 + hf * 4, 4), st(11 + hf * 4, 4), AF.Exp, scale=-0.5)
            yield
            TT("dve", V(WF, fb + hf * 384, [(96, 4), (1, 64)]), PF(0, [(128, 4), (1, 64)]), V(stt, sc + 11 + hf * 4, [(1, 4), (0, 64)]), ALU.mult)
            yield
        kfb = bb + 768
        TT("dve", V(WB, kfb, [(96, 8), (1, 64)]), V(WF, fb, [(96, 8), (1, 64)]), V(vec, VO["gk"], [(0, 8), (1, 64)]), ALU.mult)
        TT("pool", V(WB, kfb + 64, [(96, 8), (1, 32)]), V(WF, kr, [(0, 8), (1, 32)]), V(stt, sc + 11, [(1, 8), (0, 32)]), ALU.mult)
        yield
        for h in range(8):
            TR(PBv(h * 128, [(1, 128)], 0, 96), V(WB, kfb + h * 96, [(1, 96)]), sig=(h == 7))
        yield
        CP("act", V(KT, b * 128, [(SEQ, 8), (1, 128)], 0, 96), PBv(0, [(128, 8), (1, 128)], 0, 96))

    norm_pre(0, NB, 2)
    norm_trs()
    proj_group(3)
    emit_casts([5, 6, 7, 8])
    chains0 = [mla(j, j, 0) for j in range(NB)]
    proj_group(2); step_chains(chains0, 3)
    proj_group(0)
    uq_box[0] = get_granule(4)
    step_chains(chains0, 3)
    proj_group(1)
    while chains0:
        step_chains(chains0, 1)
    emit_casts(list(range(9, 20)))

    for ti in range(NTILE):
        t0 = ti * T
        xb = (ti % 2) * NB * D
        if ti + 1 < NTILE:
            xb2 = ((ti + 1) % 2) * NB * D
            for j in range(NB):
                S.dma("sp", V(xh, xb2 + j * D, [(1, D)]), dview(x_d, (t0 + T + j * 128) * D, [(D, 128), (1, D)]), xsem[(ti + 1) % 2][j])
        psG = psA[3]
        def gla(j):
            R3 = j * 432
            CP("dve", V(WB, 0, [(1, 16)]), V(r3, R3, [(1, 16)]))
            yield
            pt = bankT()
            yield
            yield
            TR(V(pt, 0, [(1, 128)], 0, 16), V(WB, 0, [(1, 16)]))
            yield
            CP("dve", V(WB, 16, [(1, 128)], 0, 16), V(pt, 0, [(1, 128)], 0, 16))
            yield
            yield
            yield
            MM(V(psG, 0, [(1, 256)]), V(WB, 16, [(1, 128)], 0, 16), V(w2b, 0, [(1, 256)], 0, 16))
            yield
            z = V(WF, 0, [(1, 256)]); l_ = V(WF, 256, [(1, 256)])
            TT("dve", z, V(psG, 0, [(1, 256)]), vcol("gb", 0, 256), ALU.add)
            yield
            ACT(z, z, AF.Exp, scale=-1.0)
            yield
            ACT(l_, z, AF.Ln, bias=1.0)
            yield
            yield
            yield
            MM(V(psG, 0, [(1, 256)]), U_f, l_)
            for h in range(4):
                MM(V(psG, 256 + h, [(1, 1)], 0, 64), V(WF, 256 + h * 64, [(1, 64)]), onesf, sig=(h == 3))
            yield
            Eq = V(WF, 512, [(1, 256)]); Ek = V(WF, 768, [(1, 256)])
            ACT(Eq, V(psG, 0, [(1, 256)]), AF.Exp, scale=-1.0 / 16)
            ACT(Ek, V(psG, 0, [(1, 256)]), AF.Exp, scale=1.0 / 16)
            ACT(V(stt, 8, [(1, 4)], 0, 64), V(psG, 256, [(1, 4)], 0, 64), AF.Exp, scale=-1.0 / 16)
            yield
            qd = V(WB, 256, [(1, 256)]); ki = V(WB, 512, [(1, 256)])
            TT("pool", qd, V(qk, j * 512, [(1, 256)]), Eq, ALU.mult)
            TT("pool", ki, V(qk, j * 512 + 256, [(1, 256)]), Ek, ALU.mult)
            yield
            pt = bankT()
            yield
            yield
            for h in range(8):
                TR(V(pt, h * 128, [(1, 128)], 0, 64), V(WB, 256 + h * 64, [(1, 64)]), sig=(h == 7))
            yield
            qkT = 768
            CP("act", V(WB, qkT, [(1, 1024)], 0, 64), V(pt, 0, [(1, 1024)], 0, 64))
            yield
            yield
            yield
            for h in range(4):
                MM(V(psG, h * 128, [(1, 128)]), V(WB, qkT + (4 + h) * 128, [(1, 128)], 0, 64), V(WB, qkT + h * 128, [(1, 128)], 0, 64), sig=(h == 3))
            yield
            ATm = 1792
            STT("dve", V(WB, ATm, [(128, 4), (1, 128)]), V(psG, 0, [(128, 4), (1, 128)]), 0.125, V(Ub, 0, [(0, 4), (1, 128)]), ALU.mult, ALU.mult)
            yield
            yield
            yield
            for h in range(4):
                MM(V(psG, h * 128, [(1, 128)]), V(WB, ATm + h * 128, [(1, 128)]), V(vtok, j * 512 + h * 128, [(1, 128)]), start=True, stop=False, sig=False)
                MM(V(psG, h * 128, [(1, 128)]), V(WB, qkT + h * 128, [(1, 128)], 0, 64), V(Sb, h * 128, [(1, 128)], 0, 64), start=False, stop=True, sig=(h == 3))
            yield
            osq = V(WF, 1024, [(1, 512)])
            ACT(osq, V(psG, 0, [(1, 512)]), AF.Square)
            yield
            so = V(stt, 12, [(1, 4)])
            RED(so, V(WF, 1024, [(128, 4), (1, 128)]))
            yield
            rstd_from(so, so, 128, 4)
            yield
            TT("dve", V(WF, 1536, [(128, 4), (1, 128)]), V(psG, 0, [(128, 4), (1, 128)]), V(stt, 12, [(1, 4), (0, 128)]), ALU.mult)
            yield
            ogv = V(xs, 0, [(1, 512)])
            TT("pool", ogv, V(WF, 1536, [(1, 512)]), V(Gt, j * 512, [(1, 512)]), ALU.mult)
            yield
            yield
            for h in range(4):
                MM(V(psG, h * 128, [(1, 128)], 0, 64), V(WB, 512 + h * 64, [(1, 64)]), V(vtok, j * 512 + h * 128, [(1, 128)]), sig=(h == 3))
            yield
            TT("dve", V(Sst, 0, [(1, 512)], 0, 64), V(psG, 0, [(1, 512)], 0, 64), V(Sst, 0, [(1, 512)], 0, 64), ALU.add)
            pt = bankT()
            for c in range(4):
                TR(V(pt, c * 128, [(1, 128)]), V(xs, c * 128, [(1, 128)]), sig=(c == 3))
            yield
            TT("dve", V(Sst, 0, [(128, 4), (1, 128)], 0, 64), V(Sst, 0, [(128, 4), (1, 128)], 0, 64), V(stt, 8, [(1, 4), (0, 128)], 0, 64), ALU.mult)
            CP("act", V(U, MIX0 + j * 128, [(T, 4), (1, 128)]), V(pt, 0, [(128, 4), (1, 128)]))
            yield
            TS("pool", V(Sb, 0, [(1, 512)], 0, 64), V(Sst, 0, [(1, 512)], 0, 64), 0.125, None, ALU.mult)
            yield

        gla_gen = itertools.chain(gla(0), gla(1)) if NB == 2 else itertools.chain(*[gla(j) for j in range(NB)])
        gla_done = [False]
        def gla_advance(k):
            for _ in range(k):
                if gla_done[0]:
                    return
                try:
                    next(gla_gen)
                except StopIteration:
                    gla_done[0] = True

        units = []
        for h in range(8):
            for kb in range(0, ti * NB, 2):
                units.append((h, "off", kb))
            units.append((h, "diag", ti * NB))
        AHEAD = 2
        pS_of = {}; pO_of = {}
        deferred = []

        def emit_S(ui):
            h, kind, kb = units[ui]
            pS = bankA(); pS_of[ui] = pS
            q_ = lambda c0, n: V(U, QT0 + h * T + c0, [(1, n)], 0, 96)
            k_ = lambda kbb: V(KT, h * SEQ + kbb * 128, [(1, 128)], 0, 96)
            if kind == "off":
                MM(V(pS, 0, [(1, T)]), k_(kb), q_(0, T), sig=False)
                MM(V(pS, T, [(1, T)]), k_(kb + 1), q_(0, T))
            else:
                MM(V(pS, 0, [(1, T)]), k_(kb), q_(0, T), sig=False)
                MM(V(pS, T, [(1, 128)]), k_(kb + 1), q_(128, 128))

        def emit_E(ui):
            h, kind, kb = units[ui]
            pS = pS_of[ui]; pto = (ui % 3) * 512
            n = 2 * T if kind == "off" else T + 128
            ACT(V(PT, pto, [(1, n)]), V(pS, 0, [(1, n)]), AF.Exp, scale=SC_MLA)
            if kind == "diag":
                TT("pool", V(PT, pto, [(T, 2), (1, 128)]), V(PT, pto, [(T, 2), (1, 128)]), V(Ub, 0, [(0, 2), (1, 128)]), ALU.mult)

        def emit_norm1(h, pO, ui):
            recrow = V(WF, 2048 + (h % 2) * T, [(1, T)], 64, 1)
            RECIP(recrow, V(pO, 0, [(1, T)], 64, 1))
            deferred.append([ui + 3, (lambda: emit_norm2(h, pO)), h])

        def emit_norm2(h, pO):
            recrow = V(WF, 2048 + (h % 2) * T, [(1, T)], 64, 1)
            pb = bankA()
            MM(V(pb, 0, [(1, T)], 0, 64), V(cst, 128 + 64, [(1, 64)], 64, 1), recrow)
            bcs = V(WF, 2048 + (h % 2) * T, [(1, T)], 0, 64)
            CP("dve", bcs, V(pb, 0, [(1, T)], 0, 64))
            TT("dve", V(U, MIX0 + (4 + h // 2) * T, [(1, T)], (h % 2) * 64, 64), V(pO, 0, [(1, T)], 0, 64), bcs, ALU.mult)

        def emit_PV(ui):
            h, kind, kb = units[ui]
            pto = (ui % 3) * 512
            first = (ui == 0) or (units[ui - 1][0] != h)
            if first:
                while any(d[2] <= h - 2 for d in deferred):
                    for d in list(deferred):
                        if d[2] <= h - 2:
                            deferred.remove(d); d[1]()
                pO_of[h] = bankB()
            pO = pO_of[h]
            va = lambda kbb: V(VA, (kbb * 8 + h) * 66, [(1, 128)])
            if kind == "off":
                MM(V(pO, 0, [(1, T)]), va(kb), V(PT, pto, [(1, T)]), start=first, stop=False, sig=False)
                MM(V(pO, 0, [(1, T)]), va(kb + 1), V(PT, pto + T, [(1, T)]), start=False, stop=False, sig=True)
            else:
                MM(V(pO, 0, [(1, T)]), va(kb), V(PT, pto, [(1, T)]), start=first, stop=False, sig=False)
                MM(V(pO, 128, [(1, 128)]), va(kb + 1), V(PT, pto + T, [(1, 128)]), start=False, stop=True, sig=True)
                deferred.append([ui + 2, (lambda hh=h, pp=pO, uu=ui: emit_norm1(hh, pp, uu + 2)), h])

        nu = len(units)
        gk_ = max(1, -(-86 // nu))
        for step in range(nu + AHEAD):
            if step < nu:
                emit_S(step); emit_E(step)
            if step - AHEAD >= 0:
                emit_PV(step - AHEAD)
            for d in list(deferred):
                if d[0] <= step - AHEAD:
                    deferred.remove(d); d[1]()
            gla_advance(gk_)
        while deferred:
            d = deferred.pop(0); d[1]()
        gla_advance(10 ** 6)

        if ti == 0:
            emit_casts(list(range(20, 26)))
        wo = [get_granule(5), get_granule(6)]
        for j in range(NB):
            for hf in range(2):
                pp = bankA()
                for c in range(8):
                    MM(V(pp, 0, [(1, 512)]), V(U, MIX0 + c * T + j * 128, [(1, 128)]), V(ring, wo[c // 4] + (c % 4) * 1024 + hf * 512, [(1, 512)]), start=(c == 0), stop=(c == 7))
                xv = V(xh, xb + j * D + hf * 512, [(1, 512)])
                TT("dve", xv, V(pp, 0, [(1, 512)]), xv, ALU.add)
        xq_off = get_granule(7)
        norm_transpose_all("nxa", NB, xb)
        def xq_chain(j):
            pq = psA[j]
            for c in range(8):
                MM(V(pq, 0, [(1, 512)]), V(nT, c * T + j * 128, [(1, 128)]), V(ring, xq_off + c * 512, [(1, 512)]), start=(c == 0), stop=(c == 7))
            yield
            ACT(V(WF, j * 1024, [(1, 512)]), V(pq, 0, [(1, 512)]), AF.Square)
            yield
            s4_ = V(stt, 4 + 4 * j, [(1, 4)])
            RED(s4_, V(WF, j * 1024, [(128, 4), (1, 128)]))
            yield
            ACT(s4_, s4_, AF.Ln, scale=1.0 / 128, bias=epsc)
            yield
            ACT(s4_, s4_, AF.Exp, scale=-0.5)
            yield
            TT("dve", V(WF, j * 1024 + 512, [(128, 4), (1, 128)]), V(pq, 0, [(128, 4), (1, 128)]), V(stt, 4 + 4 * j, [(1, 4), (0, 128)]), ALU.mult)
            yield
            TT("pool", V(WB, j * 512, [(128, 4), (1, 128)]), V(WF, j * 1024 + 512, [(128, 4), (1, 128)]), V(vec, VO["xqn"], [(0, 4), (1, 128)]), ALU.mult)
            yield
            pt = psT[j]
            for hh in range(4):
                TR(V(pt, hh * 128, [(1, 128)]), V(WB, j * 512 + hh * 128, [(1, 128)]), sig=(hh == 3))
            yield
            CP("act", V(U, XQT0 + j * 128, [(T, 4), (1, 128)]), V(pt, 0, [(128, 4), (1, 128)]))
        run_chains([xq_chain(j) for j in range(NB)])

        def xS(hh):
            base = (hh % 2) * 2 * T
            for kb in range(2):
                pS = bankA()
                MM(V(pS, 0, [(1, T)]), V(KmT, hh * 256 + kb * 128, [(1, 128)]), V(U, XQT0 + hh * T, [(1, T)]))
                ACT(V(U, PTX0 + base + kb * T, [(1, T)]), V(pS, 0, [(1, T)]), AF.Exp, scale=SC_XA)
        def xPV(hh):
            base = (hh % 2) * 2 * T
            pb_ = psB[hh % 2]
            for kb in range(2):
                MM(V(pb_, 0, [(1, T)]), V(Vm, kb * 512 + hh * 128, [(1, 128)]), V(U, PTX0 + base + kb * T, [(1, T)]), start=(kb == 0), stop=(kb == 1))
            for kb in range(2):
                MM(V(pb_, T, [(1, T)]), V(onesb, 0, [(1, 128)]), V(U, PTX0 + base + kb * T, [(1, T)]), start=(kb == 0), stop=(kb == 1))
            rd = V(WF, 2048 + (hh % 2) * T, [(1, T)])
            ACT(rd, V(pb_, T, [(1, T)]), AF.Ln)
            ACT(rd, rd, AF.Exp, scale=-1.0)
            TT("dve", V(U, XOT0 + hh * T, [(1, T)]), V(pb_, 0, [(1, T)]), rd, ALU.mult)
        xS(0); xS(1); xPV(0); xS(2); xPV(1); xS(3); xPV(2); xPV(3)
        xo_off = get_granule(8)
        for j in range(NB):
            for hf in range(2):
                pp = bankA()
                for c in range(4):
                    MM(V(pp, 0, [(1, 512)]), V(U, XOT0 + c * T + j * 128, [(1, 128)]), V(ring, xo_off + c * 1024 + hf * 512, [(1, 512)]), start=(c == 0), stop=(c == 3))
                xv = V(xh, xb + j * D + hf * 512, [(1, 512)])
                TT("dve", xv, V(pp, 0, [(1, 512)]), xv, ALU.add)
        norm_transpose_all("nffn", NB, xb)
        for fc in range(NFC):
            if fc % 2 == 0:
                gu = get_granule(9 + fc // 2)
            sub = fc % 2
            pg = psA[0] if fc % 2 == 0 else psA[2]
            pu = psA[1] if fc % 2 == 0 else psA[3]
            for c in range(8):
                MM(V(pg, 0, [(1, T)]), V(ring, gu + sub * 2048 + c * 128, [(1, 128)]), V(nT, c * T, [(1, T)]), start=(c == 0), stop=(c == 7))
            for c in range(8):
                MM(V(pu, 0, [(1, T)]), V(ring, gu + sub * 2048 + 1024 + c * 128, [(1, 128)]), V(nT, c * T, [(1, T)]), start=(c == 0), stop=(c == 7))
            sl4 = fc % 4
            gb_ = sl4 * (T + 2)
            ub = V(WB, sl4 * T, [(1, T)])
            CP("pool", V(WF, gb_, [(1, 2)]), V(halo, fc * 2, [(1, 2)]))
            CP("act", V(WF, gb_ + 2, [(1, T)]), V(pg, 0, [(1, T)]))
            CP("act", ub, V(pu, 0, [(1, T)]))
            CP("pool", V(halo, fc * 2, [(1, 2)]), V(WF, gb_ + T, [(1, 2)]))
            tc_ = V(WF, 1032 + sl4 * T, [(1, T)])
            cw = lambda i: V(vec, VO["cw"] + fc * 3 + i, [(1, 1)])
            TS("pool", tc_, V(WF, gb_ + 2, [(1, T)]), cw(2), V(vec, VO["cb"] + fc, [(1, 1)]), ALU.mult, ALU.add)
            STT("dve", tc_, V(WF, gb_ + 1, [(1, T)]), cw(1), tc_, ALU.mult, ALU.add)
            STT("dve", tc_, V(WF, gb_, [(1, T)]), cw(0), tc_, ALU.mult, ALU.add)
            ACT(tc_, tc_, AF.Silu)
            TT("pool", V(U, fc * T, [(1, T)]), tc_, ub, ALU.mult)
        nxt = ti + 1 < NTILE
        if nxt:
            norm_pre(((ti + 1) % 2) * NB * D, NB, 2)
        def dn_group(k):
            dn = get_granule(20 + k)
            nj = 4 if k < 5 else 2
            for jj in range(nj):
                fc = 4 * k + jj
                for j in range(NB):
                    for hf in range(2):
                        MM(V(psA[j * 2 + hf], 0, [(1, 512)]), V(U, fc * T + j * 128, [(1, 128)]), V(ring, dn + jj * 1024 + hf * 512, [(1, 512)]), start=(fc == 0), stop=(fc == NFC - 1), sig=(j == NB - 1 and hf == 1))
        if nxt:
            dn_group(0); dn_group(1)
            norm_trs()
            proj_group(3)
            chains = [mla(j, j, ti + 1) for j in range(NB)]
            dn_group(2); step_chains(chains, 2)
            proj_group(2); step_chains(chains, 1)
            dn_group(3); step_chains(chains, 2)
            proj_group(0)
            uq_box[0] = get_granule(4)
            step_chains(chains, 8)
            dn_group(4); step_chains(chains, 8)
            proj_group(1)
            while chains:
                step_chains(chains, 1)
            dn_group(5)
        else:
            for k in range(6):
                dn_group(k)
        for j in range(NB):
            for hf in range(2):
                xv = V(xh, xb + j * D + hf * 512, [(1, 512)])
                TT("dve", xv, V(psA[j * 2 + hf], 0, [(1, 512)]), xv, ALU.add)
            out_toks.append(S.dma("sp", dview(out_d, (t0 + j * 128) * D, [(D, 128), (1, D)]), V(xh, xb + j * D, [(1, D)]), osem[j]))
    S.wait_all("sp", [[o_, S.dma_cum[o_]] for o_ in osem])
    return nc, S


_CACHE = {}

def kernel(x, mem, positions, norm_mix, w_in, gla_gate_w2, gla_gate_b, gla_out_norm,
           mla_q_a_norm, mla_w_uq, mla_kv_a_norm, mla_w_ukv, mla_q_norm, mla_k_norm, w_out,
           norm_xa, norm_mem, xa_w_q, xa_w_kv, xa_q_norm, xa_k_norm, xa_w_o,
           norm_ffn, ffn_w_gate, ffn_w_up, ffn_conv_w, ffn_conv_b, ffn_w_down):
    f = lambda a: np.ascontiguousarray(np.asarray(a, dtype=np.float32))
    x = f(x); mem = f(mem); positions = np.asarray(positions).astype(np.int32)
    fm = lambda v, c: f(v).reshape(c, 128).T
    rep = lambda v: np.broadcast_to(f(v).reshape(1, -1), (128, f(v).size))
    cols = [fm(norm_mix[0], 8), fm(norm_xa[0], 8), fm(norm_ffn[0], 8), fm(norm_mem[0], 8), fm(mla_q_a_norm[0], 2), fm(mla_kv_a_norm[0], 1),
            rep(gla_gate_b[0]), rep(gla_out_norm[0]), rep(mla_q_norm[0]), rep(mla_k_norm[0]), rep(xa_q_norm[0]), rep(xa_k_norm[0]),
            f(ffn_conv_w[0]).reshape(3, NFC, 128).transpose(2, 1, 0).reshape(128, 66), fm(ffn_conv_b[0], NFC)]
    vec = np.ascontiguousarray(np.concatenate(cols, axis=1).astype(np.float32))
    assert vec.shape == (128, NV)
    ident = np.eye(128, dtype=np.float32)
    Utri = np.triu(np.ones((128, 128), dtype=np.float32))
    inv = (10000.0 ** (-np.arange(16, dtype=np.float32) / 16)).astype(np.float32)
    cst = np.ascontiguousarray(np.concatenate([ident, Utri, np.broadcast_to(inv[None, :], (128, 16))], axis=1).astype(np.float32))
    if "nc" not in _CACHE:
        _CACHE["nc"] = build_program()[0]
    nc = _CACHE["nc"]
    shared = {"cst": cst, "vec": vec, "w_in": f(w_in[0]), "w2": f(gla_gate_w2[0]), "w_uq": f(mla_w_uq[0]), "w_ukv": f(mla_w_ukv[0]),
              "w_out": f(w_out[0]), "xa_w_q": f(xa_w_q[0]), "xa_w_kv": f(xa_w_kv[0]), "xa_w_o": f(xa_w_o[0]),
              "w_gate": f(ffn_w_gate[0]), "w_up": f(ffn_w_up[0]), "w_down": f(ffn_w_down[0])}
    in_maps = []
    for c in range(8):
        m = dict(shared)
        m["x"] = x[c]; m["mem"] = mem[c]
        m["pos"] = np.ascontiguousarray(positions[c].reshape(32, 128).T)
        in_maps.append(m)
    res = run_bass_kernel_spmd(nc, in_maps, core_ids=list(range(8)))
    return np.stack([np.asarray(r["out"]).reshape(SEQ, D) for r in res.results], axis=0).astype(np.float32)
```
